# Optimizing a Trainium2 kernel written in Bass

```python
import math
import jax, jax.numpy as jnp
from jax import lax
import numpy as np

D_MODEL = 2048
BATCH = 2
SEQ = 16384
DEPTH = 1
DEC_BATCH = 32
DEC_SEQ = 64
PAST_LEN = 2048

CHUNK = 64
Q_BLOCK = 128
D_HEAD = 128
H_SB = 8
H_SA = 8
W_SB = H_SB * D_HEAD
W_SA = H_SA * D_HEAD
H_IDX = 16
D_IDX = 64
TOPK_MAX = 256
N_BUCKETS = 32
REL_MAX_DIST = 1024
D_FF = 5632
CONV_W = 3
EPS = 1e-6
IN_SIZES = (W_SB, W_SB, W_SB, W_SA, W_SA, W_SA, H_IDX * D_IDX, D_IDX, H_IDX)
IN_COLS = sum(IN_SIZES)
IN_SPLITS = tuple(int(s) for s in np.cumsum(IN_SIZES[:-1]))

kernel_name = "streaming_stickbreak_dsa_hybrid_step"


def rmsnorm(x, g):
    xf = x.astype(jnp.float32)
    y = xf * lax.rsqrt(jnp.mean(xf * xf, axis=-1, keepdims=True) + EPS)
    return (y * g.astype(jnp.float32)).astype(x.dtype)


def chunk_limit(pos):
    return (pos // CHUNK + 1) * CHUNK


def rel_bucket(rel):
    nb = N_BUCKETS // 2
    max_exact = nb // 2
    ret = jnp.where(rel > 0, nb, 0)
    n = jnp.abs(rel)
    nf = jnp.maximum(n, 1).astype(jnp.float32)
    large = max_exact + (jnp.log(nf / max_exact) / math.log(REL_MAX_DIST / max_exact)
                         * (nb - max_exact)).astype(jnp.int32)
    large = jnp.minimum(large, nb - 1)
    return ret + jnp.where(n < max_exact, n, large)


def sb_block(q, k, v, q_pos, k_pos):
    z = jnp.einsum('bqhd,bkhd->bhqk', q, k).astype(jnp.float32) * (D_HEAD ** -0.5)
    mask = k_pos[None, :] < q_pos[:, None]
    log_keep = jnp.where(mask, jax.nn.log_sigmoid(-z), 0.0)
    after = lax.cumsum(log_keep, axis=3, reverse=True) - log_keep
    a = jnp.where(mask, jnp.exp(jax.nn.log_sigmoid(z) + after), 0.0)
    return jnp.einsum('bhqk,bkhd->bqhd', a.astype(v.dtype), v)


def dsa_block(q, qi, wi, q_pos, k, v, ki, k_pos, rel_table, topk):
    lim = chunk_limit(q_pos)
    admissible = k_pos[None, :] < lim[:, None]
    s_idx = jnp.einsum('bqhe,bke->bqhk', qi, ki).astype(jnp.float32) * (D_IDX ** -0.5)
    score = jnp.einsum('bqh,bqhk->bqk', wi.astype(jnp.float32) * (H_IDX ** -0.5),
                       jax.nn.relu(s_idx))
    score = jnp.where(admissible[None], score, -jnp.inf)
    _, sel = lax.top_k(score, topk)
    sel_pos = k_pos[sel]
    valid = sel_pos < lim[None, :, None]
    bidx = jnp.arange(k.shape[0])[:, None, None]
    kg = k[bidx, sel]
    vg = v[bidx, sel]
    logits = jnp.einsum('bqhd,bqkhd->bhqk', q, kg).astype(jnp.float32) * (D_HEAD ** -0.5)
    bias = rel_table.astype(jnp.float32)[rel_bucket(sel_pos - q_pos[None, :, None])]
    logits = logits + jnp.transpose(bias, (0, 3, 1, 2))
    logits = jnp.where(valid[:, None], logits, -jnp.inf)
    p = jax.nn.softmax(logits, axis=-1)
    return jnp.einsum('bhqk,bqkhd->bqhd', p.astype(vg.dtype), vg)


def sweep(fn, qs, q_pos, kvs):
    B, Lq = qs[0].shape[:2]
    qb = min(Q_BLOCK, Lq)
    nblk = Lq // qb

    def step(args):
        b, i = args
        q_blk = tuple(lax.dynamic_slice_in_dim(a[b], i * qb, qb, axis=0)[None] for a in qs)
        kv_b = tuple(a[b][None] for a in kvs)
        pos = lax.dynamic_slice_in_dim(q_pos, i * qb, qb)
        return fn(q_blk, pos, kv_b)[0]

    bs, ib = jnp.meshgrid(jnp.arange(B), jnp.arange(nblk), indexing='ij')
    out = lax.map(step, (bs.reshape(-1), ib.reshape(-1)))
    return out.reshape((B, Lq) + out.shape[2:])


def layer(x, c, past, w_ada, b_ada, g_mix, w_in, w_gate, w_br_sb, w_br_sa, w_out,
          rel_table, g_ffn, w_up, conv_w, conv_b, w_down):
    B, L, _ = x.shape
    past_len = 0 if past is None else past[0].shape[1]
    mod = jax.nn.silu(c) @ w_ada + b_ada
    sh1, sc1, gt1, sh2, sc2, gt2 = jnp.split(mod[:, None, :], 6, axis=-1)
    h = rmsnorm(x, g_mix) * (1 + sc1) + sh1
    q_sb, k_sb, v_sb, q_sa, k_sa, v_sa, q_ix, k_ix, w_ix = jnp.split(h @ w_in, IN_SPLITS, axis=-1)
    q_sb, k_sb, v_sb = (a.reshape(B, L, H_SB, D_HEAD) for a in (q_sb, k_sb, v_sb))
    q_sa, k_sa, v_sa = (a.reshape(B, L, H_SA, D_HEAD) for a in (q_sa, k_sa, v_sa))
    q_ix = q_ix.reshape(B, L, H_IDX, D_IDX)
    if past is None:
        k_sb_all, v_sb_all, k_sa_all, v_sa_all, k_ix_all = k_sb, v_sb, k_sa, v_sa, k_ix
        prev = jnp.zeros((B, CONV_W - 1, D_FF), x.dtype)
    else:
        k_sb_all = jnp.concatenate([past[0], k_sb], axis=1)
        v_sb_all = jnp.concatenate([past[1], v_sb], axis=1)
        k_sa_all = jnp.concatenate([past[2], k_sa], axis=1)
        v_sa_all = jnp.concatenate([past[3], v_sa], axis=1)
        k_ix_all = jnp.concatenate([past[4], k_ix], axis=1)
        prev = past[5]
    l_keys = past_len + L
    k_pos = jnp.arange(l_keys, dtype=jnp.int32)
    q_pos = jnp.arange(past_len, l_keys, dtype=jnp.int32)
    topk = min(TOPK_MAX, l_keys // 4)
    o_sb = sweep(lambda q, p, kv: sb_block(q[0], kv[0], kv[1], p, k_pos),
                 (q_sb,), q_pos, (k_sb_all, v_sb_all))
    o_sa = sweep(lambda q, p, kv: dsa_block(q[0], q[1], q[2], p, kv[0], kv[1], kv[2],
                                            k_pos, rel_table, topk),
                 (q_sa, q_ix, w_ix), q_pos, (k_sa_all, v_sa_all, k_ix_all))
    g = jax.nn.sigmoid((h @ w_gate).astype(jnp.float32)).astype(x.dtype)
    g_sb, g_sa = jnp.split(g, 2, axis=-1)
    merged = (g_sb * (o_sb.reshape(B, L, W_SB) @ w_br_sb)
              + g_sa * (o_sa.reshape(B, L, W_SA) @ w_br_sa))
    x = x + gt1 * (merged @ w_out)
    h2 = rmsnorm(x, g_ffn) * (1 + sc2) + sh2
    u_g, u_v = jnp.split(h2 @ w_up, 2, axis=-1)
    ext = jnp.concatenate([prev.astype(u_g.dtype), u_g], axis=1)
    conv = conv_b + conv_w[0] * ext[:, 0:L]
    for j in range(1, CONV_W):
        conv = conv + conv_w[j] * ext[:, j:j + L]
    x = x + gt2 * ((jax.nn.silu(conv) * u_v) @ w_down)
    return x, (k_sb, v_sb, k_sa, v_sa, k_ix, ext[:, L:])


def setup_inputs(seed: int = 0) -> dict:
    key = jax.random.key(seed)
    ks = iter(jax.random.split(key, 32))

    def nrm(shape, s=1.0):
        return s * jax.random.normal(next(ks), shape, jnp.float32)

    D = D_MODEL
    return {
        "x_prompt": nrm((BATCH, SEQ, D)),
        "x_sample": nrm((DEC_BATCH, DEC_SEQ, D)),
        "cache_sb_k": nrm((DEPTH, DEC_BATCH, PAST_LEN, H_SB, D_HEAD)),
        "cache_sb_v": nrm((DEPTH, DEC_BATCH, PAST_LEN, H_SB, D_HEAD)),
        "cache_sa_k": nrm((DEPTH, DEC_BATCH, PAST_LEN, H_SA, D_HEAD)),
        "cache_sa_v": nrm((DEPTH, DEC_BATCH, PAST_LEN, H_SA, D_HEAD)),
        "cache_idx_k": nrm((DEPTH, DEC_BATCH, PAST_LEN, D_IDX)),
        "state_ffn_conv": nrm((DEPTH, DEC_BATCH, CONV_W - 1, D_FF)),
        "c_prompt": nrm((BATCH, D)),
        "c_sample": nrm((DEC_BATCH, D)),
        "w_ada": nrm((DEPTH, D, 6 * D), 0.5 * D ** -0.5),
        "b_ada": nrm((DEPTH, 6 * D), 0.02),
        "g_mix": 1.0 + nrm((DEPTH, D), 0.02),
        "w_in": nrm((DEPTH, D, IN_COLS), D ** -0.5),
        "w_gate": nrm((DEPTH, D, 2 * D), D ** -0.5),
        "w_br_sb": nrm((DEPTH, W_SB, D), W_SB ** -0.5),
        "w_br_sa": nrm((DEPTH, W_SA, D), W_SA ** -0.5),
        "w_out": nrm((DEPTH, D, D), D ** -0.5),
        "rel_table": nrm((N_BUCKETS, H_SA), 0.5),
        "g_ffn": 1.0 + nrm((DEPTH, D), 0.02),
        "w_up": nrm((DEPTH, D, 2 * D_FF), D ** -0.5),
        "conv_w": nrm((DEPTH, CONV_W, D_FF), CONV_W ** -0.5),
        "conv_b": nrm((DEPTH, D_FF), 0.02),
        "w_down": nrm((DEPTH, D_FF, D), D_FF ** -0.5),
        "g_final": 1.0 + nrm((D,), 0.02),
    }


def reference(x_prompt, x_sample, cache_sb_k, cache_sb_v, cache_sa_k, cache_sa_v,
              cache_idx_k, state_ffn_conv, c_prompt, c_sample, w_ada, b_ada, g_mix,
              w_in, w_gate, w_br_sb, w_br_sa, w_out, rel_table, g_ffn, w_up, conv_w,
              conv_b, w_down, g_final):
    xp, xs = x_prompt, x_sample
    new_p, new_s = [], []
    for l in range(DEPTH):
        w = (w_ada[l], b_ada[l], g_mix[l], w_in[l], w_gate[l], w_br_sb[l], w_br_sa[l],
             w_out[l], rel_table, g_ffn[l], w_up[l], conv_w[l], conv_b[l], w_down[l])
        xp, sp = layer(xp, c_prompt, None, *w)
        past = (cache_sb_k[l], cache_sb_v[l], cache_sa_k[l], cache_sa_v[l],
                cache_idx_k[l], state_ffn_conv[l])
        xs, ss = layer(xs, c_sample, past, *w)
        new_p.append(sp)
        new_s.append(ss)
    y_prompt = rmsnorm(xp, g_final)
    y_sample = rmsnorm(xs, g_final)
    p_sb_k = jnp.stack([s[0] for s in new_p])
    p_sb_v = jnp.stack([s[1] for s in new_p])
    p_sa_k = jnp.stack([s[2] for s in new_p])
    p_sa_v = jnp.stack([s[3] for s in new_p])
    p_idx_k = jnp.stack([s[4] for s in new_p])
    p_conv = jnp.stack([s[5] for s in new_p])
    s_sb_k = jnp.stack([s[0] for s in new_s])
    s_sb_v = jnp.stack([s[1] for s in new_s])
    s_sa_k = jnp.stack([s[2] for s in new_s])
    s_sa_v = jnp.stack([s[3] for s in new_s])
    s_idx_k = jnp.stack([s[4] for s in new_s])
    s_conv = jnp.stack([s[5] for s in new_s])
    return (y_prompt, y_sample, p_sb_k, p_sb_v, p_sa_k, p_sa_v, p_idx_k, p_conv,
            s_sb_k, s_sb_v, s_sa_k, s_sa_v, s_idx_k, s_conv)
```

```python
import numpy as np
import ml_dtypes
from contextlib import ExitStack
import concourse.bass as bass
import concourse.mybir as mybir
from concourse.bass_utils import run_bass_kernel_spmd

F32 = mybir.dt.float32
BF16 = mybir.dt.bfloat16
FP8 = mybir.dt.float8e4
AF = mybir.ActivationFunctionType
ALU = mybir.AluOpType
AX = mybir.AxisListType

D = 2048
DFF = 5632
NFF = DFF // 128
NH = 8
NHI = 16
INC = 7248
KVC = 4160
C_QSB, C_KSB, C_VSB, C_QSA, C_KSA, C_VSA, C_QIX, C_KIX, C_WIX = 0, 1024, 2048, 3072, 4096, 5120, 6144, 7168, 7232
EPS = 1e-6
PAST = 2048
DSEQ = 64
LKS = PAST + DSEQ
TOPK = 256
NBIS = 18
BRANGE = 32.0
NEAR = 640
SCALE = 128 ** -0.5
MASKV = -30000.0
IMASKV = -1024.0


class Cfg:
    def __init__(self, SEQ=16384, NB=2, G=4, NSLOT=9, STRIDE=456, NS=4, TT=1024):
        self.SEQ, self.NB, self.G, self.NSLOT, self.STRIDE, self.NS, self.TT = SEQ, NB, G, NSLOT, STRIDE, NS, TT
        self.ncols = STRIDE + 2
        self.ncores = NB * G
        self.NR = 1 + NS
        self.kext = [min(SEQ, STRIDE * (G * m + G)) for m in range(NSLOT)]
        self.kextb = [(k + 127) // 128 for k in self.kext]
        dmax = max(128 * (self.kextb[m] - 1) - STRIDE * G * m + 2 for m in range(NSLOT))
        self.U0 = dmax
        self.dmin = -NEAR - 127 - 128
        self.UL = self.U0 - self.dmin + self.ncols
        self.GL = self.UL + 128
        self.U0s = 128 * 16 - PAST
        self.dmins = -NEAR - 127 - 128
        self.ULs = self.U0s - self.dmins + DSEQ
        self.GLs = self.ULs + 128

    def near(self, m, kb):
        return 128 * kb - self.STRIDE * self.G * m + 2 >= -NEAR - 127

    def sbmasked(self, m, kb):
        return 128 * kb + 127 >= self.STRIDE * self.G * m - 2

    def idxmasked(self, m, kt):
        lim = ((self.STRIDE * self.G * m - 2) // 64 + 1) * 64
        return 512 * kt + 511 >= lim


def blocks_of(n):
    out = []
    o = 0
    while o < n:
        out.append((o, min(128, n - o)))
        o += 128
    return out


class Res:
    __slots__ = ("w", "r", "excl")

    def __init__(self, excl=False):
        self.w = None
        self.r = []
        self.excl = excl


class DSem:
    def __init__(self, sem):
        self.sem = sem
        self.count = 0


class Sched:
    ENGS = ("pe", "act", "dve", "pool", "sp")

    def __init__(self, nc, stack, ndsem):
        self.nc = nc
        self.ops = {e: [] for e in self.ENGS}
        self.sem = {}
        self.count = {e: 0 for e in self.ENGS}
        self.known = {e: {} for e in self.ENGS}
        for e in ("pe", "act", "dve", "pool"):
            self.sem[e] = stack.enter_context(nc.semaphore("sem_" + e))
        self.dsems = [DSem(stack.enter_context(nc.semaphore("dsem%d" % i))) for i in range(ndsem)]
        self.free = list(self.dsems[:-12])
        self.free_sw = list(self.dsems[-12:])
        self.dmap = {id(d.sem): d for d in self.dsems}

    def getd(self, sw=False):
        return self.free_sw.pop() if sw else self.free.pop()

    def putd(self, ds, sw=False):
        (self.free_sw if sw else self.free).extend(ds)

    def _deps(self, eng, reads, writes):
        deps = {}

        def add(tok):
            if tok is None:
                return
            key = id(tok[0])
            d = self.dmap.get(key)
            if d is not None:
                tok = (tok[0], d.count)
            if key not in deps or deps[key][1] < tok[1]:
                deps[key] = tok
        for r in reads:
            add(r.w)
            if r.excl:
                for t in r.r:
                    add(t)
        for w in writes:
            add(w.w)
            for t in w.r:
                add(t)
        out = []
        kn = self.known[eng]
        for key, tok in deps.items():
            if kn.get(key, 0) >= tok[1]:
                continue
            kn[key] = tok[1]
            out.append(tok)
        return out

    def op(self, eng, fn, reads=(), writes=(), dsem=None):
        waits = self._deps(eng, reads, writes)
        if dsem is not None and dsem.count > 0:
            key = id(dsem.sem)
            if self.known[eng].get(key, 0) < dsem.count:
                self.known[eng][key] = dsem.count
                waits = [w for w in waits if id(w[0]) != key] + [(dsem.sem, dsem.count)]
        if dsem is None:
            self.count[eng] += 1
            tok = (self.sem[eng], self.count[eng])
            inc = (self.sem[eng], 1)
        else:
            dsem.count += 16
            tok = (dsem.sem, dsem.count)
            inc = (dsem.sem, 16)
        self.ops[eng].append((waits, fn, inc))
        for r in reads:
            if len(r.r) > 24:
                r.r = r.r[-24:]
            r.r.append(tok)
        for w in writes:
            w.w = tok
            w.r = []
        return tok

    def barrier(self):
        toks = [(self.sem[e], self.count[e]) for e in ("pe", "act", "dve", "pool") if self.count[e] > 0]
        toks += [(d.sem, d.count) for d in self.dsems if d.count > 0]
        for e in self.ENGS:
            kn = self.known[e]
            waits = []
            for tok in toks:
                key = id(tok[0])
                if kn.get(key, 0) >= tok[1]:
                    continue
                kn[key] = tok[1]
                waits.append(tok)
            if waits:
                self.ops[e].append((waits, None, None))

    def flush(self):
        self.barrier()
        ops = self.ops
        self.ops = {e: [] for e in self.ENGS}

        def mk(e):
            def body(h):
                for waits, fn, inc in ops[e]:
                    for (s, v) in waits:
                        h.wait_ge(s, v)
                    if fn is not None:
                        inst = fn(h)
                        inst.then_inc(inc[0], inc[1])
            return body
        with self.nc.Block() as block:
            block.tensor(mk("pe"))
            block.scalar(mk("act"))
            block.vector(mk("dve"))
            block.gpsimd(mk("pool"))
            block.sync(mk("sp"))


class Ring:
    uid = 0

    def __init__(self, S, nc, st, name, shape, dt, n, with_dsem=True, sw_dsem=False):
        self.S = S
        self.d2 = [S.getd(sw=True) for _ in range(n)] if sw_dsem else None
        Ring.uid += 1
        self.t = [st.enter_context(nc.sbuf_tensor("%s%d_r%d" % (name, i, Ring.uid), shape, dt)) for i in range(n)]
        self.r = [Res() for _ in range(n)]
        self.d = [S.getd() for _ in range(n)] if with_dsem else None
        self.n = n
        self.i = 0

    def nxt(self):
        i = self.i % self.n
        self.i += 1
        return self.t[i], self.r[i], (self.d[i] if self.d else None)

    def release(self):
        if self.d:
            self.S.putd(self.d)
        if self.d2:
            self.S.putd(self.d2, sw=True)


class Pre:
    def __init__(self, thunks, depth):
        self.th = thunks
        self.depth = depth
        self.nxt_ = 0
        self.got = {}

    def get(self, i):
        while self.nxt_ < len(self.th) and self.nxt_ <= i + self.depth:
            self.got[self.nxt_] = self.th[self.nxt_]()
            self.nxt_ += 1
        return self.got.pop(i)


def build(cfg):
    nc = bass.Bass("TRN2", target_bir_lowering=False)
    SEQ, G, NSLOT, STRIDE, NS, NR, ncols = cfg.SEQ, cfg.G, cfg.NSLOT, cfg.STRIDE, cfg.NS, cfg.NR, cfg.ncols

    def din(name, shape, dt=F32):
        return nc.dram_tensor(name, list(shape), dt, kind="ExternalInput").ap()

    def dout(name, shape, dt=F32):
        return nc.dram_tensor(name, list(shape), dt, kind="ExternalOutput").ap()

    def dscr(name, shape, dt=BF16):
        return nc.dram_tensor(name, list(shape), dt, kind="Internal").ap()

    xp = din("xp", [SEQ, D])
    xw = din("xw", [NSLOT, ncols, D])
    xs = din("xs", [NS, DSEQ, D])
    cache = [din(n, [NS, PAST, 1024]) for n in ("c_sb_k", "c_sb_v", "c_sa_k", "c_sa_v")]
    cix = din("c_ix", [NS, PAST, 64])
    cst = din("c_st", [NS, 128, NFF, 2])
    cT = din("cT", [128, 16, NR])
    w_f32 = {
        "ada": din("w_ada", [D, 6 * D]), "in": din("w_in", [D, INC]), "gate": din("w_gate", [D, 2 * D]),
        "brsb": din("w_br_sb", [1024, D]), "brsa": din("w_br_sa", [1024, D]), "out": din("w_out", [D, D]),
        "up": din("w_up", [D, 2 * DFF]), "down": din("w_down", [DFF, D]),
    }
    b_ada_rows = din("b_ada_rows", [NR, 6 * D])
    g_mix = din("g_mix", [1, D])
    g_ffn = din("g_ffn", [1, D])
    g_final = din("g_final", [1, D])
    rel_table = din("rel_table", [32, 8])
    cwb_in = din("cwb", [128, NFF, 4])
    constf = din("constf", [128, 4, 128])
    sbmask_in = din("sbmask", [cfg.n_sbm, 128, ncols], BF16)
    idxmask_in = din("idxmask", [cfg.n_im, 128, 512], BF16)
    sbmask_s_in = din("sbmask_s", [128, DSEQ], BF16)
    oh_p = din("oh_p", [32, cfg.GL])
    oh_s = din("oh_s", [32, cfg.GLs])
    hflag_in = din("hflag", [128, NSLOT])
    wix_in = din("wix", [128, 16, 16])

    y_p = dout("y_p", [NSLOT, STRIDE, D])
    y_s = dout("y_s", [NS, DSEQ, D])
    okv_p = dout("okv_p", [SEQ, KVC])
    okv_s = dout("okv_s", [NS, DSEQ, KVC])
    pconv = dout("pconv", [128, NFF, 2])
    sconv = dout("sconv", [NS, 128, NFF, 2])

    wbf = {k: dscr("wbf_" + k, v.shape) for k, v in w_f32.items()}
    modrow = dscr("modrow", [NR, 6 * D], F32)

    def mkctx(name, Lk):
        return {"KT": [dscr(name + "_ktsb", [NH, 128, Lk]), dscr(name + "_ktsa", [NH, 128, Lk])],
                "V": [dscr(name + "_vsb", [Lk, 1024]), dscr(name + "_vsa", [Lk, 1024])],
                "KI2": dscr(name + "_ki2", [128, Lk]), "Lk": Lk}
    ctx_p = mkctx("cp", SEQ)
    ctx_s = [mkctx("cs%d" % s, LKS) for s in range(NS)]
    gvec_p = dscr("gvec_p", [8, cfg.GL])
    gvec_s = dscr("gvec_s", [8, cfg.GLs])
    strip_p = dscr("strip_p", [8, 128, cfg.UL])
    strip_s = dscr("strip_s", [8, 128, cfg.ULs])
    QTs = dscr("QTs", [4, 8, 128, ncols])
    hTs = dscr("hTs", [16, 128, ncols])
    OTs = dscr("OTs", [2, 8, 128, ncols])

    with ExitStack() as gst:
        S = Sched(nc, gst, 80)

        uid = [0]

        def PT(name, shape, dt=F32, st=gst):
            uid[0] += 1
            return st.enter_context(nc.sbuf_tensor("%s_u%d" % (name, uid[0]), list(shape), dt))

        identf = PT("identf", [128, 128])
        cbf = PT("cbf", [128, 4, 128], BF16)
        cwb = PT("cwb_t", [128, NFF, 4])
        CH = PT("CH", [128, 8])
        hflag = PT("hflag_t", [128, NSLOT])
        wraw = PT("wraw", [128, 4, NHI])
        convc = PT("convc", [128, NFF, 2])
        r_const = Res()
        r_wraw = Res()
        r_convc = Res()
        identb = cbf[:, 0, :]
        nidentb = cbf[:, 1, :]
        trib = cbf[:, 2, :]
        onesb = cbf[:, 3, :]
        PB = [gst.enter_context(nc.psum_tensor("pb%d" % i, [128, 512], F32)) for i in range(8)]
        RB = [Res(excl=True) for _ in range(8)]
        r_scr = {"modrow": Res(), "wbf": Res(), "strip": Res(), "QT": Res(), "hT": Res(), "OT": Res(), "ctx": Res(),
                 "out": Res()}

        def setup():
            with ExitStack() as st:
                dl = [S.getd() for _ in range(6)]
                dsw = S.getd(sw=True)
                for k in ("in", "ada", "gate", "brsb", "brsa", "out", "up", "down"):
                    src = w_f32[k]
                    n = src.shape[0] * src.shape[1] // 2048
                    s2 = src.rearrange("r c -> (r c)").rearrange("(n k) -> n k", k=2048)
                    d2 = wbf[k].rearrange("r c -> (r c)").rearrange("(n k) -> n k", k=2048)
                    o = 0
                    while o < n:
                        m = min(4096, n - o)
                        S.op("pool", (lambda a, b: lambda h: h.dma_start(out=a, in_=b))(d2[o:o + m], s2[o:o + m]),
                             writes=[Res()], dsem=dsw)
                        o += m
                ctmp = PT("ctmp", [128, 4, 128], F32, st)
                rt = Res()
                S.op("sp", lambda h: h.dma_start(out=ctmp[:], in_=constf), writes=[rt], dsem=dl[1])
                S.op("sp", lambda h: h.dma_start(out=identf[:], in_=constf[:, 0, :]), writes=[r_const], dsem=dl[2])
                S.op("sp", lambda h: h.dma_start(out=cwb[:], in_=cwb_in), writes=[r_const], dsem=dl[2])
                S.op("sp", lambda h: h.dma_start(out=hflag[:], in_=hflag_in), writes=[r_const], dsem=dl[2])
                S.op("sp", lambda h: h.dma_start(out=CH[:], in_=rel_table[15:16, :].to_broadcast([128, 8])),
                     writes=[r_const], dsem=dl[2])
                S.op("dve", lambda h: h.tensor_copy(out=cbf[:], in_=ctmp[:]), reads=[rt], writes=[r_const])
                cTt = PT("cTt", [128, 16, NR], F32, st)
                sT = PT("sT", [128, 16, NR], BF16, st)
                rc, rs = Res(), Res()
                S.op("sp", lambda h: h.dma_start(out=cTt[:], in_=cT), writes=[rc], dsem=dl[1])
                S.op("act", lambda h: h.activation(out=sT[:], in_=cTt[:], func=AF.Silu), reads=[rc], writes=[rs])
                relt = PT("relt", [32, 8], F32, st)
                rr = Res()
                S.op("sp", lambda h: h.dma_start(out=relt[:], in_=rel_table), writes=[rr], dsem=dl[1])
                def mkstrip(oh, gv, GLx, strip, ULx, nm):
                    oht = PT("oht" + nm, [32, GLx], F32, st)
                    gst_t = PT("gst" + nm, [8, GLx], BF16, st)
                    ro, rg, rgv = Res(), Res(), Res()
                    S.op("sp", (lambda a, b: lambda h: h.dma_start(out=a, in_=b))(oht[:], oh), writes=[ro], dsem=dl[1])
                    o = 0
                    i = 0
                    while o < GLx:
                        m = min(512, GLx - o)
                        bk = 6 + (i % 2)
                        S.op("pe", (lambda bk, o, m: lambda h: h.matmul(PB[bk][:8, :m], relt[:], oht[:, o:o + m],
                                                                         start=True, stop=True))(bk, o, m),
                             reads=[rr, ro], writes=[RB[bk]])
                        S.op("act", (lambda bk, o, m: lambda h: h.activation(out=gst_t[:, o:o + m], in_=PB[bk][:8, :m],
                                                                             func=AF.Copy))(bk, o, m),
                             reads=[RB[bk]], writes=[rg])
                        o += m
                        i += 1
                    S.op("sp", (lambda a, b: lambda h: h.dma_start(out=a, in_=b))(gv, gst_t[:]), reads=[rg], writes=[rgv],
                         dsem=dl[3])
                    for p in range(128):
                        S.op("sp", (lambda p, strip, gv, ULx: lambda h: h.dma_start(
                            out=strip[:, p, :], in_=gv[:, 127 - p:127 - p + ULx]))(p, strip, gv, ULx),
                            reads=[rgv], writes=[Res()], dsem=dl[4])
                mkstrip(oh_p, gvec_p, cfg.GL, strip_p, cfg.UL, "p")
                mkstrip(oh_s, gvec_s, cfg.GLs, strip_s, cfg.ULs, "s")
                S.flush()
                wr = Ring(S, nc, st, "wada", [128, 16, 512], BF16, 2)
                br = Ring(S, nc, st, "bada", [NR, 512], F32, 2)
                ms = Ring(S, nc, st, "mst", [NR, 512], F32, 2)
                wv = wbf["ada"].rearrange("(kc p) c -> p kc c", p=128)
                for pc in range(24):
                    wt, wres, wd = wr.nxt()
                    bt, bres, bd = br.nxt()
                    mt, mres, md = ms.nxt()
                    S.op("sp", (lambda wt, pc: lambda h: h.dma_start(out=wt[:], in_=wv[:, :, pc * 512:(pc + 1) * 512]))(wt, pc),
                         writes=[wres], dsem=wd)
                    S.op("sp", (lambda bt, pc: lambda h: h.dma_start(out=bt[:], in_=b_ada_rows[:, pc * 512:(pc + 1) * 512]))(bt, pc),
                         writes=[bres], dsem=bd)
                    bk = pc % 2

                    def mmf(h, wt=wt, bk=bk):
                        for kc in range(16):
                            ins = h.matmul(PB[bk][:NR, :], sT[:, kc, :], wt[:, kc, :], start=(kc == 0), stop=(kc == 15))
                        return ins
                    S.op("pe", mmf, reads=[rs, wres], writes=[RB[bk]])
                    S.op("dve", (lambda mt, bt, bk: lambda h: h.tensor_tensor(out=mt[:], in0=PB[bk][:NR, :], in1=bt[:],
                                                                             op=ALU.add))(mt, bt, bk),
                         reads=[RB[bk], bres], writes=[mres])
                    S.op("sp", (lambda mt, pc: lambda h: h.dma_start(out=modrow[:, pc * 512:(pc + 1) * 512], in_=mt[:]))(mt, pc),
                         reads=[mres], writes=[Res()], dsem=md)
                S.flush()
                wr.release(); br.release(); ms.release()
                S.putd(dl)
                S.putd([dsw], sw=True)

        def load_mod(st, dl, r, which):
            A = PT("modA", [128, D], F32, st)
            Bt = PT("modB", [128, D], F32, st)
            Gt = PT("modG", [128, D], F32, st)
            rA, rB, rG = Res(), Res(), Res()
            base = 0 if which == 1 else 3 * D
            g = g_mix if which == 1 else g_ffn
            S.op("sp", lambda h: h.dma_start(out=A[:], in_=modrow[r:r + 1, base + D:base + 2 * D].to_broadcast([128, D])),
                 writes=[rA], dsem=dl[0])
            S.op("sp", lambda h: h.dma_start(out=Bt[:], in_=modrow[r:r + 1, base:base + D].to_broadcast([128, D])),
                 writes=[rB], dsem=dl[1])
            S.op("sp", lambda h: h.dma_start(out=Gt[:], in_=g.to_broadcast([128, D])), writes=[rG], dsem=dl[2])
            S.op("dve", lambda h: h.scalar_tensor_tensor(out=A[:], in0=A[:], scalar=1.0, in1=Gt[:], op0=ALU.add, op1=ALU.mult),
                 reads=[rA, rG], writes=[rA])
            return A, Bt, rA, rB

        def load_row_rep(st, dl, name, src_row):
            t = PT(name, [128, D], F32, st)
            rr = Res()
            S.op("sp", lambda h: h.dma_start(out=t[:], in_=src_row.to_broadcast([128, D])), writes=[rr], dsem=dl)
            return t, rr

        class NormT:
            def __init__(self, st, name):
                self.junk = PT(name + "_junk", [128, D], BF16, st)
                self.hb = [PT(name + "_hb%d" % i, [128, D], F32, st) for i in range(2)]
                self.rhb = [Res(), Res()]
                self.sm = PT(name + "_sm", [128, 8], F32, st)
                self.rj = Res()
                self.rsm = Res()
                self.k = 0

            def run(self, xt, rx, rows, A, Bt, rA, rB, dst, rdst, c0):
                k = self.k
                self.k += 1
                hb, rhb = self.hb[k % 2], self.rhb[k % 2]
                sm, rsm, junk, rj = self.sm, self.rsm, self.junk, self.rj
                S.op("dve", lambda h: h.scalar_tensor_tensor(out=junk[:rows], in0=xt[:rows], scalar=1.0, in1=xt[:rows],
                                                             op0=ALU.mult, op1=ALU.mult, accum_out=sm[:rows, 0:1]),
                     reads=[rx], writes=[rj, rsm])
                S.op("dve", lambda h: h.tensor_scalar(out=sm[:rows, 1:2], in0=sm[:rows, 0:1], scalar1=1.0 / D, scalar2=EPS,
                                                      op0=ALU.mult, op1=ALU.add), reads=[rsm], writes=[rsm])
                S.op("act", lambda h: h.activation(out=sm[:rows, 2:3], in_=sm[:rows, 1:2], func=AF.Ln), reads=[rsm], writes=[rsm])
                S.op("act", lambda h: h.activation(out=sm[:rows, 3:4], in_=sm[:rows, 2:3], func=AF.Exp, scale=-0.5),
                     reads=[rsm], writes=[rsm])
                S.op("dve", lambda h: h.scalar_tensor_tensor(out=hb[:rows], in0=xt[:rows], scalar=sm[:rows, 3:4], in1=A[:rows],
                                                             op0=ALU.mult, op1=ALU.mult), reads=[rx, rsm, rA], writes=[rhb])
                S.op("pool", lambda h: h.tensor_tensor(out=hb[:rows], in0=hb[:rows], in1=Bt[:rows], op=ALU.add),
                     reads=[rhb, rB], writes=[rhb])
                for g4 in range(4):
                    bk = 6 + (g4 % 2)

                    def tp(h, g4=g4, bk=bk):
                        for q in range(4):
                            fc = g4 * 4 + q
                            ins = h.transpose(out=PB[bk][:, q * 128:q * 128 + rows], in_=hb[:rows, fc * 128:(fc + 1) * 128],
                                              identity=identf[:rows, :rows])
                        return ins
                    S.op("pe", tp, reads=[rhb, r_const], writes=[RB[bk]])
                    src = PB[bk][:, :].rearrange("p (q c) -> p q c", q=4)[:, :, 0:rows]
                    S.op("act", (lambda g4, src: lambda h: h.activation(out=dst[:, g4 * 4:(g4 + 1) * 4, c0:c0 + rows], in_=src,
                                                                        func=AF.Copy))(g4, src),
                         reads=[RB[bk]], writes=[rdst])

        KVBLK = [(C_KSB, 512, "k", 0, 0, 0), (C_KSB + 512, 512, "k", 0, 4, 512),
                 (C_VSB, 512, "v", 0, 0, 1024), (C_VSB + 512, 512, "v", 0, 4, 1536),
                 (C_KSA, 512, "k", 1, 0, 2048), (C_KSA + 512, 512, "k", 1, 4, 2560),
                 (C_VSA, 512, "v", 1, 0, 3072), (C_VSA + 512, 512, "v", 1, 4, 3584),
                 (C_KIX, 64, "i", 0, 0, 4096)]

        def phaseA(ctx, xsrc, ntok, tok0, okv, r, TT):
            with ExitStack() as st:
                dl = [S.getd() for _ in range(4)]
                A, Bt, rA, rB = load_mod(st, dl, r, 1)
                nt = NormT(st, "na")
                ncol_t = min(TT, ntok)
                hT = PT("a_hT", [128, 16, ncol_t], BF16, st)
                rhT = Res()
                xr = Ring(S, nc, st, "a_x", [128, D], F32, 3)
                wr = Ring(S, nc, st, "a_w", [128, 16, 512], BF16, 3)
                sf = Ring(S, nc, st, "a_sf", [128, 512], F32, 4, sw_dsem=True)
                kts = Ring(S, nc, st, "a_kt", [128, 4, ncol_t], BF16, 2)
                ki2 = Ring(S, nc, st, "a_ki2", [128, ncol_t], BF16, 2)
                kid = Ring(S, nc, st, "a_kid", [128, 128], F32, 2, with_dsem=False)
                win = wbf["in"].rearrange("(kc p) c -> p kc c", p=128)
                mmk = 0
                ntiles = (ntok + TT - 1) // TT

                def wthunk(wc0, wn):
                    def f():
                        wt, wres, wd = wr.nxt()
                        S.op("sp", lambda h: h.dma_start(out=wt[:, :, :wn], in_=win[:, :, wc0:wc0 + wn]),
                             reads=[r_scr["wbf"]], writes=[wres], dsem=wd)
                        return wt, wres
                    return f
                wpre = Pre([wthunk(b[0], b[1]) for _ in range(ntiles) for b in KVBLK], 1)
                wi = 0
                for t0 in range(0, ntok, TT):
                    n = min(TT, ntok - t0)
                    blks = blocks_of(n)
                    for (b0, rows) in blks:
                        xt, rx, xd = xr.nxt()
                        S.op("sp", (lambda xt, a, rows: lambda h: h.dma_start(out=xt[:rows], in_=a))(xt, xsrc[t0 + b0:t0 + b0 + rows, :], rows),
                             writes=[rx], dsem=xd)
                        nt.run(xt, rx, rows, A, Bt, rA, rB, hT, rhT, b0)
                    for (wc0, wn, kind, which, head0, oc0) in KVBLK:
                        wt, wres = wpre.get(wi)
                        wi += 1
                        if kind == "k":
                            kt, rkt, kd = kts.nxt()
                        if kind == "i":
                            k2, rk2, k2d = ki2.nxt()
                        for (b0, rows) in blks:
                            bk = mmk % 3
                            mmk += 1

                            def mmf(h, wt=wt, bk=bk, b0=b0, rows=rows, wn=wn):
                                for kc in range(16):
                                    ins = h.matmul(PB[bk][:rows, :wn], hT[:, kc, b0:b0 + rows], wt[:, kc, :wn],
                                                   start=(kc == 0), stop=(kc == 15))
                                return ins
                            S.op("pe", mmf, reads=[rhT, wres], writes=[RB[bk]])
                            sft, rsf, sfd = sf.nxt()
                            sfd2 = sf.d2[(sf.i - 1) % sf.n]
                            eng = "act" if (mmk % 2) else "dve"
                            if eng == "act":
                                S.op("act", (lambda sft, bk, rows, wn: lambda h: h.activation(out=sft[:rows, :wn], in_=PB[bk][:rows, :wn],
                                                                                              func=AF.Copy))(sft, bk, rows, wn),
                                     reads=[RB[bk]], writes=[rsf])
                            else:
                                S.op("dve", (lambda sft, bk, rows, wn: lambda h: h.tensor_copy(out=sft[:rows, :wn], in_=PB[bk][:rows, :wn]))(sft, bk, rows, wn),
                                     reads=[RB[bk]], writes=[rsf])
                            S.op("sp", (lambda sft, rows, wn, a: lambda h: h.dma_start(out=a, in_=sft[:rows, :wn]))(
                                sft, rows, wn, okv[t0 + b0:t0 + b0 + rows, oc0:oc0 + wn]),
                                reads=[rsf], writes=[Res()], dsem=sfd)
                            if kind == "v":
                                vdst = ctx["V"][which][tok0 + t0 + b0:tok0 + t0 + b0 + rows, head0 * 128:head0 * 128 + 512]
                                S.op("pool", (lambda sft, rows, a: lambda h: h.dma_start(out=a, in_=sft[:rows, :]))(sft, rows, vdst),
                                     reads=[rsf], writes=[Res()], dsem=sfd2)
                            elif kind == "k":
                                bk2 = 3 + (mmk % 2)

                                def tpf(h, sft=sft, bk2=bk2, rows=rows):
                                    for q in range(4):
                                        ins = h.transpose(out=PB[bk2][:, q * 128:q * 128 + rows], in_=sft[:rows, q * 128:(q + 1) * 128],
                                                          identity=identf[:rows, :rows])
                                    return ins
                                S.op("pe", tpf, reads=[rsf, r_const], writes=[RB[bk2]])
                                src = PB[bk2][:, :].rearrange("p (q c) -> p q c", q=4)[:, :, 0:rows]
                                S.op("dve", (lambda kt, src, b0, rows: lambda h: h.tensor_copy(out=kt[:, :, b0:b0 + rows], in_=src))(kt, src, b0, rows),
                                     reads=[RB[bk2]], writes=[rkt])
                            else:
                                kdt, rkd, _ = kid.nxt()
                                S.op("pool", (lambda kdt, sft, rows: lambda h: h.tensor_copy(out=kdt[:rows, 0:64], in_=sft[:rows, 0:64]))(kdt, sft, rows),
                                     reads=[rsf], writes=[rkd])
                                S.op("pool", (lambda kdt, sft, rows: lambda h: h.tensor_copy(out=kdt[:rows, 64:128], in_=sft[:rows, 0:64]))(kdt, sft, rows),
                                     reads=[rsf, rkd], writes=[rkd])
                                bk2 = 5
                                S.op("pe", (lambda kdt, rows: lambda h: h.transpose(out=PB[5][:, :rows], in_=kdt[:rows, :],
                                                                                    identity=identf[:rows, :rows]))(kdt, rows),
                                     reads=[rkd, r_const], writes=[RB[5]])
                                S.op("dve", (lambda k2, b0, rows: lambda h: h.tensor_copy(out=k2[:, b0:b0 + rows], in_=PB[5][:, :rows]))(k2, b0, rows),
                                     reads=[RB[5]], writes=[rk2])
                        if kind == "k":
                            for q in range(4):
                                S.op("sp", (lambda kt, q, a, n: lambda h: h.dma_start(out=a, in_=kt[:, q, :n]))(
                                    kt, q, ctx["KT"][which][head0 + q, :, tok0 + t0:tok0 + t0 + n], n),
                                    reads=[rkt], writes=[Res()], dsem=kd)
                        if kind == "i":
                            S.op("sp", (lambda k2, a, n: lambda h: h.dma_start(out=a, in_=k2[:, :n]))(
                                k2, ctx["KI2"][:, tok0 + t0:tok0 + t0 + n], n), reads=[rk2], writes=[Res()], dsem=k2d)
                S.flush()
                for rg in (xr, wr, sf, kts, ki2):
                    rg.release()
                S.putd(dl)

        def cache_import(s):
            ctx = ctx_s[s]
            with ExitStack() as st:
                dl = [S.getd() for _ in range(2)]
                dsw = [S.getd(sw=True) for _ in range(2)]
                for which, ci in ((0, 1), (1, 3)):
                    S.op("pool", (lambda a, b: lambda h: h.dma_start(out=a, in_=b))(ctx["V"][which][0:PAST, :], cache[ci][s]),
                         writes=[Res()], dsem=dsw[which])
                cr = Ring(S, nc, st, "ci_c", [128, 1024], F32, 3)
                kst = Ring(S, nc, st, "ci_k", [128, 8, 512], BF16, 2)
                for which, ci in ((0, 0), (1, 2)):
                    for g4 in range(PAST // 512):
                        kt, rkt, kd = kst.nxt()
                        for bb in range(4):
                            tb = g4 * 4 + bb
                            ct, rct, cd = cr.nxt()
                            S.op("sp", (lambda ct, a: lambda h: h.dma_start(out=ct[:], in_=a))(ct, cache[ci][s, tb * 128:(tb + 1) * 128, :]),
                                 writes=[rct], dsem=cd)
                            for hh in range(2):
                                bk = (tb * 2 + hh) % 4

                                def tpf(h, ct=ct, bk=bk, hh=hh):
                                    for q in range(4):
                                        hd = hh * 4 + q
                                        ins = h.transpose(out=PB[bk][:, q * 128:(q + 1) * 128], in_=ct[:, hd * 128:(hd + 1) * 128],
                                                          identity=identf[:])
                                    return ins
                                S.op("pe", tpf, reads=[rct, r_const], writes=[RB[bk]])
                                src = PB[bk][:, :].rearrange("p (q c) -> p q c", q=4)
                                eng = "act" if hh else "dve"
                                if eng == "act":
                                    S.op("act", (lambda kt, src, hh, bb: lambda h: h.activation(out=kt[:, hh * 4:hh * 4 + 4, bb * 128:(bb + 1) * 128],
                                                                                               in_=src, func=AF.Copy))(kt, src, hh, bb),
                                         reads=[RB[bk]], writes=[rkt])
                                else:
                                    S.op("dve", (lambda kt, src, hh, bb: lambda h: h.tensor_copy(out=kt[:, hh * 4:hh * 4 + 4, bb * 128:(bb + 1) * 128],
                                                                                                in_=src))(kt, src, hh, bb),
                                         reads=[RB[bk]], writes=[rkt])
                        for hd in range(8):
                            S.op("sp", (lambda kt, hd, a: lambda h: h.dma_start(out=a, in_=kt[:, hd, :]))(
                                kt, hd, ctx["KT"][which][hd, :, g4 * 512:(g4 + 1) * 512]), reads=[rkt], writes=[Res()], dsem=kd)
                ir = Ring(S, nc, st, "ci_i", [128, 128], F32, 3)
                i2 = Ring(S, nc, st, "ci_i2", [128, 512], BF16, 2)
                for g4 in range(PAST // 512):
                    k2, rk2, k2d = i2.nxt()
                    for bb in range(4):
                        tb = g4 * 4 + bb
                        it, rit, idd = ir.nxt()
                        S.op("sp", (lambda it, a: lambda h: h.dma_start(out=it[:, 0:64], in_=a))(it, cix[s, tb * 128:(tb + 1) * 128, :]),
                             writes=[rit], dsem=idd)
                        S.op("sp", (lambda it, a: lambda h: h.dma_start(out=it[:, 64:128], in_=a))(it, cix[s, tb * 128:(tb + 1) * 128, :]),
                             writes=[rit], dsem=idd)
                        S.op("pe", (lambda it: lambda h: h.transpose(out=PB[5][:, :128], in_=it[:], identity=identf[:]))(it),
                             reads=[rit, r_const], writes=[RB[5]])
                        S.op("dve", (lambda k2, bb: lambda h: h.tensor_copy(out=k2[:, bb * 128:(bb + 1) * 128], in_=PB[5][:, :128]))(k2, bb),
                             reads=[RB[5]], writes=[rk2])
                    S.op("sp", (lambda k2, a: lambda h: h.dma_start(out=a, in_=k2[:]))(k2, ctx["KI2"][:, g4 * 512:(g4 + 1) * 512]),
                         reads=[rk2], writes=[Res()], dsem=k2d)
                S.flush()
                for rg in (cr, kst, ir, i2):
                    rg.release()
                S.putd(dl)
                S.putd(dsw, sw=True)

        def win_q(xsrc, n, r):
            with ExitStack() as st:
                dl = [S.getd() for _ in range(4)]
                A, Bt, rA, rB = load_mod(st, dl, r, 1)
                nt = NormT(st, "nq")
                hT = PT("q_hT", [128, 16, n], BF16, st)
                rhT = Res()
                xr = Ring(S, nc, st, "q_x", [128, D], F32, 3)
                wr = Ring(S, nc, st, "q_w", [128, 16, 512], BF16, 2)
                qs = Ring(S, nc, st, "q_s", [128, n], BF16, 4)
                win = wbf["in"].rearrange("(kc p) c -> p kc c", p=128)
                blks = blocks_of(n)
                for (b0, rows) in blks:
                    xt, rx, xd = xr.nxt()
                    S.op("sp", (lambda xt, a, rows: lambda h: h.dma_start(out=xt[:rows], in_=a))(xt, xsrc[b0:b0 + rows, :], rows),
                         writes=[rx], dsem=xd)
                    nt.run(xt, rx, rows, A, Bt, rA, rB, hT, rhT, b0)
                import os as _os
                qskip = int(_os.environ.get("MK_QSKIP", "0"))
                for fc in range(16 if not (qskip & 1) else 0):
                    S.op("sp", (lambda fc: lambda h: h.dma_start(out=hTs[fc, :, :n], in_=hT[:, fc, :]))(fc), reads=[rhT],
                         writes=[r_scr["hT"]], dsem=dl[3])
                k = 0
                for (kind, c0, scale) in (((0, C_QSB, SCALE), (2, C_QSA, SCALE), (3, C_QIX, 0.125)) if not (qskip & 2) else ()):
                    for hg in range(2):
                        wt, wres, wd = wr.nxt()
                        S.op("sp", (lambda wt, c: lambda h: h.dma_start(out=wt[:], in_=win[:, :, c:c + 512]))(wt, c0 + hg * 512),
                             reads=[r_scr["wbf"]], writes=[wres], dsem=wd)
                        for h4 in range(4):
                            hd = hg * 4 + h4
                            bk = k % 3
                            k += 1

                            def mmf(h, wt=wt, bk=bk, h4=h4):
                                for kc in range(16):
                                    ins = h.matmul(PB[bk][:, :n], wt[:, kc, h4 * 128:(h4 + 1) * 128], hT[:, kc, :], start=(kc == 0), stop=(kc == 15))
                                return ins
                            S.op("pe", mmf, reads=[rhT, wres], writes=[RB[bk]])
                            qt, rq, qd = qs.nxt()
                            S.op("act", (lambda qt, bk, scale: lambda h: h.activation(out=qt[:], in_=PB[bk][:, :n], func=AF.Copy, scale=scale))(qt, bk, scale),
                                 reads=[RB[bk]], writes=[rq])
                            if not (qskip & 8):
                                S.op("sp", (lambda qt, kind, hd: lambda h: h.dma_start(out=QTs[kind, hd, :, :n], in_=qt[:]))(qt, kind, hd),
                                     reads=[rq], writes=[r_scr["QT"]], dsem=qd)
                            if kind == 0 and not (qskip & 16):
                                qt2, rq2, qd2 = qs.nxt()
                                S.op("dve", (lambda qt2, qt: lambda h: h.tensor_scalar(out=qt2[:], in0=qt[:], scalar1=-1.0, scalar2=None,
                                                                                      op0=ALU.mult))(qt2, qt), reads=[rq], writes=[rq2])
                                S.op("sp", (lambda qt2, hd: lambda h: h.dma_start(out=QTs[1, hd, :, :n], in_=qt2[:]))(qt2, hd),
                                     reads=[rq2], writes=[r_scr["QT"]], dsem=qd2)
                wx = PT("q_wx", [128, 16, 16], BF16, st)
                wxf = PT("q_wxf", [128, 16, 16], F32, st)
                rwx, rwxf = Res(), Res()
                S.op("sp", lambda h: h.dma_start(out=wxf[:], in_=wix_in), writes=[rwxf], dsem=dl[3])
                S.op("dve", lambda h: h.tensor_copy(out=wx[:], in_=wxf[:]), reads=[rwxf], writes=[rwx])
                for i, (b0, rows) in enumerate(blks if not (qskip & 4) else []):
                    def mmw(h, b0=b0, rows=rows):
                        for kc in range(16):
                            ins = h.matmul(PB[4][:rows, :16], hT[:, kc, b0:b0 + rows], wx[:, kc, :], start=(kc == 0), stop=(kc == 15))
                        return ins
                    S.op("pe", mmw, reads=[rhT, rwx], writes=[RB[4]])
                    S.op("dve", (lambda i, rows: lambda h: h.tensor_copy(out=wraw[:rows, i, :], in_=PB[4][:rows, :16]))(i, rows),
                         reads=[RB[4]], writes=[r_wraw])
                S.flush()
                for rg in (xr, wr, qs):
                    rg.release()
                S.putd(dl)

        def win_attn(ctx, n, kblocks, slot, strip, ULx, u0_of, near_of, sbmask_of, idxmask_of):
            KE = kblocks[-1][0] + kblocks[-1][1]
            qblks = blocks_of(n)
            nqb = len(qblks)
            nkb = len(kblocks)
            with ExitStack() as st:
                dl = [S.getd() for _ in range(6)]
                scores = PT("at_sc", [128, KE], F32, st)
                masks = [PT("at_m%d" % i, [128, KE], FP8, st) for i in range(nqb)]
                rsc = Res()
                rmk = [Res() for _ in range(nqb)]
                qix = PT("at_qix", [128, 8, n], BF16, st)
                rqix = Res()
                S.op("sp", lambda h: h.dma_start(out=qix[:], in_=QTs[3, :, :, :n].rearrange("a p c -> p a c")), reads=[r_scr["QT"]],
                     writes=[rqix], dsem=dl[0])
                CK = 1024
                kvr_k = Ring(S, nc, st, "at_k", [128, CK], BF16, 3)
                kvr_v = Ring(S, nc, st, "at_v", [128, CK // 128, 128], BF16, 3)
                kir = Ring(S, nc, st, "at_ki", [128, 512], BF16, 3)
                imr = Ring(S, nc, st, "at_im", [128, 512], BF16, 2)
                smr = Ring(S, nc, st, "at_sm", [128, n], BF16, 2)
                qr = Ring(S, nc, st, "at_q", [128, n], BF16, 4)
                er = Ring(S, nc, st, "at_e", [128, n], F32, 2, with_dsem=False)
                spr = Ring(S, nc, st, "at_sp", [128, n], BF16, 2, with_dsem=False)
                ar = Ring(S, nc, st, "at_a", [128, n], BF16, 3, with_dsem=False)
                rr_ = Ring(S, nc, st, "at_r", [128, 512], BF16, 4, with_dsem=False)
                osr = Ring(S, nc, st, "at_os", [128, n], BF16, 2)
                Sacc = PT("at_S", [128, n], BF16, st)
                rS = Res()
                Dh = PT("at_Dh", [128, NHI, 128], BF16, st)
                rDh = Res()
                wsm = PT("at_wsm", [128, 2, NHI], F32, st)
                rws = Res()
                bs = PT("at_bs", [128, 16], F32, st)
                rbs = Res()
                stp = PT("at_strip", [128, ULx], BF16, st)
                rstp = Res()
                dstp = dl[1]
                rden = PT("at_rden", [128, n], F32, st)
                rrden = Res()

                chunks = []
                i = 0
                while i < nkb:
                    j = i
                    while j < nkb and kblocks[j][0] + kblocks[j][1] <= kblocks[i][0] + CK:
                        j += 1
                    chunks.append((i, j))
                    i = j

                def load_kv(which, hd, ci):
                    i0, i1 = chunks[ci]
                    k0 = kblocks[i0][0]
                    kn = kblocks[i1 - 1][0] + kblocks[i1 - 1][1] - k0
                    kt, rk, kd = kvr_k.nxt()
                    vt, rv, vd = kvr_v.nxt()
                    S.op("sp", (lambda kt, a, kn: lambda h: h.dma_start(out=kt[:, :kn], in_=a))(kt, ctx["KT"][which][hd, :, k0:k0 + kn], kn),
                         reads=[r_scr["ctx"]], writes=[rk], dsem=kd)
                    nfull = kn // 128
                    if nfull:
                        S.op("sp", (lambda vt, a, nfull: lambda h: h.dma_start(out=vt[:, :nfull, :], in_=a))(
                            vt, ctx["V"][which][k0:k0 + nfull * 128, hd * 128:(hd + 1) * 128].rearrange("(b p) d -> p b d", p=128), nfull),
                            reads=[r_scr["ctx"]], writes=[rv], dsem=vd)
                    rem = kn - nfull * 128
                    if rem:
                        S.op("sp", (lambda vt, a, nfull, rem: lambda h: h.dma_start(out=vt[:rem, nfull, :], in_=a))(
                            vt, ctx["V"][which][k0 + nfull * 128:k0 + kn, hd * 128:(hd + 1) * 128], nfull, rem),
                            reads=[r_scr["ctx"]], writes=[rv], dsem=vd)
                    return kt, rk, vt, rv, k0

                def idx(qi):
                    q0, qrows = qblks[qi]
                    S.op("dve", lambda h: h.tensor_scalar(out=wsm[:qrows, 0, :], in0=wraw[:qrows, qi, :], scalar1=-0.25, scalar2=None,
                                                          op0=ALU.mult), reads=[r_wraw], writes=[rws])
                    S.op("dve", lambda h: h.scalar_tensor_tensor(out=wsm[:qrows, 0, :], in0=wraw[:qrows, qi, :], scalar=0.25, in1=wsm[:qrows, 0, :],
                                                                 op0=ALU.mult, op1=ALU.max), reads=[r_wraw, rws], writes=[rws])
                    S.op("act", lambda h: h.activation(out=wsm[:qrows, 1, :], in_=wraw[:qrows, qi, :], func=AF.Sign), reads=[r_wraw, rws],
                         writes=[rws])
                    for hh in range(NHI):
                        eng = "dve" if hh % 2 else "pool"
                        S.op(eng, (lambda hh: lambda h: h.tensor_scalar(out=Dh[:qrows, hh, :qrows], in0=identb[:qrows, :qrows],
                                                                       scalar1=wsm[:qrows, 1, hh:hh + 1], scalar2=None, op0=ALU.mult))(hh),
                             reads=[rws, r_const], writes=[rDh])
                    nkt = (KE + 511) // 512

                    def kithunk(kt_i):
                        def f():
                            k0 = kt_i * 512
                            kw = min(512, KE - k0)
                            kit, rki, kid_ = kir.nxt()
                            S.op("sp", lambda h: h.dma_start(out=kit[:, :kw], in_=ctx["KI2"][:, k0:k0 + kw]),
                                 reads=[r_scr["ctx"]], writes=[rki], dsem=kid_)
                            return kit, rki
                        return f
                    kpre = Pre([kithunk(k) for k in range(nkt)], 1)
                    def ktile(kt_i):
                        k0 = kt_i * 512
                        kw = min(512, KE - k0)
                        kit, rki = kpre.get(kt_i)
                        sb = 7
                        pend = []
                        for step in range(NHI + 2):
                            if step < NHI:
                                hh = step
                                bk = hh % 2
                                base = 64 * (hh % 2)
                                S.op("pe", (lambda hh, bk, base: lambda h: h.matmul(PB[bk][:qrows, :kw], qix[base:base + 64, hh // 2, q0:q0 + qrows],
                                                                                  kit[base:base + 64, :kw], start=True, stop=True))(hh, bk, base),
                                     reads=[rqix, rki], writes=[RB[bk]])
                                rt, rrt, _ = rr_.nxt()
                                if hh % 2 == 0:
                                    S.op("act", (lambda rt, bk, hh: lambda h: h.activation(out=rt[:qrows, :kw], in_=PB[bk][:qrows, :kw], func=AF.Relu,
                                                                                          scale=wsm[:qrows, 0, hh:hh + 1]))(rt, bk, hh),
                                         reads=[RB[bk], rws], writes=[rrt])
                                else:
                                    S.op("dve", (lambda rt, bk, hh: lambda h: h.tensor_scalar(out=rt[:qrows, :kw], in0=PB[bk][:qrows, :kw],
                                                                                             scalar1=wsm[:qrows, 0, hh:hh + 1], scalar2=0.0,
                                                                                             op0=ALU.mult, op1=ALU.max))(rt, bk, hh),
                                         reads=[RB[bk], rws], writes=[rrt])
                                pend.append((rt, rrt))
                            if step >= 2:
                                h2 = step - 2
                                rt, rrt = pend[h2]
                                S.op("pe", (lambda rt, h2: lambda h: h.matmul(PB[sb][:qrows, :kw], Dh[:qrows, h2, :qrows], rt[:qrows, :kw],
                                                                             start=(h2 == 0), stop=(h2 == NHI - 1)))(rt, h2),
                                     reads=[rrt, rDh], writes=[RB[sb]])
                        ima = idxmask_of(qi, kt_i)
                        if ima is not None:
                            imt, rim, imd = imr.nxt()
                            S.op("sp", (lambda imt, ima: lambda h: h.dma_start(out=imt[:], in_=ima))(imt, ima), writes=[rim], dsem=imd)
                            S.op("dve", (lambda imt: lambda h: h.tensor_tensor(out=scores[:qrows, k0:k0 + kw], in0=PB[sb][:qrows, :kw],
                                                                              in1=imt[:qrows, :kw], op=ALU.add))(imt),
                                 reads=[RB[sb], rim], writes=[rsc])
                        else:
                            S.op("act", lambda h: h.activation(out=scores[:qrows, k0:k0 + kw], in_=PB[sb][:qrows, :kw],
                                                               func=AF.Copy), reads=[RB[sb]], writes=[rsc])
                    for kt_i in range(nkt):
                        ktile(kt_i)
                        yield

                def bisect(qi):
                    q0, qrows = qblks[qi]
                    mk = masks[qi]
                    lo, hi, mid, cnt, ge, d1 = (bs[:qrows, c:c + 1] for c in range(6))
                    S.op("dve", lambda h: h.reduce_max(out=hi, in_=scores[:qrows, :KE], axis=AX.X), reads=[rsc], writes=[rbs])
                    S.op("dve", lambda h: h.tensor_scalar(out=lo, in0=hi, scalar1=-BRANGE, scalar2=None, op0=ALU.add), reads=[rbs], writes=[rbs])
                    S.op("dve", lambda h: h.tensor_scalar(out=hi, in0=hi, scalar1=1e-3, scalar2=None, op0=ALU.add), reads=[rbs], writes=[rbs])
                    for it in range(NBIS):
                        S.op("dve", lambda h: h.tensor_scalar(out=mid, in0=lo, scalar1=hi, scalar2=0.5, op0=ALU.add, op1=ALU.mult),
                             reads=[rbs], writes=[rbs])
                        S.op("dve", lambda h: h.tensor_scalar(out=mk[:qrows, :KE], in0=scores[:qrows, :KE], scalar1=mid, scalar2=0.0,
                                                              op0=ALU.is_ge, op1=ALU.add, accum_out=cnt, saturate=False),
                             reads=[rsc, rbs], writes=[rmk[qi], rbs])
                        S.op("dve", lambda h: h.tensor_scalar(out=ge, in0=cnt, scalar1=TOPK - 0.5, scalar2=None, op0=ALU.is_ge), reads=[rbs],
                             writes=[rbs])
                        S.op("dve", lambda h: h.tensor_tensor(out=d1, in0=mid, in1=lo, op=ALU.subtract), reads=[rbs], writes=[rbs])
                        S.op("dve", lambda h: h.scalar_tensor_tensor(out=lo, in0=d1, scalar=ge, in1=lo, op0=ALU.mult, op1=ALU.add), reads=[rbs],
                             writes=[rbs])
                        S.op("dve", lambda h: h.tensor_tensor(out=d1, in0=hi, in1=mid, op=ALU.subtract), reads=[rbs], writes=[rbs])
                        S.op("dve", lambda h: h.scalar_tensor_tensor(out=hi, in0=d1, scalar=ge, in1=mid, op0=ALU.mult, op1=ALU.add), reads=[rbs],
                             writes=[rbs])
                    S.op("dve", lambda h: h.tensor_scalar(out=mk[:qrows, :KE], in0=scores[:qrows, :KE], scalar1=lo, scalar2=-240.0,
                                                          op0=ALU.is_lt, op1=ALU.mult, saturate=False), reads=[rsc, rbs], writes=[rmk[qi]])

                def sb_head(hd):
                    qt, rq, qd = qr.nxt()
                    qn, rqn, qnd = qr.nxt()
                    S.op("sp", lambda h: h.dma_start(out=qt[:], in_=QTs[0, hd, :, :n]), reads=[r_scr["QT"]], writes=[rq], dsem=qd)
                    S.op("sp", lambda h: h.dma_start(out=qn[:], in_=QTs[1, hd, :, :n]), reads=[r_scr["QT"]], writes=[rqn], dsem=qnd)
                    S.op("pool", lambda h: h.memset(Sacc[:], 0.0), writes=[rS])
                    corder = list(reversed(range(len(chunks))))
                    kvpre = Pre([(lambda ci: lambda: load_kv(0, hd, ci))(ci) for ci in corder], 1)
                    kvc = {}
                    tiles = []
                    for pos, ci in enumerate(corder):
                        i0, i1 = chunks[ci]
                        for kbi in reversed(range(i0, i1)):
                            tiles.append((pos, ci, kbi))
                    T_ = len(tiles)
                    stt = [None] * T_

                    def Z(i):
                        pos, ci, kbi = tiles[i]
                        if ci not in kvc:
                            kvc[ci] = kvpre.get(pos)
                        kt, rk, vt, rv, kc0 = kvc[ci]
                        k0, rows = kblocks[kbi]
                        o = k0 - kc0
                        d = dict(kt=kt, rk=rk, vt=vt, rv=rv, o=o, vb=o // 128, rows=rows, zb=2 + (i % 2), lb=4 + (i % 2))
                        sma = sbmask_of(kbi)
                        d["sma"] = sma
                        if sma is not None:
                            smt, rsm, smd = smr.nxt()
                            S.op("sp", lambda h: h.dma_start(out=smt[:rows, :], in_=sma), writes=[rsm], dsem=smd)
                            d["smt"], d["rsm"] = smt, rsm
                        zb = d["zb"]

                        def zf(h):
                            ins = h.matmul(PB[zb][:rows, :n], kt[:, o:o + rows], qt[:], start=True, stop=(sma is None))
                            if sma is not None:
                                ins = h.matmul(PB[zb][:rows, :n], identb[:rows, :rows], d["smt"][:rows, :], start=False, stop=True)
                            return ins
                        S.op("pe", zf, reads=[rk, rq, r_const] + ([d["rsm"]] if sma is not None else []), writes=[RB[zb]])
                        et, ret, _ = er.nxt()
                        spt, rspt, _ = spr.nxt()
                        S.op("act", lambda h: h.activation(out=et[:rows, :], in_=PB[zb][:rows, :n], func=AF.Exp), reads=[RB[zb]], writes=[ret])
                        S.op("act", lambda h: h.activation(out=spt[:rows, :], in_=et[:rows, :], func=AF.Ln, bias=1.0), reads=[ret], writes=[rspt])
                        d["spt"], d["rspt"] = spt, rspt
                        stt[i] = d

                    def L(i):
                        d = stt[i]
                        kt, o, rows, lb, sma, spt = d["kt"], d["o"], d["rows"], d["lb"], d["sma"], d["spt"]

                        def lf(h):
                            h.matmul(PB[lb][:rows, :n], kt[:, o:o + rows], qn[:], start=True, stop=False)
                            if sma is not None:
                                h.matmul(PB[lb][:rows, :n], nidentb[:rows, :rows], d["smt"][:rows, :], start=False, stop=False)
                            h.matmul(PB[lb][:rows, :n], trib[:rows, :rows], spt[:rows, :], start=False, stop=False)
                            return h.matmul(PB[lb][:rows, :n], onesb[:, :rows], Sacc[:], start=False, stop=True)
                        S.op("pe", lf, reads=[d["rk"], rqn, r_const, d["rspt"], rS] + ([d["rsm"]] if sma is not None else []), writes=[RB[lb]])
                        at, rat, _ = ar.nxt()
                        S.op("act", lambda h: h.activation(out=at[:rows, :], in_=PB[lb][:rows, :n], func=AF.Exp, scale=-1.0), reads=[RB[lb]], writes=[rat])
                        S.op("pool", lambda h: h.tensor_tensor(out=Sacc[:rows, :], in0=Sacc[:rows, :], in1=spt[:rows, :], op=ALU.add),
                             reads=[d["rspt"], rS], writes=[rS])
                        d["at"], d["rat"] = at, rat

                    def O(i):
                        d = stt[i]
                        vt, vb, rows, at = d["vt"], d["vb"], d["rows"], d["at"]
                        S.op("pe", lambda h: h.matmul(PB[6][:, :n], vt[:rows, vb, :], at[:rows, :], start=(i == 0), stop=(i == T_ - 1)),
                             reads=[d["rv"], d["rat"]], writes=[RB[6]])
                        stt[i] = None
                    for i in range(T_ + 2):
                        if i < T_:
                            Z(i)
                        if 0 <= i - 1 < T_:
                            L(i - 1)
                        if 0 <= i - 2 < T_:
                            O(i - 2)
                        yield
                    ost, ros, osd = osr.nxt()
                    S.op("act", lambda h: h.activation(out=ost[:], in_=PB[6][:, :n], func=AF.Copy), reads=[RB[6]], writes=[ros])
                    S.op("sp", lambda h: h.dma_start(out=OTs[0, hd, :, :n], in_=ost[:]), reads=[ros], writes=[r_scr["OT"]], dsem=osd)

                def dsa_head(hd):
                    qt, rq, qd = qr.nxt()
                    S.op("sp", lambda h: h.dma_start(out=qt[:], in_=QTs[2, hd, :, :n]), reads=[r_scr["QT"]], writes=[rq], dsem=qd)
                    S.op("sp", lambda h: h.dma_start(out=stp[:], in_=strip[hd]), reads=[r_scr["strip"]], writes=[rstp], dsem=dstp)
                    corder = list(range(len(chunks)))
                    kvpre = Pre([(lambda ci: lambda: load_kv(1, hd, ci))(ci) for ci in corder], 1)
                    kvc = {}
                    tiles = []
                    for pos, ci in enumerate(corder):
                        i0, i1 = chunks[ci]
                        for kbi in range(i0, i1):
                            tiles.append((pos, ci, kbi))
                    T_ = len(tiles)
                    stt = [None] * T_

                    def LT(i):
                        pos, ci, kbi = tiles[i]
                        if ci not in kvc:
                            kvc[ci] = kvpre.get(pos)
                        kt, rk, vt, rv, kc0 = kvc[ci]
                        k0, rows = kblocks[kbi]
                        o = k0 - kc0
                        lb = i % 4
                        nearb = near_of(kbi)
                        u0 = u0_of(kbi) if nearb else 0

                        def lf(h):
                            ins = h.matmul(PB[lb][:rows, :n], kt[:, o:o + rows], qt[:], start=True, stop=False)
                            for qi, (q0, qrows) in enumerate(qblks):
                                lastm = (qi == nqb - 1) and not nearb
                                ins = h.matmul(PB[lb][:rows, q0:q0 + qrows], masks[qi][:qrows, k0:k0 + rows], identb[:qrows, :qrows],
                                               start=False, stop=lastm)
                            if nearb:
                                ins = h.matmul(PB[lb][:rows, :n], identb[:rows, :rows], stp[:rows, u0:u0 + n], start=False, stop=True)
                            return ins
                        S.op("pe", lf, reads=[rk, rq, r_const, rstp] + rmk, writes=[RB[lb]])
                        pt, rpt, _ = ar.nxt()
                        if nearb:
                            S.op("act", lambda h: h.activation(out=pt[:rows, :], in_=PB[lb][:rows, :n], func=AF.Exp), reads=[RB[lb]], writes=[rpt])
                        else:
                            S.op("act", lambda h: h.activation(out=pt[:rows, :], in_=PB[lb][:rows, :n], func=AF.Exp, bias=CH[:rows, hd:hd + 1]),
                                 reads=[RB[lb], r_const], writes=[rpt])
                        stt[i] = dict(vt=vt, rv=rv, vb=o // 128, rows=rows, pt=pt, rpt=rpt)

                    def O(i):
                        d = stt[i]
                        vt, vb, rows, pt = d["vt"], d["vb"], d["rows"], d["pt"]

                        def of(h):
                            h.matmul(PB[7][:, :n], vt[:rows, vb, :], pt[:rows, :], start=(i == 0), stop=(i == T_ - 1))
                            return h.matmul(PB[5][:, :n], onesb[:rows, :], pt[:rows, :], start=(i == 0), stop=(i == T_ - 1))
                        S.op("pe", of, reads=[d["rv"], d["rpt"], r_const], writes=[RB[7], RB[5]])
                        stt[i] = None
                    for i in range(T_ + 1):
                        if i < T_:
                            LT(i)
                        if 0 <= i - 1 < T_:
                            O(i - 1)
                    S.op("dve", lambda h: h.tensor_scalar(out=rden[:], in0=PB[5][:, :n], scalar1=1e-30, scalar2=None, op0=ALU.max), reads=[RB[5]],
                         writes=[rrden])
                    S.op("dve", lambda h: h.reciprocal(out=rden[:], in_=rden[:]), reads=[rrden], writes=[rrden])
                    ost, ros, osd = osr.nxt()
                    S.op("dve", lambda h: h.tensor_tensor(out=ost[:], in0=PB[7][:, :n], in1=rden[:], op=ALU.mult),
                         reads=[RB[7], rrden], writes=[ros])
                    S.op("sp", lambda h: h.dma_start(out=OTs[1, hd, :, :n], in_=ost[:]), reads=[ros], writes=[r_scr["OT"]], dsem=osd)

                def streamA():
                    for qi in range(nqb):
                        yield from idx(qi)
                        bisect(qi)
                        yield

                def streamB():
                    for hd in range(NH):
                        yield from sb_head(hd)
                nkt_ = (KE + 511) // 512
                totA = nqb * (nkt_ + 1)
                totB = NH * (nkb + 2)
                gA, gB = streamA(), streamB()
                doneA = doneB = 0
                liveA = liveB = True
                while liveA or liveB:
                    pickA = liveA and (not liveB or doneA * totB <= doneB * totA)
                    if pickA:
                        try:
                            next(gA)
                            doneA += 1
                        except StopIteration:
                            liveA = False
                    else:
                        try:
                            next(gB)
                            doneB += 1
                        except StopIteration:
                            liveB = False
                for hd in range(NH):
                    dsa_head(hd)
                S.flush()
                for rg in (kvr_k, kvr_v, kir, imr, smr, qr, osr):
                    rg.release()
                S.putd(dl)

        def win_ffn(xsrc, n, r, halo, ydst, nout, prev_src, conv_dst, conv_cols, slot_flag):
            blks = blocks_of(n)
            nb = len(blks)
            with ExitStack() as st0:
                x1 = [PT("f_x1_%d" % i, [128, D], F32, st0) for i in range(nb)]
                rx1 = [Res() for _ in range(nb)]
                h2T = PT("f_h2T", [128, 16, n], BF16, st0)
                rh2 = Res()
                with ExitStack() as st:
                    dl = [S.getd() for _ in range(4)]
                    hT = PT("fa_hT", [128, 16, n], BF16, st)
                    oT = PT("fa_oT", [128, 16, n], BF16, st)
                    mT = PT("fa_mT", [128, 16, n], BF16, st)
                    rhT, roT, rmT = Res(), Res(), Res()
                    S.op("sp", lambda h: h.dma_start(out=hT[:], in_=hTs[:, :, :n].rearrange("a p c -> p a c")), reads=[r_scr["hT"]], writes=[rhT], dsem=dl[0])
                    for t in range(2):
                        S.op("sp", (lambda t: lambda h: h.dma_start(out=oT[:, t * 8:(t + 1) * 8, :], in_=OTs[t, :, :, :n].rearrange("a p c -> p a c")))(t),
                             reads=[r_scr["OT"]], writes=[roT], dsem=dl[1])
                    for i, (b0, rows) in enumerate(blks):
                        S.op("sp", (lambda i, b0, rows: lambda h: h.dma_start(out=x1[i][:rows], in_=xsrc[b0:b0 + rows, :]))(i, b0, rows), writes=[rx1[i]],
                             dsem=dl[2])
                    gt1, rgt1 = load_row_rep(st, dl[3], "fa_gt1", modrow[r:r + 1, 2 * D:3 * D])
                    wg = Ring(S, nc, st, "fa_wg", [128, 16, 256], BF16, 2)
                    wb = Ring(S, nc, st, "fa_wb", [128, 16, 128], BF16, 2)
                    gs = Ring(S, nc, st, "fa_g", [128, n], F32, 4, with_dsem=False)
                    wgv = wbf["gate"].rearrange("(kc p) c -> p kc c", p=128)
                    wbv = [wbf["brsb"].rearrange("(kc p) c -> p kc c", p=128), wbf["brsa"].rearrange("(kc p) c -> p kc c", p=128)]

                    def gthunk(fb):
                        def f():
                            wt, wres, wd = wg.nxt()
                            S.op("sp", lambda h: h.dma_start(out=wt[:, :, 0:128], in_=wgv[:, :, fb * 128:(fb + 1) * 128]),
                                 reads=[r_scr["wbf"]], writes=[wres], dsem=wd)
                            S.op("sp", lambda h: h.dma_start(out=wt[:, :, 128:256], in_=wgv[:, :, D + fb * 128:D + (fb + 1) * 128]),
                                 reads=[r_scr["wbf"]], writes=[wres], dsem=wd)
                            wt2, wres2, wd2 = wb.nxt()
                            for t in range(2):
                                S.op("sp", (lambda t: lambda h: h.dma_start(out=wt2[:, t * 8:(t + 1) * 8, :], in_=wbv[t][:, :, fb * 128:(fb + 1) * 128]))(t),
                                     reads=[r_scr["wbf"]], writes=[wres2], dsem=wd2)
                            return wt, wres, wt2, wres2
                        return f
                    gpre = Pre([gthunk(fb) for fb in range(16)], 1)

                    def gate_fb(fb):
                        wt, wres, wt2, wres2 = gpre.get(fb)
                        gts = []
                        for t in range(2):
                            bk = t

                            def gf(h, bk=bk, t=t):
                                for kc in range(16):
                                    ins = h.matmul(PB[bk][:, :n], wt[:, kc, t * 128:(t + 1) * 128], hT[:, kc, :], start=(kc == 0), stop=(kc == 15))
                                return ins
                            S.op("pe", gf, reads=[rhT, wres], writes=[RB[bk]])
                            g_, rg_, _ = gs.nxt()
                            S.op("act", (lambda g_, bk: lambda h: h.activation(out=g_[:], in_=PB[bk][:, :n], func=AF.Sigmoid))(g_, bk), reads=[RB[bk]],
                                 writes=[rg_])
                            gts.append((g_, rg_))
                        for t in range(2):
                            bk = 2 + t

                            def bf_(h, bk=bk, t=t):
                                for kc in range(8):
                                    ins = h.matmul(PB[bk][:, :n], wt2[:, t * 8 + kc, :], oT[:, t * 8 + kc, :], start=(kc == 0), stop=(kc == 7))
                                return ins
                            S.op("pe", bf_, reads=[roT, wres2], writes=[RB[bk]])
                        g0, rg0 = gts[0]
                        g1, rg1 = gts[1]
                        S.op("dve", lambda h: h.tensor_tensor(out=g0[:], in0=PB[2][:, :n], in1=g0[:], op=ALU.mult), reads=[RB[2], rg0], writes=[rg0])
                        S.op("dve", lambda h: h.tensor_tensor(out=g1[:], in0=PB[3][:, :n], in1=g1[:], op=ALU.mult), reads=[RB[3], rg1], writes=[rg1])
                        S.op("pool", lambda h: h.tensor_tensor(out=mT[:, fb, :], in0=g0[:], in1=g1[:], op=ALU.add), reads=[rg0, rg1], writes=[rmT])
                    for fb in range(16):
                        gate_fb(fb)
                    wo = Ring(S, nc, st, "fa_wo", [128, 16, 512], BF16, 2)
                    tmpr = Ring(S, nc, st, "fa_tmp", [128, 512], F32, 2, with_dsem=False)
                    wov = wbf["out"].rearrange("(kc p) c -> p kc c", p=128)

                    def othunk(nbk):
                        def f():
                            wt, wres, wd = wo.nxt()
                            S.op("sp", lambda h: h.dma_start(out=wt[:], in_=wov[:, :, nbk * 512:(nbk + 1) * 512]),
                                 reads=[r_scr["wbf"]], writes=[wres], dsem=wd)
                            return wt, wres
                        return f
                    opre = Pre([othunk(k_) for k_ in range(4)], 1)

                    def out_blk(nbk, i, b0, rows, bk, wt, wres):
                        def of(h):
                            for kc in range(16):
                                ins = h.matmul(PB[bk][:rows, :], mT[:, kc, b0:b0 + rows], wt[:, kc, :], start=(kc == 0), stop=(kc == 15))
                            return ins
                        S.op("pe", of, reads=[rmT, wres], writes=[RB[bk]])
                        tt, rtt, _ = tmpr.nxt()
                        S.op("dve", lambda h: h.tensor_tensor(out=tt[:rows, :], in0=PB[bk][:rows, :], in1=gt1[:rows, nbk * 512:(nbk + 1) * 512], op=ALU.mult),
                             reads=[RB[bk], rgt1], writes=[rtt])
                        S.op("pool", lambda h: h.tensor_tensor(out=x1[i][:rows, nbk * 512:(nbk + 1) * 512], in0=x1[i][:rows, nbk * 512:(nbk + 1) * 512],
                                                               in1=tt[:rows, :], op=ALU.add), reads=[rtt, rx1[i]], writes=[rx1[i]])
                    k = 0
                    for nbk in range(4):
                        wt, wres = opre.get(nbk)
                        for i, (b0, rows) in enumerate(blks):
                            out_blk(nbk, i, b0, rows, 4 + (k % 2), wt, wres)
                            k += 1
                    S.flush()
                    for rg in (wg, wb, wo):
                        rg.release()
                    S.putd(dl)
                with ExitStack() as st:
                    dl = [S.getd() for _ in range(3)]
                    A2, B2, rA2, rB2 = load_mod(st, dl, r, 2)
                    nt = NormT(st, "nf")
                    for i, (b0, rows) in enumerate(blks):
                        nt.run(x1[i], rx1[i], rows, A2, B2, rA2, rB2, h2T, rh2, b0)
                    if slot_flag is not None and halo:
                        S.op("dve", lambda h: h.tensor_scalar(out=h2T[:, :, 0:halo], in0=h2T[:, :, 0:halo], scalar1=hflag[:, slot_flag:slot_flag + 1],
                                                              scalar2=None, op0=ALU.mult), reads=[rh2, r_const], writes=[rh2])
                    S.flush()
                    S.putd(dl)
                with ExitStack() as st:
                    dl = [S.getd() for _ in range(6)]
                    aT = PT("fb_aT", [128, NFF, n], BF16, st)
                    raT = Res()
                    if halo:
                        S.op("pool", lambda h: h.memset(aT[:, :, 0:halo], 0.0), writes=[raT])
                    gt2, rgt2 = load_row_rep(st, dl[0], "fb_gt2", modrow[r:r + 1, 5 * D:6 * D])
                    gfin, rgfin = load_row_rep(st, dl[1], "fb_gf", g_final[0:1, :])
                    ne = nout + 2
                    E = Ring(S, nc, st, "fb_E", [128, ne], F32, 2, with_dsem=False)
                    tr_ = Ring(S, nc, st, "fb_t", [128, nout], F32, 2, with_dsem=False)
                    wu = Ring(S, nc, st, "fb_wu", [128, 16, 256], BF16, 3)
                    wuv = wbf["up"].rearrange("(kc p) c -> p kc c", p=128)
                    prevt = None
                    rprev = Res()
                    if prev_src is not None:
                        prevt = PT("fb_prev", [128, NFF, 2], F32, st)
                        S.op("sp", lambda h: h.dma_start(out=prevt[:], in_=prev_src), writes=[rprev], dsem=dl[2])

                    def uthunk(fb):
                        def f():
                            wt, wres, wd = wu.nxt()
                            S.op("sp", lambda h: h.dma_start(out=wt[:, :, 0:128], in_=wuv[:, :, fb * 128:(fb + 1) * 128]),
                                 reads=[r_scr["wbf"]], writes=[wres], dsem=wd)
                            S.op("sp", lambda h: h.dma_start(out=wt[:, :, 128:256], in_=wuv[:, :, DFF + fb * 128:DFF + (fb + 1) * 128]),
                                 reads=[r_scr["wbf"]], writes=[wres], dsem=wd)
                            return wt, wres
                        return f
                    upre = Pre([uthunk(fb) for fb in range(NFF)], 2)

                    def up_fb(fb):
                        wt, wres = upre.get(fb)
                        for t in range(2):
                            bk = (fb % 2) * 2 + t

                            def uf(h, bk=bk, t=t):
                                for kc in range(16):
                                    ins = h.matmul(PB[bk][:, :n], wt[:, kc, t * 128:(t + 1) * 128], h2T[:, kc, :], start=(kc == 0), stop=(kc == 15))
                                return ins
                            S.op("pe", uf, reads=[rh2, wres], writes=[RB[bk]])
                        bg = (fb % 2) * 2
                        bv = bg + 1
                        Et, rE, _ = E.nxt()
                        tt, rtt, _ = tr_.nxt()
                        if prevt is None:
                            S.op("act", lambda h: h.activation(out=Et[:, :n], in_=PB[bg][:, :n], func=AF.Copy), reads=[RB[bg]], writes=[rE])
                        else:
                            S.op("act", lambda h: h.activation(out=Et[:, 2:2 + n], in_=PB[bg][:, :n], func=AF.Copy), reads=[RB[bg]], writes=[rE])
                            S.op("pool", lambda h: h.tensor_copy(out=Et[:, 0:2], in_=prevt[:, fb, :]), reads=[rprev, rE], writes=[rE])
                        S.op("dve", lambda h: h.tensor_scalar(out=tt[:], in0=Et[:, 2:2 + nout], scalar1=cwb[:, fb, 2:3], scalar2=cwb[:, fb, 3:4],
                                                              op0=ALU.mult, op1=ALU.add), reads=[rE, r_const], writes=[rtt])
                        S.op("dve", lambda h: h.scalar_tensor_tensor(out=tt[:], in0=Et[:, 1:1 + nout], scalar=cwb[:, fb, 1:2], in1=tt[:],
                                                                     op0=ALU.mult, op1=ALU.add), reads=[rE, rtt, r_const], writes=[rtt])
                        S.op("dve", lambda h: h.scalar_tensor_tensor(out=tt[:], in0=Et[:, 0:nout], scalar=cwb[:, fb, 0:1], in1=tt[:],
                                                                     op0=ALU.mult, op1=ALU.add), reads=[rE, rtt, r_const], writes=[rtt])
                        S.op("act", lambda h: h.activation(out=tt[:], in_=tt[:], func=AF.Silu), reads=[rtt], writes=[rtt])
                        S.op("dve", lambda h: h.tensor_tensor(out=aT[:, fb, halo:halo + nout], in0=PB[bv][:, halo:halo + nout], in1=tt[:], op=ALU.mult),
                             reads=[rtt, RB[bv]], writes=[raT])
                        if conv_dst is not None:
                            S.op("pool", lambda h: h.tensor_copy(out=convc[:, fb, :], in_=Et[:, conv_cols:conv_cols + 2]), reads=[rE], writes=[r_convc])
                    for fb in range(NFF):
                        up_fb(fb)
                    if conv_dst is not None:
                        S.op("sp", lambda h: h.dma_start(out=conv_dst, in_=convc[:]), reads=[r_convc], writes=[r_scr["out"]], dsem=dl[3])
                    wdr = Ring(S, nc, st, "fb_wd", [128, 11, 512], BF16, 3)
                    wdv = wbf["down"].rearrange("(kc p) c -> p kc c", p=128)
                    tmpr = Ring(S, nc, st, "fb_tmp", [128, 512], F32, 2, with_dsem=False)
                    assert nb <= 4

                    def dthunk(nbk, pc):
                        def f():
                            wt, wres, wd = wdr.nxt()
                            S.op("sp", lambda h: h.dma_start(out=wt[:], in_=wdv[:, pc * 11:(pc + 1) * 11, nbk * 512:(nbk + 1) * 512]),
                                 reads=[r_scr["wbf"]], writes=[wres], dsem=wd)
                            return wt, wres
                        return f
                    dpre = Pre([dthunk(nbk, pc) for nbk in range(4) for pc in range(4)], 2)

                    def down_piece(nbk, pc, wt, wres):
                        for i, (b0, rows) in enumerate(blks):
                            bk = 4 + i

                            def df(h, bk=bk, b0=b0, rows=rows):
                                for kc in range(11):
                                    ins = h.matmul(PB[bk][:rows, :], aT[:, pc * 11 + kc, b0:b0 + rows], wt[:, kc, :], start=(pc == 0 and kc == 0),
                                                   stop=(pc == 3 and kc == 10))
                                return ins
                            S.op("pe", df, reads=[raT, wres], writes=[RB[bk]])

                    def down_evac(nbk, i, b0, rows):
                        bk = 4 + i
                        tt, rtt, _ = tmpr.nxt()
                        S.op("dve", lambda h: h.tensor_tensor(out=tt[:rows, :], in0=PB[bk][:rows, :], in1=gt2[:rows, nbk * 512:(nbk + 1) * 512], op=ALU.mult),
                             reads=[RB[bk], rgt2], writes=[rtt])
                        S.op("pool", lambda h: h.tensor_tensor(out=x1[i][:rows, nbk * 512:(nbk + 1) * 512], in0=x1[i][:rows, nbk * 512:(nbk + 1) * 512],
                                                               in1=tt[:rows, :], op=ALU.add), reads=[rtt, rx1[i]], writes=[rx1[i]])
                    for nbk in range(4):
                        for pc in range(4):
                            wt, wres = dpre.get(nbk * 4 + pc)
                            down_piece(nbk, pc, wt, wres)
                        for i, (b0, rows) in enumerate(blks):
                            down_evac(nbk, i, b0, rows)
                    junk = PT("fb_junk", [128, D], BF16, st)
                    sm = PT("fb_sm", [128, 8], F32, st)
                    rj, rsm = Res(), Res()

                    def fin(i, b0, rows):
                        xt = x1[i]
                        S.op("dve", lambda h: h.scalar_tensor_tensor(out=junk[:rows], in0=xt[:rows], scalar=1.0, in1=xt[:rows], op0=ALU.mult,
                                                                     op1=ALU.mult, accum_out=sm[:rows, 0:1]), reads=[rx1[i]], writes=[rj, rsm])
                        S.op("dve", lambda h: h.tensor_scalar(out=sm[:rows, 1:2], in0=sm[:rows, 0:1], scalar1=1.0 / D, scalar2=EPS, op0=ALU.mult,
                                                              op1=ALU.add), reads=[rsm], writes=[rsm])
                        S.op("act", lambda h: h.activation(out=sm[:rows, 2:3], in_=sm[:rows, 1:2], func=AF.Ln), reads=[rsm], writes=[rsm])
                        S.op("act", lambda h: h.activation(out=sm[:rows, 3:4], in_=sm[:rows, 2:3], func=AF.Exp, scale=-0.5), reads=[rsm], writes=[rsm])
                        S.op("dve", lambda h: h.scalar_tensor_tensor(out=xt[:rows], in0=xt[:rows], scalar=sm[:rows, 3:4], in1=gfin[:rows],
                                                                     op0=ALU.mult, op1=ALU.mult), reads=[rx1[i], rsm, rgfin], writes=[rx1[i]])
                        lo_ = max(b0, halo)
                        hi_ = min(b0 + rows, halo + nout)
                        if hi_ > lo_:
                            S.op("sp", lambda h: h.dma_start(out=ydst[lo_ - halo:hi_ - halo, :], in_=xt[lo_ - b0:hi_ - b0, :]),
                                 reads=[rx1[i]], writes=[r_scr["out"]], dsem=dl[4])
                    for i, (b0, rows) in enumerate(blks):
                        fin(i, b0, rows)
                    S.flush()
                    for rg in (wu, wdr):
                        rg.release()
                    S.putd(dl)

        import os as _os
        stop = int(_os.environ.get("MK_STOP", "1000"))
        stepc = [0]

        def go():
            stepc[0] += 1
            return stepc[0] <= stop
        if go():
            setup()
        for s in range(NS):
            if go():
                cache_import(s)
            if go():
                phaseA(ctx_s[s], xs[s], DSEQ, PAST, okv_s[s], 1 + s, DSEQ)
        if go():
            phaseA(ctx_p, xp, SEQ, 0, okv_p, 0, cfg.TT)

        kb_s = [(i * 128, 128) for i in range(PAST // 128)] + [(PAST, DSEQ)]
        for s in range(NS):
            if go():
                win_q(xs[s], DSEQ, 1 + s)
            if go():
                win_attn(ctx_s[s], DSEQ, kb_s, None, strip_s, cfg.ULs,
                         u0_of=lambda kbi: cfg.U0s - (128 * kbi - PAST),
                         near_of=lambda kbi: (128 * kbi - PAST) >= -NEAR - 127,
                         sbmask_of=lambda kbi: (sbmask_s_in[0:DSEQ, :] if kbi == PAST // 128 else None),
                         idxmask_of=lambda qi, kt: None)
            if go():
                win_ffn(xs[s], DSEQ, 1 + s, 0, y_s[s], DSEQ, cst[s], sconv[s], DSEQ, None)
        for m in range(NSLOT):
            kb_p = [(i * 128, min(128, cfg.kext[m] - i * 128)) for i in range(cfg.kextb[m])]
            if go():
                win_q(xw[m], ncols, 0)
            if go():
                win_attn(ctx_p, ncols, kb_p, m, strip_p, cfg.UL,
                         u0_of=(lambda m: lambda kbi: cfg.U0 - (128 * kbi - STRIDE * G * m + 2))(m),
                         near_of=(lambda m: lambda kbi: cfg.near(m, kbi))(m),
                         sbmask_of=(lambda m: lambda kbi: (sbmask_in[cfg.sbm_index[(m, kbi)], 0:kb_p_rows(cfg, m, kbi), :] if (m, kbi) in cfg.sbm_index else None))(m),
                         idxmask_of=(lambda m: lambda qi, kt: (idxmask_in[cfg.im_index[(m, qi, kt)]] if (m, qi, kt) in cfg.im_index else None))(m))
            last = (m == NSLOT - 1)
            ccol = (SEQ - 2) - (STRIDE * (G * m + G - 1) - 2)
            if go():
                win_ffn(xw[m], ncols, 0, 2, y_p[m], STRIDE, None, pconv if last else None, ccol, m)
        S.barrier()
        S.flush()
    return nc


def kb_p_rows(cfg, m, kbi):
    return min(128, cfg.kext[m] - kbi * 128)


def rel_bucket_np(rel):
    rel = np.asarray(rel, np.int64)
    nb = 16
    max_exact = 8
    ret = np.where(rel > 0, nb, 0)
    n = np.abs(rel)
    nf = np.maximum(n, 1).astype(np.float32)
    large = max_exact + (np.log(nf / np.float32(max_exact)) / np.float32(np.log(1024 / max_exact)) * np.float32(nb - max_exact)).astype(np.int32)
    large = np.minimum(large, nb - 1)
    return ret + np.where(n < max_exact, n, large)


def prep_cfg_tables(cfg):
    cfg.sbm_index = {}
    cfg.im_index = {}
    for m in range(cfg.NSLOT):
        for kb in range(cfg.kextb[m]):
            if cfg.sbmasked(m, kb):
                cfg.sbm_index[(m, kb)] = len(cfg.sbm_index)
        nqb = len(blocks_of(cfg.ncols))
        nkt = (cfg.kext[m] + 511) // 512
        for qi in range(nqb):
            for kt in range(nkt):
                if cfg.idxmasked(m, kt):
                    cfg.im_index[(m, qi, kt)] = len(cfg.im_index)
    cfg.n_sbm = max(1, len(cfg.sbm_index))
    cfg.n_im = max(1, len(cfg.im_index))


def core_tables(cfg, j):
    STRIDE, G, ncols = cfg.STRIDE, cfg.G, cfg.ncols
    bf = ml_dtypes.bfloat16
    sbm = np.zeros((cfg.n_sbm, 128, ncols), np.float32)
    for (m, kb), ix in cfg.sbm_index.items():
        qpos = STRIDE * (G * m + j) - 2 + np.arange(ncols)
        kpos = 128 * kb + np.arange(128)
        sbm[ix] = np.where(kpos[:, None] >= qpos[None, :], MASKV, 0.0)
    im = np.zeros((cfg.n_im, 128, 512), np.float32)
    qb = blocks_of(ncols)
    for (m, qi, kt), ix in cfg.im_index.items():
        q0, qrows = qb[qi]
        qpos = STRIDE * (G * m + j) - 2 + q0 + np.arange(128)
        lim = (qpos // 64 + 1) * 64
        kpos = 512 * kt + np.arange(512)
        im[ix] = np.where(kpos[None, :] >= lim[:, None], IMASKV, 0.0)
    i = np.arange(cfg.GL)
    r = (cfg.U0 + 127) - i - STRIDE * j
    b = rel_bucket_np(r)
    ohp = np.zeros((32, cfg.GL), np.float32)
    ohp[b, i] = 1.0
    i = np.arange(cfg.GLs)
    r = (cfg.U0s + 127) - i
    b = rel_bucket_np(r)
    ohs = np.zeros((32, cfg.GLs), np.float32)
    ohs[b, i] = 1.0
    sbs = np.zeros((128, DSEQ), np.float32)
    sbs[:DSEQ] = np.where(np.arange(DSEQ)[:, None] >= np.arange(DSEQ)[None, :], MASKV, 0.0)
    hflag = np.ones((128, cfg.NSLOT), np.float32)
    if j == 0:
        hflag[:, 0] = 0.0
    return {"sbmask": sbm.astype(bf), "idxmask": im.astype(bf), "oh_p": ohp, "oh_s": ohs, "sbmask_s": sbs.astype(bf), "hflag": hflag}


_NC_CACHE = {}


def run_cfg(cfg, inp):
    prep_cfg_tables(cfg)
    import os as _os
    key = (cfg.SEQ, cfg.NB, cfg.G, cfg.NSLOT, cfg.STRIDE, cfg.NS, cfg.TT, _os.environ.get('MK_STOP'), _os.environ.get('MK_QSKIP'))
    if key not in _NC_CACHE:
        _NC_CACHE[key] = build(cfg)
    nc = _NC_CACHE[key]
    SEQ, G, NSLOT, STRIDE, NS, ncols = cfg.SEQ, cfg.G, cfg.NSLOT, cfg.STRIDE, cfg.NS, cfg.ncols
    f32 = np.float32
    ident = np.eye(128, dtype=f32)
    tri = (np.arange(128)[:, None] >= np.arange(128)[None, :]).astype(f32)
    constf = np.stack([ident, -ident, tri, np.ones((128, 128), f32)], axis=1)
    cwb = np.concatenate([inp["conv_w"][0], inp["conv_b"][0][None]], axis=0)
    cwb = np.ascontiguousarray(cwb.reshape(4, NFF, 128).transpose(2, 1, 0))
    shared = {
        "w_ada": inp["w_ada"][0], "w_in": inp["w_in"][0], "w_gate": inp["w_gate"][0], "w_br_sb": inp["w_br_sb"][0],
        "w_br_sa": inp["w_br_sa"][0], "w_out": inp["w_out"][0], "w_up": inp["w_up"][0], "w_down": inp["w_down"][0],
        "g_mix": inp["g_mix"][0][None], "g_ffn": inp["g_ffn"][0][None], "g_final": inp["g_final"][None],
        "rel_table": inp["rel_table"], "cwb": cwb, "constf": constf,
        "wix": np.ascontiguousarray(inp["w_in"][0][:, C_WIX:C_WIX + 16].reshape(16, 128, 16).transpose(1, 0, 2)),
    }
    tabs = [core_tables(cfg, j) for j in range(G)]
    in_maps = []
    tot = STRIDE * G * NSLOT
    for c in range(cfg.ncores):
        b, j = c // G, c % G
        xpad = np.zeros((2 + max(tot, SEQ) + 2, D), f32)
        xpad[2:2 + SEQ] = inp["x_prompt"][b]
        xw = np.stack([xpad[STRIDE * (G * m + j):STRIDE * (G * m + j) + ncols] for m in range(NSLOT)])
        ss = slice(NS * c, NS * (c + 1))
        cvec = np.concatenate([inp["c_prompt"][b:b + 1], inp["c_sample"][ss]], axis=0)
        cT = np.ascontiguousarray(cvec.reshape(cfg.NR, 16, 128).transpose(2, 1, 0))
        st = inp["state_ffn_conv"][0, ss]
        cst = np.ascontiguousarray(st.reshape(NS, 2, NFF, 128).transpose(0, 3, 2, 1))
        m_ = dict(shared)
        m_.update(tabs[j])
        m_.update({
            "xp": inp["x_prompt"][b], "xw": xw, "xs": inp["x_sample"][ss],
            "c_sb_k": inp["cache_sb_k"][0, ss].reshape(NS, PAST, 1024), "c_sb_v": inp["cache_sb_v"][0, ss].reshape(NS, PAST, 1024),
            "c_sa_k": inp["cache_sa_k"][0, ss].reshape(NS, PAST, 1024), "c_sa_v": inp["cache_sa_v"][0, ss].reshape(NS, PAST, 1024),
            "c_ix": inp["cache_idx_k"][0, ss], "c_st": cst, "cT": cT,
            "b_ada_rows": np.repeat(inp["b_ada"][0][None], cfg.NR, axis=0),
        })
        in_maps.append({k: np.ascontiguousarray(v) for k, v in m_.items()})
    res = run_bass_kernel_spmd(nc, in_maps, core_ids=list(range(cfg.ncores))).results
    NB = cfg.NB
    y_prompt = np.zeros((NB, SEQ, D), f32)
    okv = np.zeros((NB, SEQ, KVC), f32)
    p_conv = np.zeros((1, NB, 2, DFF), f32)
    for c in range(cfg.ncores):
        b, j = c // G, c % G
        for m in range(NSLOT):
            p0 = STRIDE * (G * m + j)
            nv = min(STRIDE, SEQ - p0)
            if nv > 0:
                y_prompt[b, p0:p0 + nv] = res[c]["y_p"][m, :nv]
        q0, q1 = SEQ * j // G, SEQ * (j + 1) // G
        okv[b, q0:q1] = res[c]["okv_p"][q0:q1]
        if j == G - 1:
            p_conv[0, b] = res[c]["pconv"].transpose(2, 1, 0).reshape(2, DFF)
    nsb = cfg.ncores * NS
    y_sample = np.concatenate([res[c]["y_s"] for c in range(cfg.ncores)], axis=0)
    okvs = np.concatenate([res[c]["okv_s"] for c in range(cfg.ncores)], axis=0)
    s_conv = np.concatenate([res[c]["sconv"].transpose(0, 3, 2, 1).reshape(NS, 2, DFF) for c in range(cfg.ncores)], axis=0)[None]

    def split(o, L):
        nb_ = o.shape[0]
        return (o[..., 0:1024].reshape(1, nb_, L, NH, 128), o[..., 1024:2048].reshape(1, nb_, L, NH, 128),
                o[..., 2048:3072].reshape(1, nb_, L, NH, 128), o[..., 3072:4096].reshape(1, nb_, L, NH, 128),
                o[..., 4096:4160].reshape(1, nb_, L, 64))
    pk = split(okv, SEQ)
    sk = split(okvs, DSEQ)
    return (y_prompt, y_sample, pk[0], pk[1], pk[2], pk[3], pk[4], p_conv,
            sk[0], sk[1], sk[2], sk[3], sk[4], s_conv)


def kernel(**inputs):
    inp = {k: np.asarray(v) for k, v in inputs.items()}
    cfg = Cfg()
    out = run_cfg(cfg, inp)
    return tuple(np.ascontiguousarray(o, dtype=np.float32) for o in out)
```

```python
import numpy as np
import ml_dtypes
from contextlib import ExitStack
import concourse.bass as bass
import concourse.mybir as mybir
from concourse.bass_utils import run_bass_kernel_spmd

F32 = mybir.dt.float32
BF16 = mybir.dt.bfloat16
FP8 = mybir.dt.float8e4
AF = mybir.ActivationFunctionType
ALU = mybir.AluOpType
AX = mybir.AxisListType

D = 2048
DFF = 5632
NFF = DFF // 128
NH = 8
NHI = 16
INC = 7248
KVC = 4160
C_QSB, C_KSB, C_VSB, C_QSA, C_KSA, C_VSA, C_QIX, C_KIX, C_WIX = 0, 1024, 2048, 3072, 4096, 5120, 6144, 7168, 7232
EPS = 1e-6
PAST = 2048
DSEQ = 64
LKS = PAST + DSEQ
TOPK = 256
NBIS = 18
BRANGE = 32.0
NEAR = 640
SCALE = 128 ** -0.5
MASKV = -30000.0
IMASKV = -1024.0


class Cfg:
    def __init__(self, SEQ=16384, NB=2, G=4, NSLOT=9, STRIDE=456, NS=4, TT=1024):
        self.SEQ, self.NB, self.G, self.NSLOT, self.STRIDE, self.NS, self.TT = SEQ, NB, G, NSLOT, STRIDE, NS, TT
        self.ncols = STRIDE + 2
        self.ncores = NB * G
        self.NR = 1 + NS
        self.kext = [min(SEQ, STRIDE * (G * m + G)) for m in range(NSLOT)]
        self.kextb = [(k + 127) // 128 for k in self.kext]
        dmax = max(128 * (self.kextb[m] - 1) - STRIDE * G * m + 2 for m in range(NSLOT))
        self.U0 = dmax
        self.dmin = -NEAR - 127 - 128
        self.UL = self.U0 - self.dmin + self.ncols
        self.GL = self.UL + 128
        self.U0s = 128 * 16 - PAST
        self.dmins = -NEAR - 127 - 128
        self.ULs = self.U0s - self.dmins + DSEQ
        self.GLs = self.ULs + 128

    def near(self, m, kb):
        return 128 * kb - self.STRIDE * self.G * m + 2 >= -NEAR - 127

    def sbmasked(self, m, kb):
        return 128 * kb + 127 >= self.STRIDE * self.G * m - 2

    def idxmasked(self, m, kt):
        lim = ((self.STRIDE * self.G * m - 2) // 64 + 1) * 64
        return 512 * kt + 511 >= lim


def blocks_of(n):
    out = []
    o = 0
    while o < n:
        out.append((o, min(128, n - o)))
        o += 128
    return out


class Res:
    __slots__ = ("w", "r", "excl")

    def __init__(self, excl=False):
        self.w = None
        self.r = []
        self.excl = excl


class DSem:
    def __init__(self, sem):
        self.sem = sem
        self.count = 0


class Sched:
    ENGS = ("pe", "act", "dve", "pool", "sp")

    def __init__(self, nc, stack, ndsem):
        self.nc = nc
        self.ops = {e: [] for e in self.ENGS}
        self.sem = {}
        self.count = {e: 0 for e in self.ENGS}
        self.known = {e: {} for e in self.ENGS}
        for e in ("pe", "act", "dve", "pool"):
            self.sem[e] = stack.enter_context(nc.semaphore("sem_" + e))
        self.dsems = [DSem(stack.enter_context(nc.semaphore("dsem%d" % i))) for i in range(ndsem)]
        self.free = list(self.dsems[:-12])
        self.free_sw = list(self.dsems[-12:])
        self.dmap = {id(d.sem): d for d in self.dsems}

    def getd(self, sw=False):
        return self.free_sw.pop() if sw else self.free.pop()

    def putd(self, ds, sw=False):
        (self.free_sw if sw else self.free).extend(ds)

    def _deps(self, eng, reads, writes):
        deps = {}

        def add(tok):
            if tok is None:
                return
            key = id(tok[0])
            d = self.dmap.get(key)
            if d is not None:
                tok = (tok[0], d.count)
            if key not in deps or deps[key][1] < tok[1]:
                deps[key] = tok
        for r in reads:
            add(r.w)
            if r.excl:
                for t in r.r:
                    add(t)
        for w in writes:
            add(w.w)
            for t in w.r:
                add(t)
        out = []
        kn = self.known[eng]
        for key, tok in deps.items():
            if kn.get(key, 0) >= tok[1]:
                continue
            kn[key] = tok[1]
            out.append(tok)
        return out

    def op(self, eng, fn, reads=(), writes=(), dsem=None):
        waits = self._deps(eng, reads, writes)
        if dsem is not None and dsem.count > 0:
            key = id(dsem.sem)
            if self.known[eng].get(key, 0) < dsem.count:
                self.known[eng][key] = dsem.count
                waits = [w for w in waits if id(w[0]) != key] + [(dsem.sem, dsem.count)]
        if dsem is None:
            self.count[eng] += 1
            tok = (self.sem[eng], self.count[eng])
            inc = (self.sem[eng], 1)
        else:
            dsem.count += 16
            tok = (dsem.sem, dsem.count)
            inc = (dsem.sem, 16)
        self.ops[eng].append((waits, fn, inc))
        for r in reads:
            if len(r.r) > 24:
                r.r = r.r[-24:]
            r.r.append(tok)
        for w in writes:
            w.w = tok
            w.r = []
        return tok

    def barrier(self):
        toks = [(self.sem[e], self.count[e]) for e in ("pe", "act", "dve", "pool") if self.count[e] > 0]
        toks += [(d.sem, d.count) for d in self.dsems if d.count > 0]
        for e in self.ENGS:
            kn = self.known[e]
            waits = []
            for tok in toks:
                key = id(tok[0])
                if kn.get(key, 0) >= tok[1]:
                    continue
                kn[key] = tok[1]
                waits.append(tok)
            if waits:
                self.ops[e].append((waits, None, None))

    def flush(self):
        self.barrier()
        ops = self.ops
        self.ops = {e: [] for e in self.ENGS}

        def mk(e):
            def body(h):
                for waits, fn, inc in ops[e]:
                    for (s, v) in waits:
                        h.wait_ge(s, v)
                    if fn is not None:
                        inst = fn(h)
                        inst.then_inc(inc[0], inc[1])
            return body
        with self.nc.Block() as block:
            block.tensor(mk("pe"))
            block.scalar(mk("act"))
            block.vector(mk("dve"))
            block.gpsimd(mk("pool"))
            block.sync(mk("sp"))


class Ring:
    uid = 0

    def __init__(self, S, nc, st, name, shape, dt, n, with_dsem=True, sw_dsem=False):
        self.S = S
        self.d2 = [S.getd(sw=True) for _ in range(n)] if sw_dsem else None
        Ring.uid += 1
        self.t = [st.enter_context(nc.sbuf_tensor("%s%d_r%d" % (name, i, Ring.uid), shape, dt)) for i in range(n)]
        self.r = [Res() for _ in range(n)]
        self.d = [S.getd() for _ in range(n)] if with_dsem else None
        self.n = n
        self.i = 0

    def nxt(self):
        i = self.i % self.n
        self.i += 1
        return self.t[i], self.r[i], (self.d[i] if self.d else None)

    def release(self):
        if self.d:
            self.S.putd(self.d)
        if self.d2:
            self.S.putd(self.d2, sw=True)


class Pre:
    def __init__(self, thunks, depth):
        self.th = thunks
        self.depth = depth
        self.nxt_ = 0
        self.got = {}

    def get(self, i):
        while self.nxt_ < len(self.th) and self.nxt_ <= i + self.depth:
            self.got[self.nxt_] = self.th[self.nxt_]()
            self.nxt_ += 1
        return self.got.pop(i)


def build(cfg):
    nc = bass.Bass("TRN2", target_bir_lowering=False)
    SEQ, G, NSLOT, STRIDE, NS, NR, ncols = cfg.SEQ, cfg.G, cfg.NSLOT, cfg.STRIDE, cfg.NS, cfg.NR, cfg.ncols

    def din(name, shape, dt=F32):
        return nc.dram_tensor(name, list(shape), dt, kind="ExternalInput").ap()

    def dout(name, shape, dt=F32):
        return nc.dram_tensor(name, list(shape), dt, kind="ExternalOutput").ap()

    def dscr(name, shape, dt=BF16):
        return nc.dram_tensor(name, list(shape), dt, kind="Internal").ap()

    xp = din("xp", [SEQ, D])
    xw = din("xw", [NSLOT, ncols, D])
    xs = din("xs", [NS, DSEQ, D])
    cache = [din(n, [NS, PAST, 1024]) for n in ("c_sb_k", "c_sb_v", "c_sa_k", "c_sa_v")]
    cix = din("c_ix", [NS, PAST, 64])
    cst = din("c_st", [NS, 128, NFF, 2])
    cT = din("cT", [128, 16, NR])
    w_f32 = {
        "ada": din("w_ada", [D, 6 * D]), "in": din("w_in", [D, INC]), "gate": din("w_gate", [D, 2 * D]),
        "brsb": din("w_br_sb", [1024, D]), "brsa": din("w_br_sa", [1024, D]), "out": din("w_out", [D, D]),
        "up": din("w_up", [D, 2 * DFF]), "down": din("w_down", [DFF, D]),
    }
    b_ada_rows = din("b_ada_rows", [NR, 6 * D])
    g_mix = din("g_mix", [1, D])
    g_ffn = din("g_ffn", [1, D])
    g_final = din("g_final", [1, D])
    rel_table = din("rel_table", [32, 8])
    cwb_in = din("cwb", [128, NFF, 4])
    constf = din("constf", [128, 4, 128])
    sbmask_in = din("sbmask", [cfg.n_sbm, 128, ncols], BF16)
    idxmask_in = din("idxmask", [cfg.n_im, 128, 512], BF16)
    sbmask_s_in = din("sbmask_s", [128, DSEQ], BF16)
    oh_p = din("oh_p", [32, cfg.GL])
    oh_s = din("oh_s", [32, cfg.GLs])
    hflag_in = din("hflag", [128, NSLOT])
    wix_in = din("wix", [128, 16, 16])

    y_p = dout("y_p", [NSLOT, STRIDE, D])
    y_s = dout("y_s", [NS, DSEQ, D])
    okv_p = dout("okv_p", [SEQ, KVC])
    okv_s = dout("okv_s", [NS, DSEQ, KVC])
    pconv = dout("pconv", [128, NFF, 2])
    sconv = dout("sconv", [NS, 128, NFF, 2])

    wbf = {k: dscr("wbf_" + k, v.shape) for k, v in w_f32.items()}
    modrow = dscr("modrow", [NR, 6 * D], F32)

    def mkctx(name, Lk):
        return {"KT": [dscr(name + "_ktsb", [NH, 128, Lk]), dscr(name + "_ktsa", [NH, 128, Lk])],
                "V": [dscr(name + "_vsb", [Lk, 1024]), dscr(name + "_vsa", [Lk, 1024])],
                "KI2": dscr(name + "_ki2", [128, Lk]), "Lk": Lk}
    ctx_p = mkctx("cp", SEQ)
    ctx_s = [mkctx("cs%d" % s, LKS) for s in range(NS)]
    gvec_p = dscr("gvec_p", [8, cfg.GL])
    gvec_s = dscr("gvec_s", [8, cfg.GLs])
    strip_p = dscr("strip_p", [8, 128, cfg.UL])
    strip_s = dscr("strip_s", [8, 128, cfg.ULs])
    QTs = dscr("QTs", [4, 8, 128, ncols])
    hTs = dscr("hTs", [16, 128, ncols])
    OTs = dscr("OTs", [2, 8, 128, ncols])

    with ExitStack() as gst:
        S = Sched(nc, gst, 80)

        uid = [0]

        def PT(name, shape, dt=F32, st=gst):
            uid[0] += 1
            return st.enter_context(nc.sbuf_tensor("%s_u%d" % (name, uid[0]), list(shape), dt))

        identf = PT("identf", [128, 128])
        cbf = PT("cbf", [128, 4, 128], BF16)
        cwb = PT("cwb_t", [128, NFF, 4])
        CH = PT("CH", [128, 8])
        hflag = PT("hflag_t", [128, NSLOT])
        wraw = PT("wraw", [128, 4, NHI])
        convc = PT("convc", [128, NFF, 2])
        r_const = Res()
        r_wraw = Res()
        r_convc = Res()
        identb = cbf[:, 0, :]
        nidentb = cbf[:, 1, :]
        trib = cbf[:, 2, :]
        onesb = cbf[:, 3, :]
        PB = [gst.enter_context(nc.psum_tensor("pb%d" % i, [128, 512], F32)) for i in range(8)]
        RB = [Res(excl=True) for _ in range(8)]
        r_scr = {"modrow": Res(), "wbf": Res(), "strip": Res(), "QT": Res(), "hT": Res(), "OT": Res(), "ctx": Res(),
                 "out": Res()}

        def setup():
            with ExitStack() as st:
                dl = [S.getd() for _ in range(6)]
                dsw = S.getd(sw=True)
                for k in ("in", "ada", "gate", "brsb", "brsa", "out", "up", "down"):
                    src = w_f32[k]
                    n = src.shape[0] * src.shape[1] // 2048
                    s2 = src.rearrange("r c -> (r c)").rearrange("(n k) -> n k", k=2048)
                    d2 = wbf[k].rearrange("r c -> (r c)").rearrange("(n k) -> n k", k=2048)
                    o = 0
                    while o < n:
                        m = min(4096, n - o)
                        S.op("pool", (lambda a, b: lambda h: h.dma_start(out=a, in_=b))(d2[o:o + m], s2[o:o + m]),
                             writes=[Res()], dsem=dsw)
                        o += m
                ctmp = PT("ctmp", [128, 4, 128], F32, st)
                rt = Res()
                S.op("sp", lambda h: h.dma_start(out=ctmp[:], in_=constf), writes=[rt], dsem=dl[1])
                S.op("sp", lambda h: h.dma_start(out=identf[:], in_=constf[:, 0, :]), writes=[r_const], dsem=dl[2])
                S.op("sp", lambda h: h.dma_start(out=cwb[:], in_=cwb_in), writes=[r_const], dsem=dl[2])
                S.op("sp", lambda h: h.dma_start(out=hflag[:], in_=hflag_in), writes=[r_const], dsem=dl[2])
                S.op("sp", lambda h: h.dma_start(out=CH[:], in_=rel_table[15:16, :].to_broadcast([128, 8])),
                     writes=[r_const], dsem=dl[2])
                S.op("dve", lambda h: h.tensor_copy(out=cbf[:], in_=ctmp[:]), reads=[rt], writes=[r_const])
                cTt = PT("cTt", [128, 16, NR], F32, st)
                sT = PT("sT", [128, 16, NR], BF16, st)
                rc, rs = Res(), Res()
                S.op("sp", lambda h: h.dma_start(out=cTt[:], in_=cT), writes=[rc], dsem=dl[1])
                S.op("act", lambda h: h.activation(out=sT[:], in_=cTt[:], func=AF.Silu), reads=[rc], writes=[rs])
                relt = PT("relt", [32, 8], F32, st)
                rr = Res()
                S.op("sp", lambda h: h.dma_start(out=relt[:], in_=rel_table), writes=[rr], dsem=dl[1])
                def mkstrip(oh, gv, GLx, strip, ULx, nm):
                    oht = PT("oht" + nm, [32, GLx], F32, st)
                    gst_t = PT("gst" + nm, [8, GLx], BF16, st)
                    ro, rg, rgv = Res(), Res(), Res()
                    S.op("sp", (lambda a, b: lambda h: h.dma_start(out=a, in_=b))(oht[:], oh), writes=[ro], dsem=dl[1])
                    o = 0
                    i = 0
                    while o < GLx:
                        m = min(512, GLx - o)
                        bk = 6 + (i % 2)
                        S.op("pe", (lambda bk, o, m: lambda h: h.matmul(PB[bk][:8, :m], relt[:], oht[:, o:o + m],
                                                                         start=True, stop=True))(bk, o, m),
                             reads=[rr, ro], writes=[RB[bk]])
                        S.op("act", (lambda bk, o, m: lambda h: h.activation(out=gst_t[:, o:o + m], in_=PB[bk][:8, :m],
                                                                             func=AF.Copy))(bk, o, m),
                             reads=[RB[bk]], writes=[rg])
                        o += m
                        i += 1
                    S.op("sp", (lambda a, b: lambda h: h.dma_start(out=a, in_=b))(gv, gst_t[:]), reads=[rg], writes=[rgv],
                         dsem=dl[3])
                    for p in range(128):
                        S.op("sp", (lambda p, strip, gv, ULx: lambda h: h.dma_start(
                            out=strip[:, p, :], in_=gv[:, 127 - p:127 - p + ULx]))(p, strip, gv, ULx),
                            reads=[rgv], writes=[Res()], dsem=dl[4])
                mkstrip(oh_p, gvec_p, cfg.GL, strip_p, cfg.UL, "p")
                mkstrip(oh_s, gvec_s, cfg.GLs, strip_s, cfg.ULs, "s")
                S.flush()
                wr = Ring(S, nc, st, "wada", [128, 16, 512], BF16, 2)
                br = Ring(S, nc, st, "bada", [NR, 512], F32, 2)
                ms = Ring(S, nc, st, "mst", [NR, 512], F32, 2)
                wv = wbf["ada"].rearrange("(kc p) c -> p kc c", p=128)
                for pc in range(24):
                    wt, wres, wd = wr.nxt()
                    bt, bres, bd = br.nxt()
                    mt, mres, md = ms.nxt()
                    S.op("sp", (lambda wt, pc: lambda h: h.dma_start(out=wt[:], in_=wv[:, :, pc * 512:(pc + 1) * 512]))(wt, pc),
                         writes=[wres], dsem=wd)
                    S.op("sp", (lambda bt, pc: lambda h: h.dma_start(out=bt[:], in_=b_ada_rows[:, pc * 512:(pc + 1) * 512]))(bt, pc),
                         writes=[bres], dsem=bd)
                    bk = pc % 2

                    def mmf(h, wt=wt, bk=bk):
                        for kc in range(16):
                            ins = h.matmul(PB[bk][:NR, :], sT[:, kc, :], wt[:, kc, :], start=(kc == 0), stop=(kc == 15))
                        return ins
                    S.op("pe", mmf, reads=[rs, wres], writes=[RB[bk]])
                    S.op("dve", (lambda mt, bt, bk: lambda h: h.tensor_tensor(out=mt[:], in0=PB[bk][:NR, :], in1=bt[:],
                                                                             op=ALU.add))(mt, bt, bk),
                         reads=[RB[bk], bres], writes=[mres])
                    S.op("sp", (lambda mt, pc: lambda h: h.dma_start(out=modrow[:, pc * 512:(pc + 1) * 512], in_=mt[:]))(mt, pc),
                         reads=[mres], writes=[Res()], dsem=md)
                S.flush()
                wr.release(); br.release(); ms.release()
                S.putd(dl)
                S.putd([dsw], sw=True)

        def load_mod(st, dl, r, which):
            A = PT("modA", [128, D], F32, st)
            Bt = PT("modB", [128, D], F32, st)
            Gt = PT("modG", [128, D], F32, st)
            rA, rB, rG = Res(), Res(), Res()
            base = 0 if which == 1 else 3 * D
            g = g_mix if which == 1 else g_ffn
            S.op("sp", lambda h: h.dma_start(out=A[:], in_=modrow[r:r + 1, base + D:base + 2 * D].to_broadcast([128, D])),
                 writes=[rA], dsem=dl[0])
            S.op("sp", lambda h: h.dma_start(out=Bt[:], in_=modrow[r:r + 1, base:base + D].to_broadcast([128, D])),
                 writes=[rB], dsem=dl[1])
            S.op("sp", lambda h: h.dma_start(out=Gt[:], in_=g.to_broadcast([128, D])), writes=[rG], dsem=dl[2])
            S.op("dve", lambda h: h.scalar_tensor_tensor(out=A[:], in0=A[:], scalar=1.0, in1=Gt[:], op0=ALU.add, op1=ALU.mult),
                 reads=[rA, rG], writes=[rA])
            return A, Bt, rA, rB

        def load_row_rep(st, dl, name, src_row):
            t = PT(name, [128, D], F32, st)
            rr = Res()
            S.op("sp", lambda h: h.dma_start(out=t[:], in_=src_row.to_broadcast([128, D])), writes=[rr], dsem=dl)
            return t, rr

        class NormT:
            def __init__(self, st, name):
                self.junk = PT(name + "_junk", [128, D], BF16, st)
                self.hb = [PT(name + "_hb%d" % i, [128, D], F32, st) for i in range(2)]
                self.rhb = [Res(), Res()]
                self.sm = PT(name + "_sm", [128, 8], F32, st)
                self.rj = Res()
                self.rsm = Res()
                self.k = 0

            def run(self, xt, rx, rows, A, Bt, rA, rB, dst, rdst, c0):
                k = self.k
                self.k += 1
                hb, rhb = self.hb[k % 2], self.rhb[k % 2]
                sm, rsm, junk, rj = self.sm, self.rsm, self.junk, self.rj
                S.op("dve", lambda h: h.scalar_tensor_tensor(out=junk[:rows], in0=xt[:rows], scalar=1.0, in1=xt[:rows],
                                                             op0=ALU.mult, op1=ALU.mult, accum_out=sm[:rows, 0:1]),
                     reads=[rx], writes=[rj, rsm])
                S.op("dve", lambda h: h.tensor_scalar(out=sm[:rows, 1:2], in0=sm[:rows, 0:1], scalar1=1.0 / D, scalar2=EPS,
                                                      op0=ALU.mult, op1=ALU.add), reads=[rsm], writes=[rsm])
                S.op("act", lambda h: h.activation(out=sm[:rows, 2:3], in_=sm[:rows, 1:2], func=AF.Ln), reads=[rsm], writes=[rsm])
                S.op("act", lambda h: h.activation(out=sm[:rows, 3:4], in_=sm[:rows, 2:3], func=AF.Exp, scale=-0.5),
                     reads=[rsm], writes=[rsm])
                S.op("dve", lambda h: h.scalar_tensor_tensor(out=hb[:rows], in0=xt[:rows], scalar=sm[:rows, 3:4], in1=A[:rows],
                                                             op0=ALU.mult, op1=ALU.mult), reads=[rx, rsm, rA], writes=[rhb])
                S.op("pool", lambda h: h.tensor_tensor(out=hb[:rows], in0=hb[:rows], in1=Bt[:rows], op=ALU.add),
                     reads=[rhb, rB], writes=[rhb])
                for g4 in range(4):
                    bk = 6 + (g4 % 2)

                    def tp(h, g4=g4, bk=bk):
                        for q in range(4):
                            fc = g4 * 4 + q
                            ins = h.transpose(out=PB[bk][:, q * 128:q * 128 + rows], in_=hb[:rows, fc * 128:(fc + 1) * 128],
                                              identity=identf[:rows, :rows])
                        return ins
                    S.op("pe", tp, reads=[rhb, r_const], writes=[RB[bk]])
                    src = PB[bk][:, :].rearrange("p (q c) -> p q c", q=4)[:, :, 0:rows]
                    S.op("act", (lambda g4, src: lambda h: h.activation(out=dst[:, g4 * 4:(g4 + 1) * 4, c0:c0 + rows], in_=src,
                                                                        func=AF.Copy))(g4, src),
                         reads=[RB[bk]], writes=[rdst])

        KVBLK = [(C_KSB, 512, "k", 0, 0, 0), (C_KSB + 512, 512, "k", 0, 4, 512),
                 (C_VSB, 512, "v", 0, 0, 1024), (C_VSB + 512, 512, "v", 0, 4, 1536),
                 (C_KSA, 512, "k", 1, 0, 2048), (C_KSA + 512, 512, "k", 1, 4, 2560),
                 (C_VSA, 512, "v", 1, 0, 3072), (C_VSA + 512, 512, "v", 1, 4, 3584),
                 (C_KIX, 64, "i", 0, 0, 4096)]

        def phaseA(ctx, xsrc, ntok, tok0, okv, r, TT):
            with ExitStack() as st:
                dl = [S.getd() for _ in range(4)]
                A, Bt, rA, rB = load_mod(st, dl, r, 1)
                nt = NormT(st, "na")
                ncol_t = min(TT, ntok)
                hT = PT("a_hT", [128, 16, ncol_t], BF16, st)
                rhT = Res()
                xr = Ring(S, nc, st, "a_x", [128, D], F32, 3)
                wr = Ring(S, nc, st, "a_w", [128, 16, 512], BF16, 3)
                sf = Ring(S, nc, st, "a_sf", [128, 512], F32, 4, sw_dsem=True)
                kts = Ring(S, nc, st, "a_kt", [128, 4, ncol_t], BF16, 2)
                ki2 = Ring(S, nc, st, "a_ki2", [128, ncol_t], BF16, 2)
                kid = Ring(S, nc, st, "a_kid", [128, 128], F32, 2, with_dsem=False)
                win = wbf["in"].rearrange("(kc p) c -> p kc c", p=128)
                mmk = 0
                ntiles = (ntok + TT - 1) // TT

                def wthunk(wc0, wn):
                    def f():
                        wt, wres, wd = wr.nxt()
                        S.op("sp", lambda h: h.dma_start(out=wt[:, :, :wn], in_=win[:, :, wc0:wc0 + wn]),
                             reads=[r_scr["wbf"]], writes=[wres], dsem=wd)
                        return wt, wres
                    return f
                wpre = Pre([wthunk(b[0], b[1]) for _ in range(ntiles) for b in KVBLK], 1)
                wi = 0
                for t0 in range(0, ntok, TT):
                    n = min(TT, ntok - t0)
                    blks = blocks_of(n)
                    for (b0, rows) in blks:
                        xt, rx, xd = xr.nxt()
                        S.op("sp", (lambda xt, a, rows: lambda h: h.dma_start(out=xt[:rows], in_=a))(xt, xsrc[t0 + b0:t0 + b0 + rows, :], rows),
                             writes=[rx], dsem=xd)
                        nt.run(xt, rx, rows, A, Bt, rA, rB, hT, rhT, b0)
                    for (wc0, wn, kind, which, head0, oc0) in KVBLK:
                        wt, wres = wpre.get(wi)
                        wi += 1
                        if kind == "k":
                            kt, rkt, kd = kts.nxt()
                        if kind == "i":
                            k2, rk2, k2d = ki2.nxt()
                        for (b0, rows) in blks:
                            bk = mmk % 3
                            mmk += 1

                            def mmf(h, wt=wt, bk=bk, b0=b0, rows=rows, wn=wn):
                                for kc in range(16):
                                    ins = h.matmul(PB[bk][:rows, :wn], hT[:, kc, b0:b0 + rows], wt[:, kc, :wn],
                                                   start=(kc == 0), stop=(kc == 15))
                                return ins
                            S.op("pe", mmf, reads=[rhT, wres], writes=[RB[bk]])
                            sft, rsf, sfd = sf.nxt()
                            sfd2 = sf.d2[(sf.i - 1) % sf.n]
                            eng = "act" if (mmk % 2) else "dve"
                            if eng == "act":
                                S.op("act", (lambda sft, bk, rows, wn: lambda h: h.activation(out=sft[:rows, :wn], in_=PB[bk][:rows, :wn],
                                                                                              func=AF.Copy))(sft, bk, rows, wn),
                                     reads=[RB[bk]], writes=[rsf])
                            else:
                                S.op("dve", (lambda sft, bk, rows, wn: lambda h: h.tensor_copy(out=sft[:rows, :wn], in_=PB[bk][:rows, :wn]))(sft, bk, rows, wn),
                                     reads=[RB[bk]], writes=[rsf])
                            S.op("sp", (lambda sft, rows, wn, a: lambda h: h.dma_start(out=a, in_=sft[:rows, :wn]))(
                                sft, rows, wn, okv[t0 + b0:t0 + b0 + rows, oc0:oc0 + wn]),
                                reads=[rsf], writes=[Res()], dsem=sfd)
                            if kind == "v":
                                vdst = ctx["V"][which][tok0 + t0 + b0:tok0 + t0 + b0 + rows, head0 * 128:head0 * 128 + 512]
                                S.op("pool", (lambda sft, rows, a: lambda h: h.dma_start(out=a, in_=sft[:rows, :]))(sft, rows, vdst),
                                     reads=[rsf], writes=[Res()], dsem=sfd2)
                            elif kind == "k":
                                bk2 = 3 + (mmk % 2)

                                def tpf(h, sft=sft, bk2=bk2, rows=rows):
                                    for q in range(4):
                                        ins = h.transpose(out=PB[bk2][:, q * 128:q * 128 + rows], in_=sft[:rows, q * 128:(q + 1) * 128],
                                                          identity=identf[:rows, :rows])
                                    return ins
                                S.op("pe", tpf, reads=[rsf, r_const], writes=[RB[bk2]])
                                src = PB[bk2][:, :].rearrange("p (q c) -> p q c", q=4)[:, :, 0:rows]
                                S.op("dve", (lambda kt, src, b0, rows: lambda h: h.tensor_copy(out=kt[:, :, b0:b0 + rows], in_=src))(kt, src, b0, rows),
                                     reads=[RB[bk2]], writes=[rkt])
                            else:
                                kdt, rkd, _ = kid.nxt()
                                S.op("pool", (lambda kdt, sft, rows: lambda h: h.tensor_copy(out=kdt[:rows, 0:64], in_=sft[:rows, 0:64]))(kdt, sft, rows),
                                     reads=[rsf], writes=[rkd])
                                S.op("pool", (lambda kdt, sft, rows: lambda h: h.tensor_copy(out=kdt[:rows, 64:128], in_=sft[:rows, 0:64]))(kdt, sft, rows),
                                     reads=[rsf, rkd], writes=[rkd])
                                bk2 = 5
                                S.op("pe", (lambda kdt, rows: lambda h: h.transpose(out=PB[5][:, :rows], in_=kdt[:rows, :],
                                                                                    identity=identf[:rows, :rows]))(kdt, rows),
                                     reads=[rkd, r_const], writes=[RB[5]])
                                S.op("dve", (lambda k2, b0, rows: lambda h: h.tensor_copy(out=k2[:, b0:b0 + rows], in_=PB[5][:, :rows]))(k2, b0, rows),
                                     reads=[RB[5]], writes=[rk2])
                        if kind == "k":
                            for q in range(4):
                                S.op("sp", (lambda kt, q, a, n: lambda h: h.dma_start(out=a, in_=kt[:, q, :n]))(
                                    kt, q, ctx["KT"][which][head0 + q, :, tok0 + t0:tok0 + t0 + n], n),
                                    reads=[rkt], writes=[Res()], dsem=kd)
                        if kind == "i":
                            S.op("sp", (lambda k2, a, n: lambda h: h.dma_start(out=a, in_=k2[:, :n]))(
                                k2, ctx["KI2"][:, tok0 + t0:tok0 + t0 + n], n), reads=[rk2], writes=[Res()], dsem=k2d)
                S.flush()
                for rg in (xr, wr, sf, kts, ki2):
                    rg.release()
                S.putd(dl)

        def cache_import(s):
            ctx = ctx_s[s]
            with ExitStack() as st:
                dl = [S.getd() for _ in range(2)]
                dsw = [S.getd(sw=True) for _ in range(2)]
                for which, ci in ((0, 1), (1, 3)):
                    S.op("pool", (lambda a, b: lambda h: h.dma_start(out=a, in_=b))(ctx["V"][which][0:PAST, :], cache[ci][s]),
                         writes=[Res()], dsem=dsw[which])
                cr = Ring(S, nc, st, "ci_c", [128, 1024], F32, 3)
                kst = Ring(S, nc, st, "ci_k", [128, 8, 512], BF16, 2)
                for which, ci in ((0, 0), (1, 2)):
                    for g4 in range(PAST // 512):
                        kt, rkt, kd = kst.nxt()
                        for bb in range(4):
                            tb = g4 * 4 + bb
                            ct, rct, cd = cr.nxt()
                            S.op("sp", (lambda ct, a: lambda h: h.dma_start(out=ct[:], in_=a))(ct, cache[ci][s, tb * 128:(tb + 1) * 128, :]),
                                 writes=[rct], dsem=cd)
                            for hh in range(2):
                                bk = (tb * 2 + hh) % 4

                                def tpf(h, ct=ct, bk=bk, hh=hh):
                                    for q in range(4):
                                        hd = hh * 4 + q
                                        ins = h.transpose(out=PB[bk][:, q * 128:(q + 1) * 128], in_=ct[:, hd * 128:(hd + 1) * 128],
                                                          identity=identf[:])
                                    return ins
                                S.op("pe", tpf, reads=[rct, r_const], writes=[RB[bk]])
                                src = PB[bk][:, :].rearrange("p (q c) -> p q c", q=4)
                                eng = "act" if hh else "dve"
                                if eng == "act":
                                    S.op("act", (lambda kt, src, hh, bb: lambda h: h.activation(out=kt[:, hh * 4:hh * 4 + 4, bb * 128:(bb + 1) * 128],
                                                                                               in_=src, func=AF.Copy))(kt, src, hh, bb),
                                         reads=[RB[bk]], writes=[rkt])
                                else:
                                    S.op("dve", (lambda kt, src, hh, bb: lambda h: h.tensor_copy(out=kt[:, hh * 4:hh * 4 + 4, bb * 128:(bb + 1) * 128],
                                                                                                in_=src))(kt, src, hh, bb),
                                         reads=[RB[bk]], writes=[rkt])
                        for hd in range(8):
                            S.op("sp", (lambda kt, hd, a: lambda h: h.dma_start(out=a, in_=kt[:, hd, :]))(
                                kt, hd, ctx["KT"][which][hd, :, g4 * 512:(g4 + 1) * 512]), reads=[rkt], writes=[Res()], dsem=kd)
                ir = Ring(S, nc, st, "ci_i", [128, 128], F32, 3)
                i2 = Ring(S, nc, st, "ci_i2", [128, 512], BF16, 2)
                for g4 in range(PAST // 512):
                    k2, rk2, k2d = i2.nxt()
                    for bb in range(4):
                        tb = g4 * 4 + bb
                        it, rit, idd = ir.nxt()
                        S.op("sp", (lambda it, a: lambda h: h.dma_start(out=it[:, 0:64], in_=a))(it, cix[s, tb * 128:(tb + 1) * 128, :]),
                             writes=[rit], dsem=idd)
                        S.op("sp", (lambda it, a: lambda h: h.dma_start(out=it[:, 64:128], in_=a))(it, cix[s, tb * 128:(tb + 1) * 128, :]),
                             writes=[rit], dsem=idd)
                        S.op("pe", (lambda it: lambda h: h.transpose(out=PB[5][:, :128], in_=it[:], identity=identf[:]))(it),
                             reads=[rit, r_const], writes=[RB[5]])
                        S.op("dve", (lambda k2, bb: lambda h: h.tensor_copy(out=k2[:, bb * 128:(bb + 1) * 128], in_=PB[5][:, :128]))(k2, bb),
                             reads=[RB[5]], writes=[rk2])
                    S.op("sp", (lambda k2, a: lambda h: h.dma_start(out=a, in_=k2[:]))(k2, ctx["KI2"][:, g4 * 512:(g4 + 1) * 512]),
                         reads=[rk2], writes=[Res()], dsem=k2d)
                S.flush()
                for rg in (cr, kst, ir, i2):
                    rg.release()
                S.putd(dl)
                S.putd(dsw, sw=True)

        def win_q(xsrc, n, r):
            with ExitStack() as st:
                dl = [S.getd() for _ in range(4)]
                A, Bt, rA, rB = load_mod(st, dl, r, 1)
                nt = NormT(st, "nq")
                hT = PT("q_hT", [128, 16, n], BF16, st)
                rhT = Res()
                xr = Ring(S, nc, st, "q_x", [128, D], F32, 3)
                wr = Ring(S, nc, st, "q_w", [128, 16, 512], BF16, 2)
                qs = Ring(S, nc, st, "q_s", [128, n], BF16, 4)
                win = wbf["in"].rearrange("(kc p) c -> p kc c", p=128)
                blks = blocks_of(n)
                for (b0, rows) in blks:
                    xt, rx, xd = xr.nxt()
                    S.op("sp", (lambda xt, a, rows: lambda h: h.dma_start(out=xt[:rows], in_=a))(xt, xsrc[b0:b0 + rows, :], rows),
                         writes=[rx], dsem=xd)
                    nt.run(xt, rx, rows, A, Bt, rA, rB, hT, rhT, b0)
                import os as _os
                qskip = int(_os.environ.get("MK_QSKIP", "0"))
                for fc in range(16 if not (qskip & 1) else 0):
                    S.op("sp", (lambda fc: lambda h: h.dma_start(out=hTs[fc, :, :n], in_=hT[:, fc, :]))(fc), reads=[rhT],
                         writes=[r_scr["hT"]], dsem=dl[3])
                k = 0
                for (kind, c0, scale) in (((0, C_QSB, SCALE), (2, C_QSA, SCALE), (3, C_QIX, 0.125)) if not (qskip & 2) else ()):
                    for hg in range(2):
                        wt, wres, wd = wr.nxt()
                        S.op("sp", (lambda wt, c: lambda h: h.dma_start(out=wt[:], in_=win[:, :, c:c + 512]))(wt, c0 + hg * 512),
                             reads=[r_scr["wbf"]], writes=[wres], dsem=wd)
                        for h4 in range(4):
                            hd = hg * 4 + h4
                            bk = k % 3
                            k += 1

                            def mmf(h, wt=wt, bk=bk, h4=h4):
                                for kc in range(16):
                                    ins = h.matmul(PB[bk][:, :n], wt[:, kc, h4 * 128:(h4 + 1) * 128], hT[:, kc, :], start=(kc == 0), stop=(kc == 15))
                                return ins
                            S.op("pe", mmf, reads=[rhT, wres], writes=[RB[bk]])
                            qt, rq, qd = qs.nxt()
                            S.op("act", (lambda qt, bk, scale: lambda h: h.activation(out=qt[:], in_=PB[bk][:, :n], func=AF.Copy, scale=scale))(qt, bk, scale),
                                 reads=[RB[bk]], writes=[rq])
                            if not (qskip & 8):
                                S.op("sp", (lambda qt, kind, hd: lambda h: h.dma_start(out=QTs[kind, hd, :, :n], in_=qt[:]))(qt, kind, hd),
                                     reads=[rq], writes=[r_scr["QT"]], dsem=qd)
                            if kind == 0 and not (qskip & 16):
                                qt2, rq2, qd2 = qs.nxt()
                                S.op("dve", (lambda qt2, qt: lambda h: h.tensor_scalar(out=qt2[:], in0=qt[:], scalar1=-1.0, scalar2=None,
                                                                                      op0=ALU.mult))(qt2, qt), reads=[rq], writes=[rq2])
                                S.op("sp", (lambda qt2, hd: lambda h: h.dma_start(out=QTs[1, hd, :, :n], in_=qt2[:]))(qt2, hd),
                                     reads=[rq2], writes=[r_scr["QT"]], dsem=qd2)
                wx = PT("q_wx", [128, 16, 16], BF16, st)
                wxf = PT("q_wxf", [128, 16, 16], F32, st)
                rwx, rwxf = Res(), Res()
                S.op("sp", lambda h: h.dma_start(out=wxf[:], in_=wix_in), writes=[rwxf], dsem=dl[3])
                S.op("dve", lambda h: h.tensor_copy(out=wx[:], in_=wxf[:]), reads=[rwxf], writes=[rwx])
                for i, (b0, rows) in enumerate(blks if not (qskip & 4) else []):
                    def mmw(h, b0=b0, rows=rows):
                        for kc in range(16):
                            ins = h.matmul(PB[4][:rows, :16], hT[:, kc, b0:b0 + rows], wx[:, kc, :], start=(kc == 0), stop=(kc == 15))
                        return ins
                    S.op("pe", mmw, reads=[rhT, rwx], writes=[RB[4]])
                    S.op("dve", (lambda i, rows: lambda h: h.tensor_copy(out=wraw[:rows, i, :], in_=PB[4][:rows, :16]))(i, rows),
                         reads=[RB[4]], writes=[r_wraw])
                S.flush()
                for rg in (xr, wr, qs):
                    rg.release()
                S.putd(dl)

        def win_attn(ctx, n, kblocks, slot, strip, ULx, u0_of, near_of, sbmask_of, idxmask_of):
            KE = kblocks[-1][0] + kblocks[-1][1]
            qblks = blocks_of(n)
            nqb = len(qblks)
            nkb = len(kblocks)
            with ExitStack() as st:
                dl = [S.getd() for _ in range(6)]
                scores = PT("at_sc", [128, KE], F32, st)
                masks = [PT("at_m%d" % i, [128, KE], FP8, st) for i in range(nqb)]
                rsc = Res()
                rmk = [Res() for _ in range(nqb)]
                qix = PT("at_qix", [128, 8, n], BF16, st)
                rqix = Res()
                S.op("sp", lambda h: h.dma_start(out=qix[:], in_=QTs[3, :, :, :n].rearrange("a p c -> p a c")), reads=[r_scr["QT"]],
                     writes=[rqix], dsem=dl[0])
                CK = 1024
                kvr_k = Ring(S, nc, st, "at_k", [128, CK], BF16, 3)
                kvr_v = Ring(S, nc, st, "at_v", [128, CK // 128, 128], BF16, 3)
                kir = Ring(S, nc, st, "at_ki", [128, 512], BF16, 3)
                imr = Ring(S, nc, st, "at_im", [128, 512], BF16, 2)
                smr = Ring(S, nc, st, "at_sm", [128, n], BF16, 2)
                qr = Ring(S, nc, st, "at_q", [128, n], BF16, 4)
                er = Ring(S, nc, st, "at_e", [128, n], F32, 2, with_dsem=False)
                spr = Ring(S, nc, st, "at_sp", [128, n], BF16, 2, with_dsem=False)
                ar = Ring(S, nc, st, "at_a", [128, n], BF16, 3, with_dsem=False)
                rr_ = Ring(S, nc, st, "at_r", [128, 512], BF16, 4, with_dsem=False)
                osr = Ring(S, nc, st, "at_os", [128, n], BF16, 2)
                Sacc = PT("at_S", [128, n], BF16, st)
                rS = Res()
                Dh = PT("at_Dh", [128, NHI, 128], BF16, st)
                rDh = Res()
                wsm = PT("at_wsm", [128, 2, NHI], F32, st)
                rws = Res()
                bs = PT("at_bs", [128, 16], F32, st)
                rbs = Res()
                stp = PT("at_strip", [128, ULx], BF16, st)
                rstp = Res()
                dstp = dl[1]
                rden = PT("at_rden", [128, n], F32, st)
                rrden = Res()

                chunks = []
                i = 0
                while i < nkb:
                    j = i
                    while j < nkb and kblocks[j][0] + kblocks[j][1] <= kblocks[i][0] + CK:
                        j += 1
                    chunks.append((i, j))
                    i = j

                def load_kv(which, hd, ci):
                    i0, i1 = chunks[ci]
                    k0 = kblocks[i0][0]
                    kn = kblocks[i1 - 1][0] + kblocks[i1 - 1][1] - k0
                    kt, rk, kd = kvr_k.nxt()
                    vt, rv, vd = kvr_v.nxt()
                    S.op("sp", (lambda kt, a, kn: lambda h: h.dma_start(out=kt[:, :kn], in_=a))(kt, ctx["KT"][which][hd, :, k0:k0 + kn], kn),
                         reads=[r_scr["ctx"]], writes=[rk], dsem=kd)
                    nfull = kn // 128
                    if nfull:
                        S.op("sp", (lambda vt, a, nfull: lambda h: h.dma_start(out=vt[:, :nfull, :], in_=a))(
                            vt, ctx["V"][which][k0:k0 + nfull * 128, hd * 128:(hd + 1) * 128].rearrange("(b p) d -> p b d", p=128), nfull),
                            reads=[r_scr["ctx"]], writes=[rv], dsem=vd)
                    rem = kn - nfull * 128
                    if rem:
                        S.op("sp", (lambda vt, a, nfull, rem: lambda h: h.dma_start(out=vt[:rem, nfull, :], in_=a))(
                            vt, ctx["V"][which][k0 + nfull * 128:k0 + kn, hd * 128:(hd + 1) * 128], nfull, rem),
                            reads=[r_scr["ctx"]], writes=[rv], dsem=vd)
                    return kt, rk, vt, rv, k0

                def idx(qi):
                    q0, qrows = qblks[qi]
                    S.op("dve", lambda h: h.tensor_scalar(out=wsm[:qrows, 0, :], in0=wraw[:qrows, qi, :], scalar1=-0.25, scalar2=None,
                                                          op0=ALU.mult), reads=[r_wraw], writes=[rws])
                    S.op("dve", lambda h: h.scalar_tensor_tensor(out=wsm[:qrows, 0, :], in0=wraw[:qrows, qi, :], scalar=0.25, in1=wsm[:qrows, 0, :],
                                                                 op0=ALU.mult, op1=ALU.max), reads=[r_wraw, rws], writes=[rws])
                    S.op("act", lambda h: h.activation(out=wsm[:qrows, 1, :], in_=wraw[:qrows, qi, :], func=AF.Sign), reads=[r_wraw, rws],
                         writes=[rws])
                    for hh in range(NHI):
                        eng = "dve" if hh % 2 else "pool"
                        S.op(eng, (lambda hh: lambda h: h.tensor_scalar(out=Dh[:qrows, hh, :qrows], in0=identb[:qrows, :qrows],
                                                                       scalar1=wsm[:qrows, 1, hh:hh + 1], scalar2=None, op0=ALU.mult))(hh),
                             reads=[rws, r_const], writes=[rDh])
                    nkt = (KE + 511) // 512

                    def kithunk(kt_i):
                        def f():
                            k0 = kt_i * 512
                            kw = min(512, KE - k0)
                            kit, rki, kid_ = kir.nxt()
                            S.op("sp", lambda h: h.dma_start(out=kit[:, :kw], in_=ctx["KI2"][:, k0:k0 + kw]),
                                 reads=[r_scr["ctx"]], writes=[rki], dsem=kid_)
                            return kit, rki
                        return f
                    kpre = Pre([kithunk(k) for k in range(nkt)], 1)
                    def ktile(kt_i):
                        k0 = kt_i * 512
                        kw = min(512, KE - k0)
                        kit, rki = kpre.get(kt_i)
                        sb = 7
                        pend = []
                        for step in range(NHI + 2):
                            if step < NHI:
                                hh = step
                                bk = hh % 2
                                base = 64 * (hh % 2)
                                S.op("pe", (lambda hh, bk, base: lambda h: h.matmul(PB[bk][:qrows, :kw], qix[base:base + 64, hh // 2, q0:q0 + qrows],
                                                                                  kit[base:base + 64, :kw], start=True, stop=True))(hh, bk, base),
                                     reads=[rqix, rki], writes=[RB[bk]])
                                rt, rrt, _ = rr_.nxt()
                                if hh % 2 == 0:
                                    S.op("act", (lambda rt, bk, hh: lambda h: h.activation(out=rt[:qrows, :kw], in_=PB[bk][:qrows, :kw], func=AF.Relu,
                                                                                          scale=wsm[:qrows, 0, hh:hh + 1]))(rt, bk, hh),
                                         reads=[RB[bk], rws], writes=[rrt])
                                else:
                                    S.op("dve", (lambda rt, bk, hh: lambda h: h.tensor_scalar(out=rt[:qrows, :kw], in0=PB[bk][:qrows, :kw],
                                                                                             scalar1=wsm[:qrows, 0, hh:hh + 1], scalar2=0.0,
                                                                                             op0=ALU.mult, op1=ALU.max))(rt, bk, hh),
                                         reads=[RB[bk], rws], writes=[rrt])
                                pend.append((rt, rrt))
                            if step >= 2:
                                h2 = step - 2
                                rt, rrt = pend[h2]
                                S.op("pe", (lambda rt, h2: lambda h: h.matmul(PB[sb][:qrows, :kw], Dh[:qrows, h2, :qrows], rt[:qrows, :kw],
                                                                             start=(h2 == 0), stop=(h2 == NHI - 1)))(rt, h2),
                                     reads=[rrt, rDh], writes=[RB[sb]])
                        ima = idxmask_of(qi, kt_i)
                        if ima is not None:
                            imt, rim, imd = imr.nxt()
                            S.op("sp", (lambda imt, ima: lambda h: h.dma_start(out=imt[:], in_=ima))(imt, ima), writes=[rim], dsem=imd)
                            S.op("dve", (lambda imt: lambda h: h.tensor_tensor(out=scores[:qrows, k0:k0 + kw], in0=PB[sb][:qrows, :kw],
                                                                              in1=imt[:qrows, :kw], op=ALU.add))(imt),
                                 reads=[RB[sb], rim], writes=[rsc])
                        else:
                            S.op("act", lambda h: h.activation(out=scores[:qrows, k0:k0 + kw], in_=PB[sb][:qrows, :kw],
                                                               func=AF.Copy), reads=[RB[sb]], writes=[rsc])
                    for kt_i in range(nkt):
                        ktile(kt_i)
                        yield

                def bisect(qi):
                    q0, qrows = qblks[qi]
                    mk = masks[qi]
                    lo, hi, mid, cnt, ge, d1 = (bs[:qrows, c:c + 1] for c in range(6))
                    S.op("dve", lambda h: h.reduce_max(out=hi, in_=scores[:qrows, :KE], axis=AX.X), reads=[rsc], writes=[rbs])
                    S.op("dve", lambda h: h.tensor_scalar(out=lo, in0=hi, scalar1=-BRANGE, scalar2=None, op0=ALU.add), reads=[rbs], writes=[rbs])
                    S.op("dve", lambda h: h.tensor_scalar(out=hi, in0=hi, scalar1=1e-3, scalar2=None, op0=ALU.add), reads=[rbs], writes=[rbs])
                    for it in range(NBIS):
                        S.op("dve", lambda h: h.tensor_scalar(out=mid, in0=lo, scalar1=hi, scalar2=0.5, op0=ALU.add, op1=ALU.mult),
                             reads=[rbs], writes=[rbs])
                        S.op("dve", lambda h: h.tensor_scalar(out=mk[:qrows, :KE], in0=scores[:qrows, :KE], scalar1=mid, scalar2=0.0,
                                                              op0=ALU.is_ge, op1=ALU.add, accum_out=cnt, saturate=False),
                             reads=[rsc, rbs], writes=[rmk[qi], rbs])
                        S.op("dve", lambda h: h.tensor_scalar(out=ge, in0=cnt, scalar1=TOPK - 0.5, scalar2=None, op0=ALU.is_ge), reads=[rbs],
                             writes=[rbs])
                        S.op("dve", lambda h: h.tensor_tensor(out=d1, in0=mid, in1=lo, op=ALU.subtract), reads=[rbs], writes=[rbs])
                        S.op("dve", lambda h: h.scalar_tensor_tensor(out=lo, in0=d1, scalar=ge, in1=lo, op0=ALU.mult, op1=ALU.add), reads=[rbs],
                             writes=[rbs])
                        S.op("dve", lambda h: h.tensor_tensor(out=d1, in0=hi, in1=mid, op=ALU.subtract), reads=[rbs], writes=[rbs])
                        S.op("dve", lambda h: h.scalar_tensor_tensor(out=hi, in0=d1, scalar=ge, in1=mid, op0=ALU.mult, op1=ALU.add), reads=[rbs],
                             writes=[rbs])
                    S.op("dve", lambda h: h.tensor_scalar(out=mk[:qrows, :KE], in0=scores[:qrows, :KE], scalar1=lo, scalar2=-240.0,
                                                          op0=ALU.is_lt, op1=ALU.mult, saturate=False), reads=[rsc, rbs], writes=[rmk[qi]])

                def sb_head(hd):
                    qt, rq, qd = qr.nxt()
                    qn, rqn, qnd = qr.nxt()
                    S.op("sp", lambda h: h.dma_start(out=qt[:], in_=QTs[0, hd, :, :n]), reads=[r_scr["QT"]], writes=[rq], dsem=qd)
                    S.op("sp", lambda h: h.dma_start(out=qn[:], in_=QTs[1, hd, :, :n]), reads=[r_scr["QT"]], writes=[rqn], dsem=qnd)
                    S.op("pool", lambda h: h.memset(Sacc[:], 0.0), writes=[rS])
                    corder = list(reversed(range(len(chunks))))
                    kvpre = Pre([(lambda ci: lambda: load_kv(0, hd, ci))(ci) for ci in corder], 1)
                    kvc = {}
                    tiles = []
                    for pos, ci in enumerate(corder):
                        i0, i1 = chunks[ci]
                        for kbi in reversed(range(i0, i1)):
                            tiles.append((pos, ci, kbi))
                    T_ = len(tiles)
                    stt = [None] * T_

                    def Z(i):
                        pos, ci, kbi = tiles[i]
                        if ci not in kvc:
                            kvc[ci] = kvpre.get(pos)
                        kt, rk, vt, rv, kc0 = kvc[ci]
                        k0, rows = kblocks[kbi]
                        o = k0 - kc0
                        d = dict(kt=kt, rk=rk, vt=vt, rv=rv, o=o, vb=o // 128, rows=rows, zb=2 + (i % 2), lb=4 + (i % 2))
                        sma = sbmask_of(kbi)
                        d["sma"] = sma
                        if sma is not None:
                            smt, rsm, smd = smr.nxt()
                            S.op("sp", lambda h: h.dma_start(out=smt[:rows, :], in_=sma), writes=[rsm], dsem=smd)
                            d["smt"], d["rsm"] = smt, rsm
                        zb = d["zb"]

                        def zf(h):
                            ins = h.matmul(PB[zb][:rows, :n], kt[:, o:o + rows], qt[:], start=True, stop=(sma is None))
                            if sma is not None:
                                ins = h.matmul(PB[zb][:rows, :n], identb[:rows, :rows], d["smt"][:rows, :], start=False, stop=True)
                            return ins
                        S.op("pe", zf, reads=[rk, rq, r_const] + ([d["rsm"]] if sma is not None else []), writes=[RB[zb]])
                        et, ret, _ = er.nxt()
                        spt, rspt, _ = spr.nxt()
                        S.op("act", lambda h: h.activation(out=et[:rows, :], in_=PB[zb][:rows, :n], func=AF.Exp), reads=[RB[zb]], writes=[ret])
                        S.op("act", lambda h: h.activation(out=spt[:rows, :], in_=et[:rows, :], func=AF.Ln, bias=1.0), reads=[ret], writes=[rspt])
                        d["spt"], d["rspt"] = spt, rspt
                        stt[i] = d

                    def L(i):
                        d = stt[i]
                        kt, o, rows, lb, sma, spt = d["kt"], d["o"], d["rows"], d["lb"], d["sma"], d["spt"]

                        def lf(h):
                            h.matmul(PB[lb][:rows, :n], kt[:, o:o + rows], qn[:], start=True, stop=False)
                            if sma is not None:
                                h.matmul(PB[lb][:rows, :n], nidentb[:rows, :rows], d["smt"][:rows, :], start=False, stop=False)
                            h.matmul(PB[lb][:rows, :n], trib[:rows, :rows], spt[:rows, :], start=False, stop=False)
                            return h.matmul(PB[lb][:rows, :n], onesb[:, :rows], Sacc[:], start=False, stop=True)
                        S.op("pe", lf, reads=[d["rk"], rqn, r_const, d["rspt"], rS] + ([d["rsm"]] if sma is not None else []), writes=[RB[lb]])
                        at, rat, _ = ar.nxt()
                        S.op("act", lambda h: h.activation(out=at[:rows, :], in_=PB[lb][:rows, :n], func=AF.Exp, scale=-1.0), reads=[RB[lb]], writes=[rat])
                        S.op("pool", lambda h: h.tensor_tensor(out=Sacc[:rows, :], in0=Sacc[:rows, :], in1=spt[:rows, :], op=ALU.add),
                             reads=[d["rspt"], rS], writes=[rS])
                        d["at"], d["rat"] = at, rat

                    def O(i):
                        d = stt[i]
                        vt, vb, rows, at = d["vt"], d["vb"], d["rows"], d["at"]
                        S.op("pe", lambda h: h.matmul(PB[6][:, :n], vt[:rows, vb, :], at[:rows, :], start=(i == 0), stop=(i == T_ - 1)),
                             reads=[d["rv"], d["rat"]], writes=[RB[6]])
                        stt[i] = None
                    for i in range(T_ + 2):
                        if i < T_:
                            Z(i)
                        if 0 <= i - 1 < T_:
                            L(i - 1)
                        if 0 <= i - 2 < T_:
                            O(i - 2)
                        yield
                    ost, ros, osd = osr.nxt()
                    S.op("act", lambda h: h.activation(out=ost[:], in_=PB[6][:, :n], func=AF.Copy), reads=[RB[6]], writes=[ros])
                    S.op("sp", lambda h: h.dma_start(out=OTs[0, hd, :, :n], in_=ost[:]), reads=[ros], writes=[r_scr["OT"]], dsem=osd)

                def dsa_head(hd):
                    qt, rq, qd = qr.nxt()
                    S.op("sp", lambda h: h.dma_start(out=qt[:], in_=QTs[2, hd, :, :n]), reads=[r_scr["QT"]], writes=[rq], dsem=qd)
                    S.op("sp", lambda h: h.dma_start(out=stp[:], in_=strip[hd]), reads=[r_scr["strip"]], writes=[rstp], dsem=dstp)
                    corder = list(range(len(chunks)))
                    kvpre = Pre([(lambda ci: lambda: load_kv(1, hd, ci))(ci) for ci in corder], 1)
                    kvc = {}
                    tiles = []
                    for pos, ci in enumerate(corder):
                        i0, i1 = chunks[ci]
                        for kbi in range(i0, i1):
                            tiles.append((pos, ci, kbi))
                    T_ = len(tiles)
                    stt = [None] * T_

                    def LT(i):
                        pos, ci, kbi = tiles[i]
                        if ci not in kvc:
                            kvc[ci] = kvpre.get(pos)
                        kt, rk, vt, rv, kc0 = kvc[ci]
                        k0, rows = kblocks[kbi]
                        o = k0 - kc0
                        lb = i % 4
                        nearb = near_of(kbi)
                        u0 = u0_of(kbi) if nearb else 0

                        def lf(h):
                            ins = h.matmul(PB[lb][:rows, :n], kt[:, o:o + rows], qt[:], start=True, stop=False)
                            for qi, (q0, qrows) in enumerate(qblks):
                                lastm = (qi == nqb - 1) and not nearb
                                ins = h.matmul(PB[lb][:rows, q0:q0 + qrows], masks[qi][:qrows, k0:k0 + rows], identb[:qrows, :qrows],
                                               start=False, stop=lastm)
                            if nearb:
                                ins = h.matmul(PB[lb][:rows, :n], identb[:rows, :rows], stp[:rows, u0:u0 + n], start=False, stop=True)
                            return ins
                        S.op("pe", lf, reads=[rk, rq, r_const, rstp] + rmk, writes=[RB[lb]])
                        pt, rpt, _ = ar.nxt()
                        if nearb:
                            S.op("act", lambda h: h.activation(out=pt[:rows, :], in_=PB[lb][:rows, :n], func=AF.Exp), reads=[RB[lb]], writes=[rpt])
                        else:
                            S.op("act", lambda h: h.activation(out=pt[:rows, :], in_=PB[lb][:rows, :n], func=AF.Exp, bias=CH[:rows, hd:hd + 1]),
                                 reads=[RB[lb], r_const], writes=[rpt])
                        stt[i] = dict(vt=vt, rv=rv, vb=o // 128, rows=rows, pt=pt, rpt=rpt)

                    def O(i):
                        d = stt[i]
                        vt, vb, rows, pt = d["vt"], d["vb"], d["rows"], d["pt"]

                        def of(h):
                            h.matmul(PB[7][:, :n], vt[:rows, vb, :], pt[:rows, :], start=(i == 0), stop=(i == T_ - 1))
                            return h.matmul(PB[5][:, :n], onesb[:rows, :], pt[:rows, :], start=(i == 0), stop=(i == T_ - 1))
                        S.op("pe", of, reads=[d["rv"], d["rpt"], r_const], writes=[RB[7], RB[5]])
                        stt[i] = None
                    for i in range(T_ + 1):
                        if i < T_:
                            LT(i)
                        if 0 <= i - 1 < T_:
                            O(i - 1)
                    S.op("dve", lambda h: h.tensor_scalar(out=rden[:], in0=PB[5][:, :n], scalar1=1e-30, scalar2=None, op0=ALU.max), reads=[RB[5]],
                         writes=[rrden])
                    S.op("dve", lambda h: h.reciprocal(out=rden[:], in_=rden[:]), reads=[rrden], writes=[rrden])
                    ost, ros, osd = osr.nxt()
                    S.op("dve", lambda h: h.tensor_tensor(out=ost[:], in0=PB[7][:, :n], in1=rden[:], op=ALU.mult),
                         reads=[RB[7], rrden], writes=[ros])
                    S.op("sp", lambda h: h.dma_start(out=OTs[1, hd, :, :n], in_=ost[:]), reads=[ros], writes=[r_scr["OT"]], dsem=osd)

                heads = list(range(NH))

                def drain(g):
                    for _ in g:
                        pass
                nkt_ = (KE + 511) // 512
                for qi in range(nqb):
                    gI = idx(qi)
                    gS = sb_head(heads.pop(0)) if heads else iter(())
                    totI, totS = nkt_, nkb + 2
                    dI = dS = 0
                    lI = lS = True
                    while lI or lS:
                        pickI = lI and (not lS or dI * totS <= dS * totI)
                        try:
                            if pickI:
                                next(gI)
                                dI += 1
                            else:
                                next(gS)
                                dS += 1
                        except StopIteration:
                            if pickI:
                                lI = False
                            else:
                                lS = False
                    bisect(qi)
                    if heads:
                        drain(sb_head(heads.pop(0)))
                while heads:
                    drain(sb_head(heads.pop(0)))
                for hd in range(NH):
                    dsa_head(hd)
                S.flush()
                for rg in (kvr_k, kvr_v, kir, imr, smr, qr, osr):
                    rg.release()
                S.putd(dl)

        def win_ffn(xsrc, n, r, halo, ydst, nout, prev_src, conv_dst, conv_cols, slot_flag):
            blks = blocks_of(n)
            nb = len(blks)
            with ExitStack() as st0:
                x1 = [PT("f_x1_%d" % i, [128, D], F32, st0) for i in range(nb)]
                rx1 = [Res() for _ in range(nb)]
                h2T = PT("f_h2T", [128, 16, n], BF16, st0)
                rh2 = Res()
                with ExitStack() as st:
                    dl = [S.getd() for _ in range(4)]
                    hT = PT("fa_hT", [128, 16, n], BF16, st)
                    oT = PT("fa_oT", [128, 16, n], BF16, st)
                    mT = PT("fa_mT", [128, 16, n], BF16, st)
                    rhT, roT, rmT = Res(), Res(), Res()
                    S.op("sp", lambda h: h.dma_start(out=hT[:], in_=hTs[:, :, :n].rearrange("a p c -> p a c")), reads=[r_scr["hT"]], writes=[rhT], dsem=dl[0])
                    for t in range(2):
                        S.op("sp", (lambda t: lambda h: h.dma_start(out=oT[:, t * 8:(t + 1) * 8, :], in_=OTs[t, :, :, :n].rearrange("a p c -> p a c")))(t),
                             reads=[r_scr["OT"]], writes=[roT], dsem=dl[1])
                    for i, (b0, rows) in enumerate(blks):
                        S.op("sp", (lambda i, b0, rows: lambda h: h.dma_start(out=x1[i][:rows], in_=xsrc[b0:b0 + rows, :]))(i, b0, rows), writes=[rx1[i]],
                             dsem=dl[2])
                    gt1, rgt1 = load_row_rep(st, dl[3], "fa_gt1", modrow[r:r + 1, 2 * D:3 * D])
                    wg = Ring(S, nc, st, "fa_wg", [128, 16, 256], BF16, 2)
                    wb = Ring(S, nc, st, "fa_wb", [128, 16, 128], BF16, 2)
                    gs = Ring(S, nc, st, "fa_g", [128, n], F32, 4, with_dsem=False)
                    wgv = wbf["gate"].rearrange("(kc p) c -> p kc c", p=128)
                    wbv = [wbf["brsb"].rearrange("(kc p) c -> p kc c", p=128), wbf["brsa"].rearrange("(kc p) c -> p kc c", p=128)]

                    def gthunk(fb):
                        def f():
                            wt, wres, wd = wg.nxt()
                            S.op("sp", lambda h: h.dma_start(out=wt[:, :, 0:128], in_=wgv[:, :, fb * 128:(fb + 1) * 128]),
                                 reads=[r_scr["wbf"]], writes=[wres], dsem=wd)
                            S.op("sp", lambda h: h.dma_start(out=wt[:, :, 128:256], in_=wgv[:, :, D + fb * 128:D + (fb + 1) * 128]),
                                 reads=[r_scr["wbf"]], writes=[wres], dsem=wd)
                            wt2, wres2, wd2 = wb.nxt()
                            for t in range(2):
                                S.op("sp", (lambda t: lambda h: h.dma_start(out=wt2[:, t * 8:(t + 1) * 8, :], in_=wbv[t][:, :, fb * 128:(fb + 1) * 128]))(t),
                                     reads=[r_scr["wbf"]], writes=[wres2], dsem=wd2)
                            return wt, wres, wt2, wres2
                        return f
                    gpre = Pre([gthunk(fb) for fb in range(16)], 1)

                    def gate_fb(fb):
                        wt, wres, wt2, wres2 = gpre.get(fb)
                        gts = []
                        for t in range(2):
                            bk = t

                            def gf(h, bk=bk, t=t):
                                for kc in range(16):
                                    ins = h.matmul(PB[bk][:, :n], wt[:, kc, t * 128:(t + 1) * 128], hT[:, kc, :], start=(kc == 0), stop=(kc == 15))
                                return ins
                            S.op("pe", gf, reads=[rhT, wres], writes=[RB[bk]])
                            g_, rg_, _ = gs.nxt()
                            S.op("act", (lambda g_, bk: lambda h: h.activation(out=g_[:], in_=PB[bk][:, :n], func=AF.Sigmoid))(g_, bk), reads=[RB[bk]],
                                 writes=[rg_])
                            gts.append((g_, rg_))
                        for t in range(2):
                            bk = 2 + t

                            def bf_(h, bk=bk, t=t):
                                for kc in range(8):
                                    ins = h.matmul(PB[bk][:, :n], wt2[:, t * 8 + kc, :], oT[:, t * 8 + kc, :], start=(kc == 0), stop=(kc == 7))
                                return ins
                            S.op("pe", bf_, reads=[roT, wres2], writes=[RB[bk]])
                        g0, rg0 = gts[0]
                        g1, rg1 = gts[1]
                        S.op("dve", lambda h: h.tensor_tensor(out=g0[:], in0=PB[2][:, :n], in1=g0[:], op=ALU.mult), reads=[RB[2], rg0], writes=[rg0])
                        S.op("dve", lambda h: h.tensor_tensor(out=g1[:], in0=PB[3][:, :n], in1=g1[:], op=ALU.mult), reads=[RB[3], rg1], writes=[rg1])
                        S.op("pool", lambda h: h.tensor_tensor(out=mT[:, fb, :], in0=g0[:], in1=g1[:], op=ALU.add), reads=[rg0, rg1], writes=[rmT])
                    for fb in range(16):
                        gate_fb(fb)
                    wo = Ring(S, nc, st, "fa_wo", [128, 16, 512], BF16, 2)
                    tmpr = Ring(S, nc, st, "fa_tmp", [128, 512], F32, 2, with_dsem=False)
                    wov = wbf["out"].rearrange("(kc p) c -> p kc c", p=128)

                    def othunk(nbk):
                        def f():
                            wt, wres, wd = wo.nxt()
                            S.op("sp", lambda h: h.dma_start(out=wt[:], in_=wov[:, :, nbk * 512:(nbk + 1) * 512]),
                                 reads=[r_scr["wbf"]], writes=[wres], dsem=wd)
                            return wt, wres
                        return f
                    opre = Pre([othunk(k_) for k_ in range(4)], 1)

                    def out_blk(nbk, i, b0, rows, bk, wt, wres):
                        def of(h):
                            for kc in range(16):
                                ins = h.matmul(PB[bk][:rows, :], mT[:, kc, b0:b0 + rows], wt[:, kc, :], start=(kc == 0), stop=(kc == 15))
                            return ins
                        S.op("pe", of, reads=[rmT, wres], writes=[RB[bk]])
                        tt, rtt, _ = tmpr.nxt()
                        S.op("dve", lambda h: h.tensor_tensor(out=tt[:rows, :], in0=PB[bk][:rows, :], in1=gt1[:rows, nbk * 512:(nbk + 1) * 512], op=ALU.mult),
                             reads=[RB[bk], rgt1], writes=[rtt])
                        S.op("pool", lambda h: h.tensor_tensor(out=x1[i][:rows, nbk * 512:(nbk + 1) * 512], in0=x1[i][:rows, nbk * 512:(nbk + 1) * 512],
                                                               in1=tt[:rows, :], op=ALU.add), reads=[rtt, rx1[i]], writes=[rx1[i]])
                    k = 0
                    for nbk in range(4):
                        wt, wres = opre.get(nbk)
                        for i, (b0, rows) in enumerate(blks):
                            out_blk(nbk, i, b0, rows, 4 + (k % 2), wt, wres)
                            k += 1
                    S.flush()
                    for rg in (wg, wb, wo):
                        rg.release()
                    S.putd(dl)
                with ExitStack() as st:
                    dl = [S.getd() for _ in range(3)]
                    A2, B2, rA2, rB2 = load_mod(st, dl, r, 2)
                    nt = NormT(st, "nf")
                    for i, (b0, rows) in enumerate(blks):
                        nt.run(x1[i], rx1[i], rows, A2, B2, rA2, rB2, h2T, rh2, b0)
                    if slot_flag is not None and halo:
                        S.op("dve", lambda h: h.tensor_scalar(out=h2T[:, :, 0:halo], in0=h2T[:, :, 0:halo], scalar1=hflag[:, slot_flag:slot_flag + 1],
                                                              scalar2=None, op0=ALU.mult), reads=[rh2, r_const], writes=[rh2])
                    S.flush()
                    S.putd(dl)
                with ExitStack() as st:
                    dl = [S.getd() for _ in range(6)]
                    aT = PT("fb_aT", [128, NFF, n], BF16, st)
                    raT = Res()
                    if halo:
                        S.op("pool", lambda h: h.memset(aT[:, :, 0:halo], 0.0), writes=[raT])
                    gt2, rgt2 = load_row_rep(st, dl[0], "fb_gt2", modrow[r:r + 1, 5 * D:6 * D])
                    gfin, rgfin = load_row_rep(st, dl[1], "fb_gf", g_final[0:1, :])
                    ne = nout + 2
                    E = Ring(S, nc, st, "fb_E", [128, ne], F32, 2, with_dsem=False)
                    tr_ = Ring(S, nc, st, "fb_t", [128, nout], F32, 2, with_dsem=False)
                    wu = Ring(S, nc, st, "fb_wu", [128, 16, 256], BF16, 3)
                    wuv = wbf["up"].rearrange("(kc p) c -> p kc c", p=128)
                    prevt = None
                    rprev = Res()
                    if prev_src is not None:
                        prevt = PT("fb_prev", [128, NFF, 2], F32, st)
                        S.op("sp", lambda h: h.dma_start(out=prevt[:], in_=prev_src), writes=[rprev], dsem=dl[2])

                    def uthunk(fb):
                        def f():
                            wt, wres, wd = wu.nxt()
                            S.op("sp", lambda h: h.dma_start(out=wt[:, :, 0:128], in_=wuv[:, :, fb * 128:(fb + 1) * 128]),
                                 reads=[r_scr["wbf"]], writes=[wres], dsem=wd)
                            S.op("sp", lambda h: h.dma_start(out=wt[:, :, 128:256], in_=wuv[:, :, DFF + fb * 128:DFF + (fb + 1) * 128]),
                                 reads=[r_scr["wbf"]], writes=[wres], dsem=wd)
                            return wt, wres
                        return f
                    upre = Pre([uthunk(fb) for fb in range(NFF)], 2)

                    def up_fb(fb):
                        wt, wres = upre.get(fb)
                        for t in range(2):
                            bk = (fb % 2) * 2 + t

                            def uf(h, bk=bk, t=t):
                                for kc in range(16):
                                    ins = h.matmul(PB[bk][:, :n], wt[:, kc, t * 128:(t + 1) * 128], h2T[:, kc, :], start=(kc == 0), stop=(kc == 15))
                                return ins
                            S.op("pe", uf, reads=[rh2, wres], writes=[RB[bk]])
                        bg = (fb % 2) * 2
                        bv = bg + 1
                        Et, rE, _ = E.nxt()
                        tt, rtt, _ = tr_.nxt()
                        if prevt is None:
                            S.op("act", lambda h: h.activation(out=Et[:, :n], in_=PB[bg][:, :n], func=AF.Copy), reads=[RB[bg]], writes=[rE])
                        else:
                            S.op("act", lambda h: h.activation(out=Et[:, 2:2 + n], in_=PB[bg][:, :n], func=AF.Copy), reads=[RB[bg]], writes=[rE])
                            S.op("pool", lambda h: h.tensor_copy(out=Et[:, 0:2], in_=prevt[:, fb, :]), reads=[rprev, rE], writes=[rE])
                        S.op("dve", lambda h: h.tensor_scalar(out=tt[:], in0=Et[:, 2:2 + nout], scalar1=cwb[:, fb, 2:3], scalar2=cwb[:, fb, 3:4],
                                                              op0=ALU.mult, op1=ALU.add), reads=[rE, r_const], writes=[rtt])
                        S.op("dve", lambda h: h.scalar_tensor_tensor(out=tt[:], in0=Et[:, 1:1 + nout], scalar=cwb[:, fb, 1:2], in1=tt[:],
                                                                     op0=ALU.mult, op1=ALU.add), reads=[rE, rtt, r_const], writes=[rtt])
                        S.op("dve", lambda h: h.scalar_tensor_tensor(out=tt[:], in0=Et[:, 0:nout], scalar=cwb[:, fb, 0:1], in1=tt[:],
                                                                     op0=ALU.mult, op1=ALU.add), reads=[rE, rtt, r_const], writes=[rtt])
                        S.op("act", lambda h: h.activation(out=tt[:], in_=tt[:], func=AF.Silu), reads=[rtt], writes=[rtt])
                        S.op("dve", lambda h: h.tensor_tensor(out=aT[:, fb, halo:halo + nout], in0=PB[bv][:, halo:halo + nout], in1=tt[:], op=ALU.mult),
                             reads=[rtt, RB[bv]], writes=[raT])
                        if conv_dst is not None:
                            S.op("pool", lambda h: h.tensor_copy(out=convc[:, fb, :], in_=Et[:, conv_cols:conv_cols + 2]), reads=[rE], writes=[r_convc])
                    for fb in range(NFF):
                        up_fb(fb)
                    if conv_dst is not None:
                        S.op("sp", lambda h: h.dma_start(out=conv_dst, in_=convc[:]), reads=[r_convc], writes=[r_scr["out"]], dsem=dl[3])
                    wdr = Ring(S, nc, st, "fb_wd", [128, 11, 512], BF16, 3)
                    wdv = wbf["down"].rearrange("(kc p) c -> p kc c", p=128)
                    tmpr = Ring(S, nc, st, "fb_tmp", [128, 512], F32, 2, with_dsem=False)
                    assert nb <= 4

                    def dthunk(nbk, pc):
                        def f():
                            wt, wres, wd = wdr.nxt()
                            S.op("sp", lambda h: h.dma_start(out=wt[:], in_=wdv[:, pc * 11:(pc + 1) * 11, nbk * 512:(nbk + 1) * 512]),
                                 reads=[r_scr["wbf"]], writes=[wres], dsem=wd)
                            return wt, wres
                        return f
                    dpre = Pre([dthunk(nbk, pc) for nbk in range(4) for pc in range(4)], 2)

                    def down_piece(nbk, pc, wt, wres):
                        for i, (b0, rows) in enumerate(blks):
                            bk = 4 + i

                            def df(h, bk=bk, b0=b0, rows=rows):
                                for kc in range(11):
                                    ins = h.matmul(PB[bk][:rows, :], aT[:, pc * 11 + kc, b0:b0 + rows], wt[:, kc, :], start=(pc == 0 and kc == 0),
                                                   stop=(pc == 3 and kc == 10))
                                return ins
                            S.op("pe", df, reads=[raT, wres], writes=[RB[bk]])

                    def down_evac(nbk, i, b0, rows):
                        bk = 4 + i
                        tt, rtt, _ = tmpr.nxt()
                        S.op("dve", lambda h: h.tensor_tensor(out=tt[:rows, :], in0=PB[bk][:rows, :], in1=gt2[:rows, nbk * 512:(nbk + 1) * 512], op=ALU.mult),
                             reads=[RB[bk], rgt2], writes=[rtt])
                        S.op("pool", lambda h: h.tensor_tensor(out=x1[i][:rows, nbk * 512:(nbk + 1) * 512], in0=x1[i][:rows, nbk * 512:(nbk + 1) * 512],
                                                               in1=tt[:rows, :], op=ALU.add), reads=[rtt, rx1[i]], writes=[rx1[i]])
                    for nbk in range(4):
                        for pc in range(4):
                            wt, wres = dpre.get(nbk * 4 + pc)
                            down_piece(nbk, pc, wt, wres)
                        for i, (b0, rows) in enumerate(blks):
                            down_evac(nbk, i, b0, rows)
                    junk = PT("fb_junk", [128, D], BF16, st)
                    sm = PT("fb_sm", [128, 8], F32, st)
                    rj, rsm = Res(), Res()

                    def fin(i, b0, rows):
                        xt = x1[i]
                        S.op("dve", lambda h: h.scalar_tensor_tensor(out=junk[:rows], in0=xt[:rows], scalar=1.0, in1=xt[:rows], op0=ALU.mult,
                                                                     op1=ALU.mult, accum_out=sm[:rows, 0:1]), reads=[rx1[i]], writes=[rj, rsm])
                        S.op("dve", lambda h: h.tensor_scalar(out=sm[:rows, 1:2], in0=sm[:rows, 0:1], scalar1=1.0 / D, scalar2=EPS, op0=ALU.mult,
                                                              op1=ALU.add), reads=[rsm], writes=[rsm])
                        S.op("act", lambda h: h.activation(out=sm[:rows, 2:3], in_=sm[:rows, 1:2], func=AF.Ln), reads=[rsm], writes=[rsm])
                        S.op("act", lambda h: h.activation(out=sm[:rows, 3:4], in_=sm[:rows, 2:3], func=AF.Exp, scale=-0.5), reads=[rsm], writes=[rsm])
                        S.op("dve", lambda h: h.scalar_tensor_tensor(out=xt[:rows], in0=xt[:rows], scalar=sm[:rows, 3:4], in1=gfin[:rows],
                                                                     op0=ALU.mult, op1=ALU.mult), reads=[rx1[i], rsm, rgfin], writes=[rx1[i]])
                        lo_ = max(b0, halo)
                        hi_ = min(b0 + rows, halo + nout)
                        if hi_ > lo_:
                            S.op("sp", lambda h: h.dma_start(out=ydst[lo_ - halo:hi_ - halo, :], in_=xt[lo_ - b0:hi_ - b0, :]),
                                 reads=[rx1[i]], writes=[r_scr["out"]], dsem=dl[4])
                    for i, (b0, rows) in enumerate(blks):
                        fin(i, b0, rows)
                    S.flush()
                    for rg in (wu, wdr):
                        rg.release()
                    S.putd(dl)

        import os as _os
        stop = int(_os.environ.get("MK_STOP", "1000"))
        stepc = [0]

        def go():
            stepc[0] += 1
            return stepc[0] <= stop
        if go():
            setup()
        for s in range(NS):
            if go():
                cache_import(s)
            if go():
                phaseA(ctx_s[s], xs[s], DSEQ, PAST, okv_s[s], 1 + s, DSEQ)
        if go():
            phaseA(ctx_p, xp, SEQ, 0, okv_p, 0, cfg.TT)

        kb_s = [(i * 128, 128) for i in range(PAST // 128)] + [(PAST, DSEQ)]
        for s in range(NS):
            if go():
                win_q(xs[s], DSEQ, 1 + s)
            if go():
                win_attn(ctx_s[s], DSEQ, kb_s, None, strip_s, cfg.ULs,
                         u0_of=lambda kbi: cfg.U0s - (128 * kbi - PAST),
                         near_of=lambda kbi: (128 * kbi - PAST) >= -NEAR - 127,
                         sbmask_of=lambda kbi: (sbmask_s_in[0:DSEQ, :] if kbi == PAST // 128 else None),
                         idxmask_of=lambda qi, kt: None)
            if go():
                win_ffn(xs[s], DSEQ, 1 + s, 0, y_s[s], DSEQ, cst[s], sconv[s], DSEQ, None)
        for m in range(NSLOT):
            kb_p = [(i * 128, min(128, cfg.kext[m] - i * 128)) for i in range(cfg.kextb[m])]
            if go():
                win_q(xw[m], ncols, 0)
            if go():
                win_attn(ctx_p, ncols, kb_p, m, strip_p, cfg.UL,
                         u0_of=(lambda m: lambda kbi: cfg.U0 - (128 * kbi - STRIDE * G * m + 2))(m),
                         near_of=(lambda m: lambda kbi: cfg.near(m, kbi))(m),
                         sbmask_of=(lambda m: lambda kbi: (sbmask_in[cfg.sbm_index[(m, kbi)], 0:kb_p_rows(cfg, m, kbi), :] if (m, kbi) in cfg.sbm_index else None))(m),
                         idxmask_of=(lambda m: lambda qi, kt: (idxmask_in[cfg.im_index[(m, qi, kt)]] if (m, qi, kt) in cfg.im_index else None))(m))
            last = (m == NSLOT - 1)
            ccol = (SEQ - 2) - (STRIDE * (G * m + G - 1) - 2)
            if go():
                win_ffn(xw[m], ncols, 0, 2, y_p[m], STRIDE, None, pconv if last else None, ccol, m)
        S.barrier()
        S.flush()
    return nc


def kb_p_rows(cfg, m, kbi):
    return min(128, cfg.kext[m] - kbi * 128)


def rel_bucket_np(rel):
    rel = np.asarray(rel, np.int64)
    nb = 16
    max_exact = 8
    ret = np.where(rel > 0, nb, 0)
    n = np.abs(rel)
    nf = np.maximum(n, 1).astype(np.float32)
    large = max_exact + (np.log(nf / np.float32(max_exact)) / np.float32(np.log(1024 / max_exact)) * np.float32(nb - max_exact)).astype(np.int32)
    large = np.minimum(large, nb - 1)
    return ret + np.where(n < max_exact, n, large)


def prep_cfg_tables(cfg):
    cfg.sbm_index = {}
    cfg.im_index = {}
    for m in range(cfg.NSLOT):
        for kb in range(cfg.kextb[m]):
            if cfg.sbmasked(m, kb):
                cfg.sbm_index[(m, kb)] = len(cfg.sbm_index)
        nqb = len(blocks_of(cfg.ncols))
        nkt = (cfg.kext[m] + 511) // 512
        for qi in range(nqb):
            for kt in range(nkt):
                if cfg.idxmasked(m, kt):
                    cfg.im_index[(m, qi, kt)] = len(cfg.im_index)
    cfg.n_sbm = max(1, len(cfg.sbm_index))
    cfg.n_im = max(1, len(cfg.im_index))


def core_tables(cfg, j):
    STRIDE, G, ncols = cfg.STRIDE, cfg.G, cfg.ncols
    bf = ml_dtypes.bfloat16
    sbm = np.zeros((cfg.n_sbm, 128, ncols), np.float32)
    for (m, kb), ix in cfg.sbm_index.items():
        qpos = STRIDE * (G * m + j) - 2 + np.arange(ncols)
        kpos = 128 * kb + np.arange(128)
        sbm[ix] = np.where(kpos[:, None] >= qpos[None, :], MASKV, 0.0)
    im = np.zeros((cfg.n_im, 128, 512), np.float32)
    qb = blocks_of(ncols)
    for (m, qi, kt), ix in cfg.im_index.items():
        q0, qrows = qb[qi]
        qpos = STRIDE * (G * m + j) - 2 + q0 + np.arange(128)
        lim = (qpos // 64 + 1) * 64
        kpos = 512 * kt + np.arange(512)
        im[ix] = np.where(kpos[None, :] >= lim[:, None], IMASKV, 0.0)
    i = np.arange(cfg.GL)
    r = (cfg.U0 + 127) - i - STRIDE * j
    b = rel_bucket_np(r)
    ohp = np.zeros((32, cfg.GL), np.float32)
    ohp[b, i] = 1.0
    i = np.arange(cfg.GLs)
    r = (cfg.U0s + 127) - i
    b = rel_bucket_np(r)
    ohs = np.zeros((32, cfg.GLs), np.float32)
    ohs[b, i] = 1.0
    sbs = np.zeros((128, DSEQ), np.float32)
    sbs[:DSEQ] = np.where(np.arange(DSEQ)[:, None] >= np.arange(DSEQ)[None, :], MASKV, 0.0)
    hflag = np.ones((128, cfg.NSLOT), np.float32)
    if j == 0:
        hflag[:, 0] = 0.0
    return {"sbmask": sbm.astype(bf), "idxmask": im.astype(bf), "oh_p": ohp, "oh_s": ohs, "sbmask_s": sbs.astype(bf), "hflag": hflag}


_NC_CACHE = {}


def run_cfg(cfg, inp):
    prep_cfg_tables(cfg)
    import os as _os
    key = (cfg.SEQ, cfg.NB, cfg.G, cfg.NSLOT, cfg.STRIDE, cfg.NS, cfg.TT, _os.environ.get('MK_STOP'), _os.environ.get('MK_QSKIP'))
    if key not in _NC_CACHE:
        _NC_CACHE[key] = build(cfg)
    nc = _NC_CACHE[key]
    SEQ, G, NSLOT, STRIDE, NS, ncols = cfg.SEQ, cfg.G, cfg.NSLOT, cfg.STRIDE, cfg.NS, cfg.ncols
    f32 = np.float32
    ident = np.eye(128, dtype=f32)
    tri = (np.arange(128)[:, None] >= np.arange(128)[None, :]).astype(f32)
    constf = np.stack([ident, -ident, tri, np.ones((128, 128), f32)], axis=1)
    cwb = np.concatenate([inp["conv_w"][0], inp["conv_b"][0][None]], axis=0)
    cwb = np.ascontiguousarray(cwb.reshape(4, NFF, 128).transpose(2, 1, 0))
    shared = {
        "w_ada": inp["w_ada"][0], "w_in": inp["w_in"][0], "w_gate": inp["w_gate"][0], "w_br_sb": inp["w_br_sb"][0],
        "w_br_sa": inp["w_br_sa"][0], "w_out": inp["w_out"][0], "w_up": inp["w_up"][0], "w_down": inp["w_down"][0],
        "g_mix": inp["g_mix"][0][None], "g_ffn": inp["g_ffn"][0][None], "g_final": inp["g_final"][None],
        "rel_table": inp["rel_table"], "cwb": cwb, "constf": constf,
        "wix": np.ascontiguousarray(inp["w_in"][0][:, C_WIX:C_WIX + 16].reshape(16, 128, 16).transpose(1, 0, 2)),
    }
    tabs = [core_tables(cfg, j) for j in range(G)]
    in_maps = []
    tot = STRIDE * G * NSLOT
    for c in range(cfg.ncores):
        b, j = c // G, c % G
        xpad = np.zeros((2 + max(tot, SEQ) + 2, D), f32)
        xpad[2:2 + SEQ] = inp["x_prompt"][b]
        xw = np.stack([xpad[STRIDE * (G * m + j):STRIDE * (G * m + j) + ncols] for m in range(NSLOT)])
        ss = slice(NS * c, NS * (c + 1))
        cvec = np.concatenate([inp["c_prompt"][b:b + 1], inp["c_sample"][ss]], axis=0)
        cT = np.ascontiguousarray(cvec.reshape(cfg.NR, 16, 128).transpose(2, 1, 0))
        st = inp["state_ffn_conv"][0, ss]
        cst = np.ascontiguousarray(st.reshape(NS, 2, NFF, 128).transpose(0, 3, 2, 1))
        m_ = dict(shared)
        m_.update(tabs[j])
        m_.update({
            "xp": inp["x_prompt"][b], "xw": xw, "xs": inp["x_sample"][ss],
            "c_sb_k": inp["cache_sb_k"][0, ss].reshape(NS, PAST, 1024), "c_sb_v": inp["cache_sb_v"][0, ss].reshape(NS, PAST, 1024),
            "c_sa_k": inp["cache_sa_k"][0, ss].reshape(NS, PAST, 1024), "c_sa_v": inp["cache_sa_v"][0, ss].reshape(NS, PAST, 1024),
            "c_ix": inp["cache_idx_k"][0, ss], "c_st": cst, "cT": cT,
            "b_ada_rows": np.repeat(inp["b_ada"][0][None], cfg.NR, axis=0),
        })
        in_maps.append({k: np.ascontiguousarray(v) for k, v in m_.items()})
    res = run_bass_kernel_spmd(nc, in_maps, core_ids=list(range(cfg.ncores))).results
    NB = cfg.NB
    y_prompt = np.zeros((NB, SEQ, D), f32)
    okv = np.zeros((NB, SEQ, KVC), f32)
    p_conv = np.zeros((1, NB, 2, DFF), f32)
    for c in range(cfg.ncores):
        b, j = c // G, c % G
        for m in range(NSLOT):
            p0 = STRIDE * (G * m + j)
            nv = min(STRIDE, SEQ - p0)
            if nv > 0:
                y_prompt[b, p0:p0 + nv] = res[c]["y_p"][m, :nv]
        q0, q1 = SEQ * j // G, SEQ * (j + 1) // G
        okv[b, q0:q1] = res[c]["okv_p"][q0:q1]
        if j == G - 1:
            p_conv[0, b] = res[c]["pconv"].transpose(2, 1, 0).reshape(2, DFF)
    nsb = cfg.ncores * NS
    y_sample = np.concatenate([res[c]["y_s"] for c in range(cfg.ncores)], axis=0)
    okvs = np.concatenate([res[c]["okv_s"] for c in range(cfg.ncores)], axis=0)
    s_conv = np.concatenate([res[c]["sconv"].transpose(0, 3, 2, 1).reshape(NS, 2, DFF) for c in range(cfg.ncores)], axis=0)[None]

    def split(o, L):
        nb_ = o.shape[0]
        return (o[..., 0:1024].reshape(1, nb_, L, NH, 128), o[..., 1024:2048].reshape(1, nb_, L, NH, 128),
                o[..., 2048:3072].reshape(1, nb_, L, NH, 128), o[..., 3072:4096].reshape(1, nb_, L, NH, 128),
                o[..., 4096:4160].reshape(1, nb_, L, 64))
    pk = split(okv, SEQ)
    sk = split(okvs, DSEQ)
    return (y_prompt, y_sample, pk[0], pk[1], pk[2], pk[3], pk[4], p_conv,
            sk[0], sk[1], sk[2], sk[3], sk[4], s_conv)


def kernel(**inputs):
    inp = {k: np.asarray(v) for k, v in inputs.items()}
    cfg = Cfg()
    out = run_cfg(cfg, inp)
    return tuple(np.ascontiguousarray(o, dtype=np.float32) for o in out)
```

```python
import numpy as np
import ml_dtypes
from contextlib import ExitStack
import concourse.bass as bass
import concourse.mybir as mybir
from concourse.bass_utils import run_bass_kernel_spmd

F32 = mybir.dt.float32
BF16 = mybir.dt.bfloat16
FP8 = mybir.dt.float8e4
AF = mybir.ActivationFunctionType
ALU = mybir.AluOpType
AX = mybir.AxisListType

D = 2048
DFF = 5632
NFF = DFF // 128
NH = 8
NHI = 16
INC = 7248
KVC = 4160
C_QSB, C_KSB, C_VSB, C_QSA, C_KSA, C_VSA, C_QIX, C_KIX, C_WIX = 0, 1024, 2048, 3072, 4096, 5120, 6144, 7168, 7232
EPS = 1e-6
PAST = 2048
DSEQ = 64
LKS = PAST + DSEQ
TOPK = 256
NBIS = 18
BRANGE = 32.0
NEAR = 640
SCALE = 128 ** -0.5
MASKV = -30000.0
IMASKV = -1024.0


class Cfg:
    def __init__(self, SEQ=16384, NB=2, G=4, NSLOT=9, STRIDE=456, NS=4, TT=1024):
        self.SEQ, self.NB, self.G, self.NSLOT, self.STRIDE, self.NS, self.TT = SEQ, NB, G, NSLOT, STRIDE, NS, TT
        self.ncols = STRIDE + 2
        self.ncores = NB * G
        self.NR = 1 + NS
        self.kext = [min(SEQ, STRIDE * (G * m + G)) for m in range(NSLOT)]
        self.kextb = [(k + 127) // 128 for k in self.kext]
        dmax = max(128 * (self.kextb[m] - 1) - STRIDE * G * m + 2 for m in range(NSLOT))
        self.U0 = dmax
        self.dmin = -NEAR - 127 - 128
        self.UL = self.U0 - self.dmin + self.ncols
        self.GL = self.UL + 128
        self.U0s = 128 * 16 - PAST
        self.dmins = -NEAR - 127 - 128
        self.ULs = self.U0s - self.dmins + DSEQ
        self.GLs = self.ULs + 128

    def near(self, m, kb):
        return 128 * kb - self.STRIDE * self.G * m + 2 >= -NEAR - 127

    def sbmasked(self, m, kb):
        return 128 * kb + 127 >= self.STRIDE * self.G * m - 2

    def idxmasked(self, m, kt):
        lim = ((self.STRIDE * self.G * m - 2) // 64 + 1) * 64
        return 512 * kt + 511 >= lim


def blocks_of(n):
    out = []
    o = 0
    while o < n:
        out.append((o, min(128, n - o)))
        o += 128
    return out


class Res:
    __slots__ = ("w", "r", "excl")

    def __init__(self, excl=False):
        self.w = None
        self.r = []
        self.excl = excl


class DSem:
    def __init__(self, sem):
        self.sem = sem
        self.count = 0


class Sched:
    ENGS = ("pe", "act", "dve", "pool", "sp")

    def __init__(self, nc, stack, ndsem):
        self.nc = nc
        self.ops = {e: [] for e in self.ENGS}
        self.sem = {}
        self.count = {e: 0 for e in self.ENGS}
        self.known = {e: {} for e in self.ENGS}
        for e in ("pe", "act", "dve", "pool"):
            self.sem[e] = stack.enter_context(nc.semaphore("sem_" + e))
        self.dsems = [DSem(stack.enter_context(nc.semaphore("dsem%d" % i))) for i in range(ndsem)]
        self.free = list(self.dsems[:-12])
        self.free_sw = list(self.dsems[-12:])
        self.dmap = {id(d.sem): d for d in self.dsems}

    def getd(self, sw=False):
        return self.free_sw.pop() if sw else self.free.pop()

    def putd(self, ds, sw=False):
        (self.free_sw if sw else self.free).extend(ds)

    def _deps(self, eng, reads, writes):
        deps = {}

        def add(tok):
            if tok is None:
                return
            key = id(tok[0])
            d = self.dmap.get(key)
            if d is not None:
                tok = (tok[0], d.count)
            if key not in deps or deps[key][1] < tok[1]:
                deps[key] = tok
        for r in reads:
            add(r.w)
            if r.excl:
                for t in r.r:
                    add(t)
        for w in writes:
            add(w.w)
            for t in w.r:
                add(t)
        out = []
        kn = self.known[eng]
        for key, tok in deps.items():
            if kn.get(key, 0) >= tok[1]:
                continue
            kn[key] = tok[1]
            out.append(tok)
        return out

    def op(self, eng, fn, reads=(), writes=(), dsem=None):
        waits = self._deps(eng, reads, writes)
        if dsem is not None and dsem.count > 0:
            key = id(dsem.sem)
            if self.known[eng].get(key, 0) < dsem.count:
                self.known[eng][key] = dsem.count
                waits = [w for w in waits if id(w[0]) != key] + [(dsem.sem, dsem.count)]
        if dsem is None:
            self.count[eng] += 1
            tok = (self.sem[eng], self.count[eng])
            inc = (self.sem[eng], 1)
        else:
            dsem.count += 16
            tok = (dsem.sem, dsem.count)
            inc = (dsem.sem, 16)
        self.ops[eng].append((waits, fn, inc))
        for r in reads:
            if len(r.r) > 24:
                r.r = r.r[-24:]
            r.r.append(tok)
        for w in writes:
            w.w = tok
            w.r = []
        return tok

    def barrier(self):
        toks = [(self.sem[e], self.count[e]) for e in ("pe", "act", "dve", "pool") if self.count[e] > 0]
        toks += [(d.sem, d.count) for d in self.dsems if d.count > 0]
        for e in self.ENGS:
            kn = self.known[e]
            waits = []
            for tok in toks:
                key = id(tok[0])
                if kn.get(key, 0) >= tok[1]:
                    continue
                kn[key] = tok[1]
                waits.append(tok)
            if waits:
                self.ops[e].append((waits, None, None))

    def flush(self):
        self.barrier()
        ops = self.ops
        self.ops = {e: [] for e in self.ENGS}

        def mk(e):
            def body(h):
                for waits, fn, inc in ops[e]:
                    for (s, v) in waits:
                        h.wait_ge(s, v)
                    if fn is not None:
                        inst = fn(h)
                        inst.then_inc(inc[0], inc[1])
            return body
        with self.nc.Block() as block:
            block.tensor(mk("pe"))
            block.scalar(mk("act"))
            block.vector(mk("dve"))
            block.gpsimd(mk("pool"))
            block.sync(mk("sp"))


class Ring:
    uid = 0

    def __init__(self, S, nc, st, name, shape, dt, n, with_dsem=True, sw_dsem=False):
        self.S = S
        self.d2 = [S.getd(sw=True) for _ in range(n)] if sw_dsem else None
        Ring.uid += 1
        self.t = [st.enter_context(nc.sbuf_tensor("%s%d_r%d" % (name, i, Ring.uid), shape, dt)) for i in range(n)]
        self.r = [Res() for _ in range(n)]
        self.d = [S.getd() for _ in range(n)] if with_dsem else None
        self.n = n
        self.i = 0

    def nxt(self):
        i = self.i % self.n
        self.i += 1
        return self.t[i], self.r[i], (self.d[i] if self.d else None)

    def release(self):
        if self.d:
            self.S.putd(self.d)
        if self.d2:
            self.S.putd(self.d2, sw=True)


class Pre:
    def __init__(self, thunks, depth):
        self.th = thunks
        self.depth = depth
        self.nxt_ = 0
        self.got = {}

    def get(self, i):
        while self.nxt_ < len(self.th) and self.nxt_ <= i + self.depth:
            self.got[self.nxt_] = self.th[self.nxt_]()
            self.nxt_ += 1
        return self.got.pop(i)


def build(cfg):
    nc = bass.Bass("TRN2", target_bir_lowering=False)
    SEQ, G, NSLOT, STRIDE, NS, NR, ncols = cfg.SEQ, cfg.G, cfg.NSLOT, cfg.STRIDE, cfg.NS, cfg.NR, cfg.ncols

    def din(name, shape, dt=F32):
        return nc.dram_tensor(name, list(shape), dt, kind="ExternalInput").ap()

    def dout(name, shape, dt=F32):
        return nc.dram_tensor(name, list(shape), dt, kind="ExternalOutput").ap()

    def dscr(name, shape, dt=BF16):
        return nc.dram_tensor(name, list(shape), dt, kind="Internal").ap()

    xp = din("xp", [SEQ, D])
    xw = din("xw", [NSLOT, ncols, D])
    xs = din("xs", [NS, DSEQ, D])
    cache = [din(n, [NS, PAST, 1024]) for n in ("c_sb_k", "c_sb_v", "c_sa_k", "c_sa_v")]
    cix = din("c_ix", [NS, PAST, 64])
    cst = din("c_st", [NS, 128, NFF, 2])
    cT = din("cT", [128, 16, NR])
    w_f32 = {
        "ada": din("w_ada", [D, 6 * D]), "in": din("w_in", [D, INC]), "gate": din("w_gate", [D, 2 * D]),
        "brsb": din("w_br_sb", [1024, D]), "brsa": din("w_br_sa", [1024, D]), "out": din("w_out", [D, D]),
        "up": din("w_up", [D, 2 * DFF]), "down": din("w_down", [DFF, D]),
    }
    b_ada_rows = din("b_ada_rows", [NR, 6 * D])
    g_mix = din("g_mix", [1, D])
    g_ffn = din("g_ffn", [1, D])
    g_final = din("g_final", [1, D])
    rel_table = din("rel_table", [32, 8])
    cwb_in = din("cwb", [128, NFF, 4])
    constf = din("constf", [128, 4, 128])
    sbmask_in = din("sbmask", [cfg.n_sbm, 128, ncols], BF16)
    idxmask_in = din("idxmask", [cfg.n_im, 128, 512], BF16)
    sbmask_s_in = din("sbmask_s", [128, DSEQ], BF16)
    oh_p = din("oh_p", [32, cfg.GL])
    oh_s = din("oh_s", [32, cfg.GLs])
    hflag_in = din("hflag", [128, NSLOT])
    wix_in = din("wix", [128, 16, 16])

    y_p = dout("y_p", [NSLOT, STRIDE, D])
    y_s = dout("y_s", [NS, DSEQ, D])
    okv_p = dout("okv_p", [SEQ, KVC])
    okv_s = dout("okv_s", [NS, DSEQ, KVC])
    pconv = dout("pconv", [128, NFF, 2])
    sconv = dout("sconv", [NS, 128, NFF, 2])

    wbf = {k: dscr("wbf_" + k, v.shape) for k, v in w_f32.items()}
    modrow = dscr("modrow", [NR, 6 * D], F32)

    def mkctx(name, Lk):
        return {"KT": [dscr(name + "_ktsb", [NH, 128, Lk]), dscr(name + "_ktsa", [NH, 128, Lk])],
                "V": [dscr(name + "_vsb", [Lk, 1024]), dscr(name + "_vsa", [Lk, 1024])],
                "KI2": dscr(name + "_ki2", [128, Lk]), "Lk": Lk}
    ctx_p = mkctx("cp", SEQ)
    ctx_s = [mkctx("cs%d" % s, LKS) for s in range(NS)]
    gvec_p = dscr("gvec_p", [8, cfg.GL])
    gvec_s = dscr("gvec_s", [8, cfg.GLs])
    strip_p = dscr("strip_p", [8, 128, cfg.UL])
    strip_s = dscr("strip_s", [8, 128, cfg.ULs])
    QTs = dscr("QTs", [4, 8, 128, ncols])
    hTs = dscr("hTs", [16, 128, ncols])
    OTs = dscr("OTs", [2, 8, 128, ncols])

    with ExitStack() as gst:
        S = Sched(nc, gst, 80)

        uid = [0]

        def PT(name, shape, dt=F32, st=gst):
            uid[0] += 1
            return st.enter_context(nc.sbuf_tensor("%s_u%d" % (name, uid[0]), list(shape), dt))

        identf = PT("identf", [128, 128])
        cbf = PT("cbf", [128, 4, 128], BF16)
        cwb = PT("cwb_t", [128, NFF, 4])
        CH = PT("CH", [128, 8])
        hflag = PT("hflag_t", [128, NSLOT])
        wraw = PT("wraw", [128, 4, NHI])
        convc = PT("convc", [128, NFF, 2])
        r_const = Res()
        r_wraw = Res()
        r_convc = Res()
        identb = cbf[:, 0, :]
        nidentb = cbf[:, 1, :]
        trib = cbf[:, 2, :]
        onesb = cbf[:, 3, :]
        PB = [gst.enter_context(nc.psum_tensor("pb%d" % i, [128, 512], F32)) for i in range(8)]
        RB = [Res(excl=True) for _ in range(8)]
        r_scr = {"modrow": Res(), "wbf": Res(), "strip": Res(), "QT": Res(), "hT": Res(), "OT": Res(), "ctx": Res(),
                 "out": Res()}

        def setup():
            with ExitStack() as st:
                dl = [S.getd() for _ in range(6)]
                dsw = S.getd(sw=True)
                for k in ("in", "ada", "gate", "brsb", "brsa", "out", "up", "down"):
                    src = w_f32[k]
                    n = src.shape[0] * src.shape[1] // 2048
                    s2 = src.rearrange("r c -> (r c)").rearrange("(n k) -> n k", k=2048)
                    d2 = wbf[k].rearrange("r c -> (r c)").rearrange("(n k) -> n k", k=2048)
                    o = 0
                    while o < n:
                        m = min(4096, n - o)
                        S.op("pool", (lambda a, b: lambda h: h.dma_start(out=a, in_=b))(d2[o:o + m], s2[o:o + m]),
                             writes=[Res()], dsem=dsw)
                        o += m
                ctmp = PT("ctmp", [128, 4, 128], F32, st)
                rt = Res()
                S.op("sp", lambda h: h.dma_start(out=ctmp[:], in_=constf), writes=[rt], dsem=dl[1])
                S.op("sp", lambda h: h.dma_start(out=identf[:], in_=constf[:, 0, :]), writes=[r_const], dsem=dl[2])
                S.op("sp", lambda h: h.dma_start(out=cwb[:], in_=cwb_in), writes=[r_const], dsem=dl[2])
                S.op("sp", lambda h: h.dma_start(out=hflag[:], in_=hflag_in), writes=[r_const], dsem=dl[2])
                S.op("sp", lambda h: h.dma_start(out=CH[:], in_=rel_table[15:16, :].to_broadcast([128, 8])),
                     writes=[r_const], dsem=dl[2])
                S.op("dve", lambda h: h.tensor_copy(out=cbf[:], in_=ctmp[:]), reads=[rt], writes=[r_const])
                cTt = PT("cTt", [128, 16, NR], F32, st)
                sT = PT("sT", [128, 16, NR], BF16, st)
                rc, rs = Res(), Res()
                S.op("sp", lambda h: h.dma_start(out=cTt[:], in_=cT), writes=[rc], dsem=dl[1])
                S.op("act", lambda h: h.activation(out=sT[:], in_=cTt[:], func=AF.Silu), reads=[rc], writes=[rs])
                relt = PT("relt", [32, 8], F32, st)
                rr = Res()
                S.op("sp", lambda h: h.dma_start(out=relt[:], in_=rel_table), writes=[rr], dsem=dl[1])
                def mkstrip(oh, gv, GLx, strip, ULx, nm):
                    oht = PT("oht" + nm, [32, GLx], F32, st)
                    gst_t = PT("gst" + nm, [8, GLx], BF16, st)
                    ro, rg, rgv = Res(), Res(), Res()
                    S.op("sp", (lambda a, b: lambda h: h.dma_start(out=a, in_=b))(oht[:], oh), writes=[ro], dsem=dl[1])
                    o = 0
                    i = 0
                    while o < GLx:
                        m = min(512, GLx - o)
                        bk = 6 + (i % 2)
                        S.op("pe", (lambda bk, o, m: lambda h: h.matmul(PB[bk][:8, :m], relt[:], oht[:, o:o + m],
                                                                         start=True, stop=True))(bk, o, m),
                             reads=[rr, ro], writes=[RB[bk]])
                        S.op("act", (lambda bk, o, m: lambda h: h.activation(out=gst_t[:, o:o + m], in_=PB[bk][:8, :m],
                                                                             func=AF.Copy))(bk, o, m),
                             reads=[RB[bk]], writes=[rg])
                        o += m
                        i += 1
                    S.op("sp", (lambda a, b: lambda h: h.dma_start(out=a, in_=b))(gv, gst_t[:]), reads=[rg], writes=[rgv],
                         dsem=dl[3])
                    for p in range(128):
                        S.op("sp", (lambda p, strip, gv, ULx: lambda h: h.dma_start(
                            out=strip[:, p, :], in_=gv[:, 127 - p:127 - p + ULx]))(p, strip, gv, ULx),
                            reads=[rgv], writes=[Res()], dsem=dl[4])
                mkstrip(oh_p, gvec_p, cfg.GL, strip_p, cfg.UL, "p")
                mkstrip(oh_s, gvec_s, cfg.GLs, strip_s, cfg.ULs, "s")
                S.flush()
                wr = Ring(S, nc, st, "wada", [128, 16, 512], BF16, 2)
                br = Ring(S, nc, st, "bada", [NR, 512], F32, 2)
                ms = Ring(S, nc, st, "mst", [NR, 512], F32, 2)
                wv = wbf["ada"].rearrange("(kc p) c -> p kc c", p=128)
                for pc in range(24):
                    wt, wres, wd = wr.nxt()
                    bt, bres, bd = br.nxt()
                    mt, mres, md = ms.nxt()
                    S.op("sp", (lambda wt, pc: lambda h: h.dma_start(out=wt[:], in_=wv[:, :, pc * 512:(pc + 1) * 512]))(wt, pc),
                         writes=[wres], dsem=wd)
                    S.op("sp", (lambda bt, pc: lambda h: h.dma_start(out=bt[:], in_=b_ada_rows[:, pc * 512:(pc + 1) * 512]))(bt, pc),
                         writes=[bres], dsem=bd)
                    bk = pc % 2

                    def mmf(h, wt=wt, bk=bk):
                        for kc in range(16):
                            ins = h.matmul(PB[bk][:NR, :], sT[:, kc, :], wt[:, kc, :], start=(kc == 0), stop=(kc == 15))
                        return ins
                    S.op("pe", mmf, reads=[rs, wres], writes=[RB[bk]])
                    S.op("dve", (lambda mt, bt, bk: lambda h: h.tensor_tensor(out=mt[:], in0=PB[bk][:NR, :], in1=bt[:],
                                                                             op=ALU.add))(mt, bt, bk),
                         reads=[RB[bk], bres], writes=[mres])
                    S.op("sp", (lambda mt, pc: lambda h: h.dma_start(out=modrow[:, pc * 512:(pc + 1) * 512], in_=mt[:]))(mt, pc),
                         reads=[mres], writes=[Res()], dsem=md)
                S.flush()
                wr.release(); br.release(); ms.release()
                S.putd(dl)
                S.putd([dsw], sw=True)

        def load_mod(st, dl, r, which):
            A = PT("modA", [128, D], F32, st)
            Bt = PT("modB", [128, D], F32, st)
            Gt = PT("modG", [128, D], F32, st)
            rA, rB, rG = Res(), Res(), Res()
            base = 0 if which == 1 else 3 * D
            g = g_mix if which == 1 else g_ffn
            S.op("sp", lambda h: h.dma_start(out=A[:], in_=modrow[r:r + 1, base + D:base + 2 * D].to_broadcast([128, D])),
                 writes=[rA], dsem=dl[0])
            S.op("sp", lambda h: h.dma_start(out=Bt[:], in_=modrow[r:r + 1, base:base + D].to_broadcast([128, D])),
                 writes=[rB], dsem=dl[1])
            S.op("sp", lambda h: h.dma_start(out=Gt[:], in_=g.to_broadcast([128, D])), writes=[rG], dsem=dl[2])
            S.op("dve", lambda h: h.scalar_tensor_tensor(out=A[:], in0=A[:], scalar=1.0, in1=Gt[:], op0=ALU.add, op1=ALU.mult),
                 reads=[rA, rG], writes=[rA])
            return A, Bt, rA, rB

        def load_row_rep(st, dl, name, src_row):
            t = PT(name, [128, D], F32, st)
            rr = Res()
            S.op("sp", lambda h: h.dma_start(out=t[:], in_=src_row.to_broadcast([128, D])), writes=[rr], dsem=dl)
            return t, rr

        class NormT:
            def __init__(self, st, name):
                self.junk = PT(name + "_junk", [128, D], BF16, st)
                self.hb = [PT(name + "_hb%d" % i, [128, D], F32, st) for i in range(2)]
                self.rhb = [Res(), Res()]
                self.sm = PT(name + "_sm", [128, 8], F32, st)
                self.rj = Res()
                self.rsm = Res()
                self.k = 0

            def run(self, xt, rx, rows, A, Bt, rA, rB, dst, rdst, c0):
                k = self.k
                self.k += 1
                hb, rhb = self.hb[k % 2], self.rhb[k % 2]
                sm, rsm, junk, rj = self.sm, self.rsm, self.junk, self.rj
                S.op("dve", lambda h: h.scalar_tensor_tensor(out=junk[:rows], in0=xt[:rows], scalar=1.0, in1=xt[:rows],
                                                             op0=ALU.mult, op1=ALU.mult, accum_out=sm[:rows, 0:1]),
                     reads=[rx], writes=[rj, rsm])
                S.op("dve", lambda h: h.tensor_scalar(out=sm[:rows, 1:2], in0=sm[:rows, 0:1], scalar1=1.0 / D, scalar2=EPS,
                                                      op0=ALU.mult, op1=ALU.add), reads=[rsm], writes=[rsm])
                S.op("act", lambda h: h.activation(out=sm[:rows, 2:3], in_=sm[:rows, 1:2], func=AF.Ln), reads=[rsm], writes=[rsm])
                S.op("act", lambda h: h.activation(out=sm[:rows, 3:4], in_=sm[:rows, 2:3], func=AF.Exp, scale=-0.5),
                     reads=[rsm], writes=[rsm])
                S.op("dve", lambda h: h.scalar_tensor_tensor(out=hb[:rows], in0=xt[:rows], scalar=sm[:rows, 3:4], in1=A[:rows],
                                                             op0=ALU.mult, op1=ALU.mult), reads=[rx, rsm, rA], writes=[rhb])
                S.op("pool", lambda h: h.tensor_tensor(out=hb[:rows], in0=hb[:rows], in1=Bt[:rows], op=ALU.add),
                     reads=[rhb, rB], writes=[rhb])
                for g4 in range(4):
                    bk = 6 + (g4 % 2)

                    def tp(h, g4=g4, bk=bk):
                        for q in range(4):
                            fc = g4 * 4 + q
                            ins = h.transpose(out=PB[bk][:, q * 128:q * 128 + rows], in_=hb[:rows, fc * 128:(fc + 1) * 128],
                                              identity=identf[:rows, :rows])
                        return ins
                    S.op("pe", tp, reads=[rhb, r_const], writes=[RB[bk]])
                    src = PB[bk][:, :].rearrange("p (q c) -> p q c", q=4)[:, :, 0:rows]
                    S.op("act", (lambda g4, src: lambda h: h.activation(out=dst[:, g4 * 4:(g4 + 1) * 4, c0:c0 + rows], in_=src,
                                                                        func=AF.Copy))(g4, src),
                         reads=[RB[bk]], writes=[rdst])

        KVBLK = [(C_KSB, 512, "k", 0, 0, 0), (C_KSB + 512, 512, "k", 0, 4, 512),
                 (C_VSB, 512, "v", 0, 0, 1024), (C_VSB + 512, 512, "v", 0, 4, 1536),
                 (C_KSA, 512, "k", 1, 0, 2048), (C_KSA + 512, 512, "k", 1, 4, 2560),
                 (C_VSA, 512, "v", 1, 0, 3072), (C_VSA + 512, 512, "v", 1, 4, 3584),
                 (C_KIX, 64, "i", 0, 0, 4096)]

        def phaseA(ctx, xsrc, ntok, tok0, okv, r, TT):
            with ExitStack() as st:
                dl = [S.getd() for _ in range(4)]
                A, Bt, rA, rB = load_mod(st, dl, r, 1)
                nt = NormT(st, "na")
                ncol_t = min(TT, ntok)
                hT = PT("a_hT", [128, 16, ncol_t], BF16, st)
                rhT = Res()
                xr = Ring(S, nc, st, "a_x", [128, D], F32, 3)
                wr = Ring(S, nc, st, "a_w", [128, 16, 512], BF16, 3)
                sf = Ring(S, nc, st, "a_sf", [128, 512], F32, 4, sw_dsem=True)
                kts = Ring(S, nc, st, "a_kt", [128, 4, ncol_t], BF16, 2)
                ki2 = Ring(S, nc, st, "a_ki2", [128, ncol_t], BF16, 2)
                kid = Ring(S, nc, st, "a_kid", [128, 128], F32, 2, with_dsem=False)
                win = wbf["in"].rearrange("(kc p) c -> p kc c", p=128)
                mmk = 0
                ntiles = (ntok + TT - 1) // TT

                def wthunk(wc0, wn):
                    def f():
                        wt, wres, wd = wr.nxt()
                        S.op("sp", lambda h: h.dma_start(out=wt[:, :, :wn], in_=win[:, :, wc0:wc0 + wn]),
                             reads=[r_scr["wbf"]], writes=[wres], dsem=wd)
                        return wt, wres
                    return f
                wpre = Pre([wthunk(b[0], b[1]) for _ in range(ntiles) for b in KVBLK], 1)
                wi = 0
                for t0 in range(0, ntok, TT):
                    n = min(TT, ntok - t0)
                    blks = blocks_of(n)
                    for (b0, rows) in blks:
                        xt, rx, xd = xr.nxt()
                        S.op("sp", (lambda xt, a, rows: lambda h: h.dma_start(out=xt[:rows], in_=a))(xt, xsrc[t0 + b0:t0 + b0 + rows, :], rows),
                             writes=[rx], dsem=xd)
                        nt.run(xt, rx, rows, A, Bt, rA, rB, hT, rhT, b0)
                    for (wc0, wn, kind, which, head0, oc0) in KVBLK:
                        wt, wres = wpre.get(wi)
                        wi += 1
                        if kind == "k":
                            kt, rkt, kd = kts.nxt()
                        if kind == "i":
                            k2, rk2, k2d = ki2.nxt()
                        for (b0, rows) in blks:
                            bk = mmk % 3
                            mmk += 1

                            def mmf(h, wt=wt, bk=bk, b0=b0, rows=rows, wn=wn):
                                for kc in range(16):
                                    ins = h.matmul(PB[bk][:rows, :wn], hT[:, kc, b0:b0 + rows], wt[:, kc, :wn],
                                                   start=(kc == 0), stop=(kc == 15))
                                return ins
                            S.op("pe", mmf, reads=[rhT, wres], writes=[RB[bk]])
                            sft, rsf, sfd = sf.nxt()
                            sfd2 = sf.d2[(sf.i - 1) % sf.n]
                            eng = "act" if (mmk % 2) else "dve"
                            if eng == "act":
                                S.op("act", (lambda sft, bk, rows, wn: lambda h: h.activation(out=sft[:rows, :wn], in_=PB[bk][:rows, :wn],
                                                                                              func=AF.Copy))(sft, bk, rows, wn),
                                     reads=[RB[bk]], writes=[rsf])
                            else:
                                S.op("dve", (lambda sft, bk, rows, wn: lambda h: h.tensor_copy(out=sft[:rows, :wn], in_=PB[bk][:rows, :wn]))(sft, bk, rows, wn),
                                     reads=[RB[bk]], writes=[rsf])
                            S.op("sp", (lambda sft, rows, wn, a: lambda h: h.dma_start(out=a, in_=sft[:rows, :wn]))(
                                sft, rows, wn, okv[t0 + b0:t0 + b0 + rows, oc0:oc0 + wn]),
                                reads=[rsf], writes=[Res()], dsem=sfd)
                            if kind == "v":
                                vdst = ctx["V"][which][tok0 + t0 + b0:tok0 + t0 + b0 + rows, head0 * 128:head0 * 128 + 512]
                                S.op("pool", (lambda sft, rows, a: lambda h: h.dma_start(out=a, in_=sft[:rows, :]))(sft, rows, vdst),
                                     reads=[rsf], writes=[Res()], dsem=sfd2)
                            elif kind == "k":
                                bk2 = 3 + (mmk % 2)

                                def tpf(h, sft=sft, bk2=bk2, rows=rows):
                                    for q in range(4):
                                        ins = h.transpose(out=PB[bk2][:, q * 128:q * 128 + rows], in_=sft[:rows, q * 128:(q + 1) * 128],
                                                          identity=identf[:rows, :rows])
                                    return ins
                                S.op("pe", tpf, reads=[rsf, r_const], writes=[RB[bk2]])
                                src = PB[bk2][:, :].rearrange("p (q c) -> p q c", q=4)[:, :, 0:rows]
                                S.op("dve", (lambda kt, src, b0, rows: lambda h: h.tensor_copy(out=kt[:, :, b0:b0 + rows], in_=src))(kt, src, b0, rows),
                                     reads=[RB[bk2]], writes=[rkt])
                            else:
                                kdt, rkd, _ = kid.nxt()
                                S.op("pool", (lambda kdt, sft, rows: lambda h: h.tensor_copy(out=kdt[:rows, 0:64], in_=sft[:rows, 0:64]))(kdt, sft, rows),
                                     reads=[rsf], writes=[rkd])
                                S.op("pool", (lambda kdt, sft, rows: lambda h: h.tensor_copy(out=kdt[:rows, 64:128], in_=sft[:rows, 0:64]))(kdt, sft, rows),
                                     reads=[rsf, rkd], writes=[rkd])
                                bk2 = 5
                                S.op("pe", (lambda kdt, rows: lambda h: h.transpose(out=PB[5][:, :rows], in_=kdt[:rows, :],
                                                                                    identity=identf[:rows, :rows]))(kdt, rows),
                                     reads=[rkd, r_const], writes=[RB[5]])
                                S.op("dve", (lambda k2, b0, rows: lambda h: h.tensor_copy(out=k2[:, b0:b0 + rows], in_=PB[5][:, :rows]))(k2, b0, rows),
                                     reads=[RB[5]], writes=[rk2])
                        if kind == "k":
                            for q in range(4):
                                S.op("sp", (lambda kt, q, a, n: lambda h: h.dma_start(out=a, in_=kt[:, q, :n]))(
                                    kt, q, ctx["KT"][which][head0 + q, :, tok0 + t0:tok0 + t0 + n], n),
                                    reads=[rkt], writes=[Res()], dsem=kd)
                        if kind == "i":
                            S.op("sp", (lambda k2, a, n: lambda h: h.dma_start(out=a, in_=k2[:, :n]))(
                                k2, ctx["KI2"][:, tok0 + t0:tok0 + t0 + n], n), reads=[rk2], writes=[Res()], dsem=k2d)
                S.flush()
                for rg in (xr, wr, sf, kts, ki2):
                    rg.release()
                S.putd(dl)

        def cache_import(s):
            ctx = ctx_s[s]
            with ExitStack() as st:
                dl = [S.getd() for _ in range(2)]
                dsw = [S.getd(sw=True) for _ in range(2)]
                for which, ci in ((0, 1), (1, 3)):
                    S.op("pool", (lambda a, b: lambda h: h.dma_start(out=a, in_=b))(ctx["V"][which][0:PAST, :], cache[ci][s]),
                         writes=[Res()], dsem=dsw[which])
                cr = Ring(S, nc, st, "ci_c", [128, 1024], F32, 3)
                kst = Ring(S, nc, st, "ci_k", [128, 8, 512], BF16, 2)
                for which, ci in ((0, 0), (1, 2)):
                    for g4 in range(PAST // 512):
                        kt, rkt, kd = kst.nxt()
                        for bb in range(4):
                            tb = g4 * 4 + bb
                            ct, rct, cd = cr.nxt()
                            S.op("sp", (lambda ct, a: lambda h: h.dma_start(out=ct[:], in_=a))(ct, cache[ci][s, tb * 128:(tb + 1) * 128, :]),
                                 writes=[rct], dsem=cd)
                            for hh in range(2):
                                bk = (tb * 2 + hh) % 4

                                def tpf(h, ct=ct, bk=bk, hh=hh):
                                    for q in range(4):
                                        hd = hh * 4 + q
                                        ins = h.transpose(out=PB[bk][:, q * 128:(q + 1) * 128], in_=ct[:, hd * 128:(hd + 1) * 128],
                                                          identity=identf[:])
                                    return ins
                                S.op("pe", tpf, reads=[rct, r_const], writes=[RB[bk]])
                                src = PB[bk][:, :].rearrange("p (q c) -> p q c", q=4)
                                eng = "act" if hh else "dve"
                                if eng == "act":
                                    S.op("act", (lambda kt, src, hh, bb: lambda h: h.activation(out=kt[:, hh * 4:hh * 4 + 4, bb * 128:(bb + 1) * 128],
                                                                                               in_=src, func=AF.Copy))(kt, src, hh, bb),
                                         reads=[RB[bk]], writes=[rkt])
                                else:
                                    S.op("dve", (lambda kt, src, hh, bb: lambda h: h.tensor_copy(out=kt[:, hh * 4:hh * 4 + 4, bb * 128:(bb + 1) * 128],
                                                                                                in_=src))(kt, src, hh, bb),
                                         reads=[RB[bk]], writes=[rkt])
                        for hd in range(8):
                            S.op("sp", (lambda kt, hd, a: lambda h: h.dma_start(out=a, in_=kt[:, hd, :]))(
                                kt, hd, ctx["KT"][which][hd, :, g4 * 512:(g4 + 1) * 512]), reads=[rkt], writes=[Res()], dsem=kd)
                ir = Ring(S, nc, st, "ci_i", [128, 128], F32, 3)
                i2 = Ring(S, nc, st, "ci_i2", [128, 512], BF16, 2)
                for g4 in range(PAST // 512):
                    k2, rk2, k2d = i2.nxt()
                    for bb in range(4):
                        tb = g4 * 4 + bb
                        it, rit, idd = ir.nxt()
                        S.op("sp", (lambda it, a: lambda h: h.dma_start(out=it[:, 0:64], in_=a))(it, cix[s, tb * 128:(tb + 1) * 128, :]),
                             writes=[rit], dsem=idd)
                        S.op("sp", (lambda it, a: lambda h: h.dma_start(out=it[:, 64:128], in_=a))(it, cix[s, tb * 128:(tb + 1) * 128, :]),
                             writes=[rit], dsem=idd)
                        S.op("pe", (lambda it: lambda h: h.transpose(out=PB[5][:, :128], in_=it[:], identity=identf[:]))(it),
                             reads=[rit, r_const], writes=[RB[5]])
                        S.op("dve", (lambda k2, bb: lambda h: h.tensor_copy(out=k2[:, bb * 128:(bb + 1) * 128], in_=PB[5][:, :128]))(k2, bb),
                             reads=[RB[5]], writes=[rk2])
                    S.op("sp", (lambda k2, a: lambda h: h.dma_start(out=a, in_=k2[:]))(k2, ctx["KI2"][:, g4 * 512:(g4 + 1) * 512]),
                         reads=[rk2], writes=[Res()], dsem=k2d)
                S.flush()
                for rg in (cr, kst, ir, i2):
                    rg.release()
                S.putd(dl)
                S.putd(dsw, sw=True)

        def win_q(xsrc, n, r):
            with ExitStack() as st:
                dl = [S.getd() for _ in range(4)]
                A, Bt, rA, rB = load_mod(st, dl, r, 1)
                nt = NormT(st, "nq")
                hT = PT("q_hT", [128, 16, n], BF16, st)
                rhT = Res()
                xr = Ring(S, nc, st, "q_x", [128, D], F32, 3)
                wr = Ring(S, nc, st, "q_w", [128, 16, 512], BF16, 2)
                qs = Ring(S, nc, st, "q_s", [128, n], BF16, 4)
                win = wbf["in"].rearrange("(kc p) c -> p kc c", p=128)
                blks = blocks_of(n)
                for (b0, rows) in blks:
                    xt, rx, xd = xr.nxt()
                    S.op("sp", (lambda xt, a, rows: lambda h: h.dma_start(out=xt[:rows], in_=a))(xt, xsrc[b0:b0 + rows, :], rows),
                         writes=[rx], dsem=xd)
                    nt.run(xt, rx, rows, A, Bt, rA, rB, hT, rhT, b0)
                import os as _os
                qskip = int(_os.environ.get("MK_QSKIP", "0"))
                for fc in range(16 if not (qskip & 1) else 0):
                    S.op("sp", (lambda fc: lambda h: h.dma_start(out=hTs[fc, :, :n], in_=hT[:, fc, :]))(fc), reads=[rhT],
                         writes=[r_scr["hT"]], dsem=dl[3])
                k = 0
                for (kind, c0, scale) in (((0, C_QSB, SCALE), (2, C_QSA, SCALE), (3, C_QIX, 0.125)) if not (qskip & 2) else ()):
                    for hg in range(2):
                        wt, wres, wd = wr.nxt()
                        S.op("sp", (lambda wt, c: lambda h: h.dma_start(out=wt[:], in_=win[:, :, c:c + 512]))(wt, c0 + hg * 512),
                             reads=[r_scr["wbf"]], writes=[wres], dsem=wd)
                        for h4 in range(4):
                            hd = hg * 4 + h4
                            bk = k % 3
                            k += 1

                            def mmf(h, wt=wt, bk=bk, h4=h4):
                                for kc in range(16):
                                    ins = h.matmul(PB[bk][:, :n], wt[:, kc, h4 * 128:(h4 + 1) * 128], hT[:, kc, :], start=(kc == 0), stop=(kc == 15))
                                return ins
                            S.op("pe", mmf, reads=[rhT, wres], writes=[RB[bk]])
                            qt, rq, qd = qs.nxt()
                            S.op("act", (lambda qt, bk, scale: lambda h: h.activation(out=qt[:], in_=PB[bk][:, :n], func=AF.Copy, scale=scale))(qt, bk, scale),
                                 reads=[RB[bk]], writes=[rq])
                            if not (qskip & 8):
                                S.op("sp", (lambda qt, kind, hd: lambda h: h.dma_start(out=QTs[kind, hd, :, :n], in_=qt[:]))(qt, kind, hd),
                                     reads=[rq], writes=[r_scr["QT"]], dsem=qd)
                            if kind == 0 and not (qskip & 16):
                                qt2, rq2, qd2 = qs.nxt()
                                S.op("dve", (lambda qt2, qt: lambda h: h.tensor_scalar(out=qt2[:], in0=qt[:], scalar1=-1.0, scalar2=None,
                                                                                      op0=ALU.mult))(qt2, qt), reads=[rq], writes=[rq2])
                                S.op("sp", (lambda qt2, hd: lambda h: h.dma_start(out=QTs[1, hd, :, :n], in_=qt2[:]))(qt2, hd),
                                     reads=[rq2], writes=[r_scr["QT"]], dsem=qd2)
                wx = PT("q_wx", [128, 16, 16], BF16, st)
                wxf = PT("q_wxf", [128, 16, 16], F32, st)
                rwx, rwxf = Res(), Res()
                S.op("sp", lambda h: h.dma_start(out=wxf[:], in_=wix_in), writes=[rwxf], dsem=dl[3])
                S.op("dve", lambda h: h.tensor_copy(out=wx[:], in_=wxf[:]), reads=[rwxf], writes=[rwx])
                for i, (b0, rows) in enumerate(blks if not (qskip & 4) else []):
                    def mmw(h, b0=b0, rows=rows):
                        for kc in range(16):
                            ins = h.matmul(PB[4][:rows, :16], hT[:, kc, b0:b0 + rows], wx[:, kc, :], start=(kc == 0), stop=(kc == 15))
                        return ins
                    S.op("pe", mmw, reads=[rhT, rwx], writes=[RB[4]])
                    S.op("dve", (lambda i, rows: lambda h: h.tensor_copy(out=wraw[:rows, i, :], in_=PB[4][:rows, :16]))(i, rows),
                         reads=[RB[4]], writes=[r_wraw])
                S.flush()
                for rg in (xr, wr, qs):
                    rg.release()
                S.putd(dl)

        def win_attn(ctx, n, kblocks, slot, strip, ULx, u0_of, near_of, sbmask_of, idxmask_of):
            KE = kblocks[-1][0] + kblocks[-1][1]
            qblks = blocks_of(n)
            nqb = len(qblks)
            nkb = len(kblocks)
            with ExitStack() as st:
                dl = [S.getd() for _ in range(6)]
                scores = PT("at_sc", [128, KE], F32, st)
                masks = [PT("at_m%d" % i, [128, KE], FP8, st) for i in range(nqb)]
                rsc = Res()
                rmk = [Res() for _ in range(nqb)]
                qix = PT("at_qix", [128, 8, n], BF16, st)
                rqix = Res()
                S.op("sp", lambda h: h.dma_start(out=qix[:], in_=QTs[3, :, :, :n].rearrange("a p c -> p a c")), reads=[r_scr["QT"]],
                     writes=[rqix], dsem=dl[0])
                CK = 1024
                kvr_k = Ring(S, nc, st, "at_k", [128, CK], BF16, 3)
                kvr_v = Ring(S, nc, st, "at_v", [128, CK // 128, 128], BF16, 3)
                kir = Ring(S, nc, st, "at_ki", [128, 512], BF16, 3)
                imr = Ring(S, nc, st, "at_im", [128, 512], BF16, 2)
                smr = Ring(S, nc, st, "at_sm", [128, n], BF16, 4)
                qr = Ring(S, nc, st, "at_q", [128, n], BF16, 4)
                er = Ring(S, nc, st, "at_e", [128, n], F32, 3, with_dsem=False)
                spr = Ring(S, nc, st, "at_sp", [128, n], BF16, 4, with_dsem=False)
                ar = Ring(S, nc, st, "at_a", [128, n], BF16, 4, with_dsem=False)
                rr_ = Ring(S, nc, st, "at_r", [128, 512], BF16, 6, with_dsem=False)
                osr = Ring(S, nc, st, "at_os", [128, n], BF16, 2)
                Sacc = PT("at_S", [128, n], BF16, st)
                rS = Res()
                Dh = PT("at_Dh", [128, NHI, 128], BF16, st)
                rDh = Res()
                wsm = PT("at_wsm", [128, 2, NHI], F32, st)
                rws = Res()
                bs = PT("at_bs", [128, 16], F32, st)
                rbs = Res()
                stp = PT("at_strip", [128, ULx], BF16, st)
                rstp = Res()
                dstp = dl[1]
                rden = PT("at_rden", [128, n], F32, st)
                rrden = Res()

                chunks = []
                i = 0
                while i < nkb:
                    j = i
                    while j < nkb and kblocks[j][0] + kblocks[j][1] <= kblocks[i][0] + CK:
                        j += 1
                    chunks.append((i, j))
                    i = j

                def load_kv(which, hd, ci):
                    i0, i1 = chunks[ci]
                    k0 = kblocks[i0][0]
                    kn = kblocks[i1 - 1][0] + kblocks[i1 - 1][1] - k0
                    kt, rk, kd = kvr_k.nxt()
                    vt, rv, vd = kvr_v.nxt()
                    S.op("sp", (lambda kt, a, kn: lambda h: h.dma_start(out=kt[:, :kn], in_=a))(kt, ctx["KT"][which][hd, :, k0:k0 + kn], kn),
                         reads=[r_scr["ctx"]], writes=[rk], dsem=kd)
                    nfull = kn // 128
                    if nfull:
                        S.op("sp", (lambda vt, a, nfull: lambda h: h.dma_start(out=vt[:, :nfull, :], in_=a))(
                            vt, ctx["V"][which][k0:k0 + nfull * 128, hd * 128:(hd + 1) * 128].rearrange("(b p) d -> p b d", p=128), nfull),
                            reads=[r_scr["ctx"]], writes=[rv], dsem=vd)
                    rem = kn - nfull * 128
                    if rem:
                        S.op("sp", (lambda vt, a, nfull, rem: lambda h: h.dma_start(out=vt[:rem, nfull, :], in_=a))(
                            vt, ctx["V"][which][k0 + nfull * 128:k0 + kn, hd * 128:(hd + 1) * 128], nfull, rem),
                            reads=[r_scr["ctx"]], writes=[rv], dsem=vd)
                    return kt, rk, vt, rv, k0

                def idx(qi):
                    q0, qrows = qblks[qi]
                    S.op("dve", lambda h: h.tensor_scalar(out=wsm[:qrows, 0, :], in0=wraw[:qrows, qi, :], scalar1=-0.25, scalar2=None,
                                                          op0=ALU.mult), reads=[r_wraw], writes=[rws])
                    S.op("dve", lambda h: h.scalar_tensor_tensor(out=wsm[:qrows, 0, :], in0=wraw[:qrows, qi, :], scalar=0.25, in1=wsm[:qrows, 0, :],
                                                                 op0=ALU.mult, op1=ALU.max), reads=[r_wraw, rws], writes=[rws])
                    S.op("act", lambda h: h.activation(out=wsm[:qrows, 1, :], in_=wraw[:qrows, qi, :], func=AF.Sign), reads=[r_wraw, rws],
                         writes=[rws])
                    for hh in range(NHI):
                        eng = "dve" if hh % 2 else "pool"
                        S.op(eng, (lambda hh: lambda h: h.tensor_scalar(out=Dh[:qrows, hh, :qrows], in0=identb[:qrows, :qrows],
                                                                       scalar1=wsm[:qrows, 1, hh:hh + 1], scalar2=None, op0=ALU.mult))(hh),
                             reads=[rws, r_const], writes=[rDh])
                    nkt = (KE + 511) // 512

                    def kithunk(kt_i):
                        def f():
                            k0 = kt_i * 512
                            kw = min(512, KE - k0)
                            kit, rki, kid_ = kir.nxt()
                            S.op("sp", lambda h: h.dma_start(out=kit[:, :kw], in_=ctx["KI2"][:, k0:k0 + kw]),
                                 reads=[r_scr["ctx"]], writes=[rki], dsem=kid_)
                            return kit, rki
                        return f
                    kpre = Pre([kithunk(k) for k in range(nkt)], 1)
                    def ktile(kt_i):
                        k0 = kt_i * 512
                        kw = min(512, KE - k0)
                        kit, rki = kpre.get(kt_i)
                        sb = 4 + (kt_i % 2)
                        pend = []
                        for step in range(NHI + 3):
                            if step < NHI:
                                hh = step
                                bk = hh % 4
                                base = 64 * (hh % 2)
                                S.op("pe", (lambda hh, bk, base: lambda h: h.matmul(PB[bk][:qrows, :kw], qix[base:base + 64, hh // 2, q0:q0 + qrows],
                                                                                  kit[base:base + 64, :kw], start=True, stop=True))(hh, bk, base),
                                     reads=[rqix, rki], writes=[RB[bk]])
                                rt, rrt, _ = rr_.nxt()
                                if hh % 2 == 0:
                                    S.op("act", (lambda rt, bk, hh: lambda h: h.activation(out=rt[:qrows, :kw], in_=PB[bk][:qrows, :kw], func=AF.Relu,
                                                                                          scale=wsm[:qrows, 0, hh:hh + 1]))(rt, bk, hh),
                                         reads=[RB[bk], rws], writes=[rrt])
                                else:
                                    S.op("dve", (lambda rt, bk, hh: lambda h: h.tensor_scalar(out=rt[:qrows, :kw], in0=PB[bk][:qrows, :kw],
                                                                                             scalar1=wsm[:qrows, 0, hh:hh + 1], scalar2=0.0,
                                                                                             op0=ALU.mult, op1=ALU.max))(rt, bk, hh),
                                         reads=[RB[bk], rws], writes=[rrt])
                                pend.append((rt, rrt))
                            if step >= 3:
                                h2 = step - 3
                                rt, rrt = pend[h2]
                                S.op("pe", (lambda rt, h2: lambda h: h.matmul(PB[sb][:qrows, :kw], Dh[:qrows, h2, :qrows], rt[:qrows, :kw],
                                                                             start=(h2 == 0), stop=(h2 == NHI - 1)))(rt, h2),
                                     reads=[rrt, rDh], writes=[RB[sb]])
                        ima = idxmask_of(qi, kt_i)
                        if ima is not None:
                            imt, rim, imd = imr.nxt()
                            S.op("sp", (lambda imt, ima: lambda h: h.dma_start(out=imt[:], in_=ima))(imt, ima), writes=[rim], dsem=imd)
                            S.op("dve", (lambda imt: lambda h: h.tensor_tensor(out=scores[:qrows, k0:k0 + kw], in0=PB[sb][:qrows, :kw],
                                                                              in1=imt[:qrows, :kw], op=ALU.add))(imt),
                                 reads=[RB[sb], rim], writes=[rsc])
                        else:
                            S.op("act", lambda h: h.activation(out=scores[:qrows, k0:k0 + kw], in_=PB[sb][:qrows, :kw],
                                                               func=AF.Copy), reads=[RB[sb]], writes=[rsc])
                    for kt_i in range(nkt):
                        ktile(kt_i)

                def bisect(qi):
                    q0, qrows = qblks[qi]
                    mk = masks[qi]
                    lo, hi, mid, cnt, ge, d1 = (bs[:qrows, c:c + 1] for c in range(6))
                    S.op("dve", lambda h: h.reduce_max(out=hi, in_=scores[:qrows, :KE], axis=AX.X), reads=[rsc], writes=[rbs])
                    S.op("dve", lambda h: h.tensor_scalar(out=lo, in0=hi, scalar1=-BRANGE, scalar2=None, op0=ALU.add), reads=[rbs], writes=[rbs])
                    S.op("dve", lambda h: h.tensor_scalar(out=hi, in0=hi, scalar1=1e-3, scalar2=None, op0=ALU.add), reads=[rbs], writes=[rbs])
                    for it in range(NBIS):
                        S.op("dve", lambda h: h.tensor_scalar(out=mid, in0=lo, scalar1=hi, scalar2=0.5, op0=ALU.add, op1=ALU.mult),
                             reads=[rbs], writes=[rbs])
                        S.op("dve", lambda h: h.tensor_scalar(out=mk[:qrows, :KE], in0=scores[:qrows, :KE], scalar1=mid, scalar2=0.0,
                                                              op0=ALU.is_ge, op1=ALU.add, accum_out=cnt, saturate=False),
                             reads=[rsc, rbs], writes=[rmk[qi], rbs])
                        S.op("dve", lambda h: h.tensor_scalar(out=ge, in0=cnt, scalar1=TOPK - 0.5, scalar2=None, op0=ALU.is_ge), reads=[rbs],
                             writes=[rbs])
                        S.op("dve", lambda h: h.tensor_tensor(out=d1, in0=mid, in1=lo, op=ALU.subtract), reads=[rbs], writes=[rbs])
                        S.op("dve", lambda h: h.scalar_tensor_tensor(out=lo, in0=d1, scalar=ge, in1=lo, op0=ALU.mult, op1=ALU.add), reads=[rbs],
                             writes=[rbs])
                        S.op("dve", lambda h: h.tensor_tensor(out=d1, in0=hi, in1=mid, op=ALU.subtract), reads=[rbs], writes=[rbs])
                        S.op("dve", lambda h: h.scalar_tensor_tensor(out=hi, in0=d1, scalar=ge, in1=mid, op0=ALU.mult, op1=ALU.add), reads=[rbs],
                             writes=[rbs])
                    S.op("dve", lambda h: h.tensor_scalar(out=mk[:qrows, :KE], in0=scores[:qrows, :KE], scalar1=lo, scalar2=-240.0,
                                                          op0=ALU.is_lt, op1=ALU.mult, saturate=False), reads=[rsc, rbs], writes=[rmk[qi]])

                def sb_head(hd):
                    qt, rq, qd = qr.nxt()
                    qn, rqn, qnd = qr.nxt()
                    S.op("sp", lambda h: h.dma_start(out=qt[:], in_=QTs[0, hd, :, :n]), reads=[r_scr["QT"]], writes=[rq], dsem=qd)
                    S.op("sp", lambda h: h.dma_start(out=qn[:], in_=QTs[1, hd, :, :n]), reads=[r_scr["QT"]], writes=[rqn], dsem=qnd)
                    S.op("pool", lambda h: h.memset(Sacc[:], 0.0), writes=[rS])
                    corder = list(reversed(range(len(chunks))))
                    kvpre = Pre([(lambda ci: lambda: load_kv(0, hd, ci))(ci) for ci in corder], 1)
                    kvc = {}
                    tiles = []
                    for pos, ci in enumerate(corder):
                        i0, i1 = chunks[ci]
                        for kbi in reversed(range(i0, i1)):
                            tiles.append((pos, ci, kbi))
                    T_ = len(tiles)
                    stt = [None] * T_

                    def Z(i):
                        pos, ci, kbi = tiles[i]
                        if ci not in kvc:
                            kvc[ci] = kvpre.get(pos)
                        kt, rk, vt, rv, kc0 = kvc[ci]
                        k0, rows = kblocks[kbi]
                        o = k0 - kc0
                        d = dict(kt=kt, rk=rk, vt=vt, rv=rv, o=o, vb=o // 128, rows=rows, zb=(0, 1, 4)[i % 3], lb=(2, 3, 5)[i % 3])
                        sma = sbmask_of(kbi)
                        d["sma"] = sma
                        if sma is not None:
                            smt, rsm, smd = smr.nxt()
                            S.op("sp", lambda h: h.dma_start(out=smt[:rows, :], in_=sma), writes=[rsm], dsem=smd)
                            d["smt"], d["rsm"] = smt, rsm
                        zb = d["zb"]

                        def zf(h):
                            ins = h.matmul(PB[zb][:rows, :n], kt[:, o:o + rows], qt[:], start=True, stop=(sma is None))
                            if sma is not None:
                                ins = h.matmul(PB[zb][:rows, :n], identb[:rows, :rows], d["smt"][:rows, :], start=False, stop=True)
                            return ins
                        S.op("pe", zf, reads=[rk, rq, r_const] + ([d["rsm"]] if sma is not None else []), writes=[RB[zb]])
                        et, ret, _ = er.nxt()
                        spt, rspt, _ = spr.nxt()
                        S.op("act", lambda h: h.activation(out=et[:rows, :], in_=PB[zb][:rows, :n], func=AF.Exp), reads=[RB[zb]], writes=[ret])
                        S.op("act", lambda h: h.activation(out=spt[:rows, :], in_=et[:rows, :], func=AF.Ln, bias=1.0), reads=[ret], writes=[rspt])
                        d["spt"], d["rspt"] = spt, rspt
                        stt[i] = d

                    def L(i):
                        d = stt[i]
                        kt, o, rows, lb, sma, spt = d["kt"], d["o"], d["rows"], d["lb"], d["sma"], d["spt"]

                        def lf(h):
                            h.matmul(PB[lb][:rows, :n], kt[:, o:o + rows], qn[:], start=True, stop=False)
                            if sma is not None:
                                h.matmul(PB[lb][:rows, :n], nidentb[:rows, :rows], d["smt"][:rows, :], start=False, stop=False)
                            h.matmul(PB[lb][:rows, :n], trib[:rows, :rows], spt[:rows, :], start=False, stop=False)
                            return h.matmul(PB[lb][:rows, :n], onesb[:, :rows], Sacc[:], start=False, stop=True)
                        S.op("pe", lf, reads=[d["rk"], rqn, r_const, d["rspt"], rS] + ([d["rsm"]] if sma is not None else []), writes=[RB[lb]])
                        at, rat, _ = ar.nxt()
                        S.op("act", lambda h: h.activation(out=at[:rows, :], in_=PB[lb][:rows, :n], func=AF.Exp, scale=-1.0), reads=[RB[lb]], writes=[rat])
                        S.op("pool", lambda h: h.tensor_tensor(out=Sacc[:rows, :], in0=Sacc[:rows, :], in1=spt[:rows, :], op=ALU.add),
                             reads=[d["rspt"], rS], writes=[rS])
                        d["at"], d["rat"] = at, rat

                    def O(i):
                        d = stt[i]
                        vt, vb, rows, at = d["vt"], d["vb"], d["rows"], d["at"]
                        S.op("pe", lambda h: h.matmul(PB[6][:, :n], vt[:rows, vb, :], at[:rows, :], start=(i == 0), stop=(i == T_ - 1)),
                             reads=[d["rv"], d["rat"]], writes=[RB[6]])
                        stt[i] = None
                    for i in range(T_ + 3):
                        if i < T_:
                            Z(i)
                        if 0 <= i - 2 < T_:
                            L(i - 2)
                        if 0 <= i - 3 < T_:
                            O(i - 3)
                    ost, ros, osd = osr.nxt()
                    S.op("act", lambda h: h.activation(out=ost[:], in_=PB[6][:, :n], func=AF.Copy), reads=[RB[6]], writes=[ros])
                    S.op("sp", lambda h: h.dma_start(out=OTs[0, hd, :, :n], in_=ost[:]), reads=[ros], writes=[r_scr["OT"]], dsem=osd)

                def dsa_head(hd):
                    qt, rq, qd = qr.nxt()
                    S.op("sp", lambda h: h.dma_start(out=qt[:], in_=QTs[2, hd, :, :n]), reads=[r_scr["QT"]], writes=[rq], dsem=qd)
                    S.op("sp", lambda h: h.dma_start(out=stp[:], in_=strip[hd]), reads=[r_scr["strip"]], writes=[rstp], dsem=dstp)
                    corder = list(range(len(chunks)))
                    kvpre = Pre([(lambda ci: lambda: load_kv(1, hd, ci))(ci) for ci in corder], 1)
                    kvc = {}
                    tiles = []
                    for pos, ci in enumerate(corder):
                        i0, i1 = chunks[ci]
                        for kbi in range(i0, i1):
                            tiles.append((pos, ci, kbi))
                    T_ = len(tiles)
                    stt = [None] * T_

                    def LT(i):
                        pos, ci, kbi = tiles[i]
                        if ci not in kvc:
                            kvc[ci] = kvpre.get(pos)
                        kt, rk, vt, rv, kc0 = kvc[ci]
                        k0, rows = kblocks[kbi]
                        o = k0 - kc0
                        lb = i % 4
                        nearb = near_of(kbi)
                        u0 = u0_of(kbi) if nearb else 0

                        def lf(h):
                            ins = h.matmul(PB[lb][:rows, :n], kt[:, o:o + rows], qt[:], start=True, stop=False)
                            for qi, (q0, qrows) in enumerate(qblks):
                                lastm = (qi == nqb - 1) and not nearb
                                ins = h.matmul(PB[lb][:rows, q0:q0 + qrows], masks[qi][:qrows, k0:k0 + rows], identb[:qrows, :qrows],
                                               start=False, stop=lastm)
                            if nearb:
                                ins = h.matmul(PB[lb][:rows, :n], identb[:rows, :rows], stp[:rows, u0:u0 + n], start=False, stop=True)
                            return ins
                        S.op("pe", lf, reads=[rk, rq, r_const, rstp] + rmk, writes=[RB[lb]])
                        pt, rpt, _ = ar.nxt()
                        if nearb:
                            S.op("act", lambda h: h.activation(out=pt[:rows, :], in_=PB[lb][:rows, :n], func=AF.Exp), reads=[RB[lb]], writes=[rpt])
                        else:
                            S.op("act", lambda h: h.activation(out=pt[:rows, :], in_=PB[lb][:rows, :n], func=AF.Exp, bias=CH[:rows, hd:hd + 1]),
                                 reads=[RB[lb], r_const], writes=[rpt])
                        stt[i] = dict(vt=vt, rv=rv, vb=o // 128, rows=rows, pt=pt, rpt=rpt)

                    def O(i):
                        d = stt[i]
                        vt, vb, rows, pt = d["vt"], d["vb"], d["rows"], d["pt"]

                        def of(h):
                            h.matmul(PB[7][:, :n], vt[:rows, vb, :], pt[:rows, :], start=(i == 0), stop=(i == T_ - 1))
                            return h.matmul(PB[5][:, :n], onesb[:rows, :], pt[:rows, :], start=(i == 0), stop=(i == T_ - 1))
                        S.op("pe", of, reads=[d["rv"], d["rpt"], r_const], writes=[RB[7], RB[5]])
                        stt[i] = None
                    for i in range(T_ + 1):
                        if i < T_:
                            LT(i)
                        if 0 <= i - 1 < T_:
                            O(i - 1)
                    S.op("dve", lambda h: h.tensor_scalar(out=rden[:], in0=PB[5][:, :n], scalar1=1e-30, scalar2=None, op0=ALU.max), reads=[RB[5]],
                         writes=[rrden])
                    S.op("dve", lambda h: h.reciprocal(out=rden[:], in_=rden[:]), reads=[rrden], writes=[rrden])
                    ost, ros, osd = osr.nxt()
                    S.op("dve", lambda h: h.tensor_tensor(out=ost[:], in0=PB[7][:, :n], in1=rden[:], op=ALU.mult),
                         reads=[RB[7], rrden], writes=[ros])
                    S.op("sp", lambda h: h.dma_start(out=OTs[1, hd, :, :n], in_=ost[:]), reads=[ros], writes=[r_scr["OT"]], dsem=osd)

                hpq = (NH + nqb - 1) // nqb
                hd_next = 0
                for qi in range(nqb):
                    idx(qi)
                    bisect(qi)
                    for _ in range(hpq):
                        if hd_next < NH:
                            sb_head(hd_next)
                            hd_next += 1
                while hd_next < NH:
                    sb_head(hd_next)
                    hd_next += 1
                for hd in range(NH):
                    dsa_head(hd)
                S.flush()
                for rg in (kvr_k, kvr_v, kir, imr, smr, qr, osr):
                    rg.release()
                S.putd(dl)

        def win_ffn(xsrc, n, r, halo, ydst, nout, prev_src, conv_dst, conv_cols, slot_flag):
            blks = blocks_of(n)
            nb = len(blks)
            with ExitStack() as st0:
                x1 = [PT("f_x1_%d" % i, [128, D], F32, st0) for i in range(nb)]
                rx1 = [Res() for _ in range(nb)]
                h2T = PT("f_h2T", [128, 16, n], BF16, st0)
                rh2 = Res()
                with ExitStack() as st:
                    dl = [S.getd() for _ in range(4)]
                    hT = PT("fa_hT", [128, 16, n], BF16, st)
                    oT = PT("fa_oT", [128, 16, n], BF16, st)
                    mT = PT("fa_mT", [128, 16, n], BF16, st)
                    rhT, roT, rmT = Res(), Res(), Res()
                    S.op("sp", lambda h: h.dma_start(out=hT[:], in_=hTs[:, :, :n].rearrange("a p c -> p a c")), reads=[r_scr["hT"]], writes=[rhT], dsem=dl[0])
                    for t in range(2):
                        S.op("sp", (lambda t: lambda h: h.dma_start(out=oT[:, t * 8:(t + 1) * 8, :], in_=OTs[t, :, :, :n].rearrange("a p c -> p a c")))(t),
                             reads=[r_scr["OT"]], writes=[roT], dsem=dl[1])
                    for i, (b0, rows) in enumerate(blks):
                        S.op("sp", (lambda i, b0, rows: lambda h: h.dma_start(out=x1[i][:rows], in_=xsrc[b0:b0 + rows, :]))(i, b0, rows), writes=[rx1[i]],
                             dsem=dl[2])
                    gt1, rgt1 = load_row_rep(st, dl[3], "fa_gt1", modrow[r:r + 1, 2 * D:3 * D])
                    wg = Ring(S, nc, st, "fa_wg", [128, 16, 256], BF16, 2)
                    wb = Ring(S, nc, st, "fa_wb", [128, 16, 128], BF16, 2)
                    gs = Ring(S, nc, st, "fa_g", [128, n], F32, 4, with_dsem=False)
                    wgv = wbf["gate"].rearrange("(kc p) c -> p kc c", p=128)
                    wbv = [wbf["brsb"].rearrange("(kc p) c -> p kc c", p=128), wbf["brsa"].rearrange("(kc p) c -> p kc c", p=128)]

                    def gthunk(fb):
                        def f():
                            wt, wres, wd = wg.nxt()
                            S.op("sp", lambda h: h.dma_start(out=wt[:, :, 0:128], in_=wgv[:, :, fb * 128:(fb + 1) * 128]),
                                 reads=[r_scr["wbf"]], writes=[wres], dsem=wd)
                            S.op("sp", lambda h: h.dma_start(out=wt[:, :, 128:256], in_=wgv[:, :, D + fb * 128:D + (fb + 1) * 128]),
                                 reads=[r_scr["wbf"]], writes=[wres], dsem=wd)
                            wt2, wres2, wd2 = wb.nxt()
                            for t in range(2):
                                S.op("sp", (lambda t: lambda h: h.dma_start(out=wt2[:, t * 8:(t + 1) * 8, :], in_=wbv[t][:, :, fb * 128:(fb + 1) * 128]))(t),
                                     reads=[r_scr["wbf"]], writes=[wres2], dsem=wd2)
                            return wt, wres, wt2, wres2
                        return f
                    gpre = Pre([gthunk(fb) for fb in range(16)], 1)

                    def gate_fb(fb):
                        wt, wres, wt2, wres2 = gpre.get(fb)
                        gts = []
                        for t in range(2):
                            bk = t

                            def gf(h, bk=bk, t=t):
                                for kc in range(16):
                                    ins = h.matmul(PB[bk][:, :n], wt[:, kc, t * 128:(t + 1) * 128], hT[:, kc, :], start=(kc == 0), stop=(kc == 15))
                                return ins
                            S.op("pe", gf, reads=[rhT, wres], writes=[RB[bk]])
                            g_, rg_, _ = gs.nxt()
                            S.op("act", (lambda g_, bk: lambda h: h.activation(out=g_[:], in_=PB[bk][:, :n], func=AF.Sigmoid))(g_, bk), reads=[RB[bk]],
                                 writes=[rg_])
                            gts.append((g_, rg_))
                        for t in range(2):
                            bk = 2 + t

                            def bf_(h, bk=bk, t=t):
                                for kc in range(8):
                                    ins = h.matmul(PB[bk][:, :n], wt2[:, t * 8 + kc, :], oT[:, t * 8 + kc, :], start=(kc == 0), stop=(kc == 7))
                                return ins
                            S.op("pe", bf_, reads=[roT, wres2], writes=[RB[bk]])
                        g0, rg0 = gts[0]
                        g1, rg1 = gts[1]
                        S.op("dve", lambda h: h.tensor_tensor(out=g0[:], in0=PB[2][:, :n], in1=g0[:], op=ALU.mult), reads=[RB[2], rg0], writes=[rg0])
                        S.op("dve", lambda h: h.tensor_tensor(out=g1[:], in0=PB[3][:, :n], in1=g1[:], op=ALU.mult), reads=[RB[3], rg1], writes=[rg1])
                        S.op("pool", lambda h: h.tensor_tensor(out=mT[:, fb, :], in0=g0[:], in1=g1[:], op=ALU.add), reads=[rg0, rg1], writes=[rmT])
                    for fb in range(16):
                        gate_fb(fb)
                    wo = Ring(S, nc, st, "fa_wo", [128, 16, 512], BF16, 2)
                    tmpr = Ring(S, nc, st, "fa_tmp", [128, 512], F32, 2, with_dsem=False)
                    wov = wbf["out"].rearrange("(kc p) c -> p kc c", p=128)

                    def othunk(nbk):
                        def f():
                            wt, wres, wd = wo.nxt()
                            S.op("sp", lambda h: h.dma_start(out=wt[:], in_=wov[:, :, nbk * 512:(nbk + 1) * 512]),
                                 reads=[r_scr["wbf"]], writes=[wres], dsem=wd)
                            return wt, wres
                        return f
                    opre = Pre([othunk(k_) for k_ in range(4)], 1)

                    def out_blk(nbk, i, b0, rows, bk, wt, wres):
                        def of(h):
                            for kc in range(16):
                                ins = h.matmul(PB[bk][:rows, :], mT[:, kc, b0:b0 + rows], wt[:, kc, :], start=(kc == 0), stop=(kc == 15))
                            return ins
                        S.op("pe", of, reads=[rmT, wres], writes=[RB[bk]])
                        tt, rtt, _ = tmpr.nxt()
                        S.op("dve", lambda h: h.tensor_tensor(out=tt[:rows, :], in0=PB[bk][:rows, :], in1=gt1[:rows, nbk * 512:(nbk + 1) * 512], op=ALU.mult),
                             reads=[RB[bk], rgt1], writes=[rtt])
                        S.op("pool", lambda h: h.tensor_tensor(out=x1[i][:rows, nbk * 512:(nbk + 1) * 512], in0=x1[i][:rows, nbk * 512:(nbk + 1) * 512],
                                                               in1=tt[:rows, :], op=ALU.add), reads=[rtt, rx1[i]], writes=[rx1[i]])
                    k = 0
                    for nbk in range(4):
                        wt, wres = opre.get(nbk)
                        for i, (b0, rows) in enumerate(blks):
                            out_blk(nbk, i, b0, rows, 4 + (k % 2), wt, wres)
                            k += 1
                    S.flush()
                    for rg in (wg, wb, wo):
                        rg.release()
                    S.putd(dl)
                with ExitStack() as st:
                    dl = [S.getd() for _ in range(3)]
                    A2, B2, rA2, rB2 = load_mod(st, dl, r, 2)
                    nt = NormT(st, "nf")
                    for i, (b0, rows) in enumerate(blks):
                        nt.run(x1[i], rx1[i], rows, A2, B2, rA2, rB2, h2T, rh2, b0)
                    if slot_flag is not None and halo:
                        S.op("dve", lambda h: h.tensor_scalar(out=h2T[:, :, 0:halo], in0=h2T[:, :, 0:halo], scalar1=hflag[:, slot_flag:slot_flag + 1],
                                                              scalar2=None, op0=ALU.mult), reads=[rh2, r_const], writes=[rh2])
                    S.flush()
                    S.putd(dl)
                with ExitStack() as st:
                    dl = [S.getd() for _ in range(6)]
                    aT = PT("fb_aT", [128, NFF, n], BF16, st)
                    raT = Res()
                    if halo:
                        S.op("pool", lambda h: h.memset(aT[:, :, 0:halo], 0.0), writes=[raT])
                    gt2, rgt2 = load_row_rep(st, dl[0], "fb_gt2", modrow[r:r + 1, 5 * D:6 * D])
                    gfin, rgfin = load_row_rep(st, dl[1], "fb_gf", g_final[0:1, :])
                    ne = nout + 2
                    E = Ring(S, nc, st, "fb_E", [128, ne], F32, 2, with_dsem=False)
                    tr_ = Ring(S, nc, st, "fb_t", [128, nout], F32, 2, with_dsem=False)
                    wu = Ring(S, nc, st, "fb_wu", [128, 16, 256], BF16, 3)
                    wuv = wbf["up"].rearrange("(kc p) c -> p kc c", p=128)
                    prevt = None
                    rprev = Res()
                    if prev_src is not None:
                        prevt = PT("fb_prev", [128, NFF, 2], F32, st)
                        S.op("sp", lambda h: h.dma_start(out=prevt[:], in_=prev_src), writes=[rprev], dsem=dl[2])

                    def uthunk(fb):
                        def f():
                            wt, wres, wd = wu.nxt()
                            S.op("sp", lambda h: h.dma_start(out=wt[:, :, 0:128], in_=wuv[:, :, fb * 128:(fb + 1) * 128]),
                                 reads=[r_scr["wbf"]], writes=[wres], dsem=wd)
                            S.op("sp", lambda h: h.dma_start(out=wt[:, :, 128:256], in_=wuv[:, :, DFF + fb * 128:DFF + (fb + 1) * 128]),
                                 reads=[r_scr["wbf"]], writes=[wres], dsem=wd)
                            return wt, wres
                        return f
                    upre = Pre([uthunk(fb) for fb in range(NFF)], 2)

                    def up_fb(fb):
                        wt, wres = upre.get(fb)
                        for t in range(2):
                            bk = (fb % 2) * 2 + t

                            def uf(h, bk=bk, t=t):
                                for kc in range(16):
                                    ins = h.matmul(PB[bk][:, :n], wt[:, kc, t * 128:(t + 1) * 128], h2T[:, kc, :], start=(kc == 0), stop=(kc == 15))
                                return ins
                            S.op("pe", uf, reads=[rh2, wres], writes=[RB[bk]])
                        bg = (fb % 2) * 2
                        bv = bg + 1
                        Et, rE, _ = E.nxt()
                        tt, rtt, _ = tr_.nxt()
                        if prevt is None:
                            S.op("act", lambda h: h.activation(out=Et[:, :n], in_=PB[bg][:, :n], func=AF.Copy), reads=[RB[bg]], writes=[rE])
                        else:
                            S.op("act", lambda h: h.activation(out=Et[:, 2:2 + n], in_=PB[bg][:, :n], func=AF.Copy), reads=[RB[bg]], writes=[rE])
                            S.op("pool", lambda h: h.tensor_copy(out=Et[:, 0:2], in_=prevt[:, fb, :]), reads=[rprev, rE], writes=[rE])
                        S.op("dve", lambda h: h.tensor_scalar(out=tt[:], in0=Et[:, 2:2 + nout], scalar1=cwb[:, fb, 2:3], scalar2=cwb[:, fb, 3:4],
                                                              op0=ALU.mult, op1=ALU.add), reads=[rE, r_const], writes=[rtt])
                        S.op("dve", lambda h: h.scalar_tensor_tensor(out=tt[:], in0=Et[:, 1:1 + nout], scalar=cwb[:, fb, 1:2], in1=tt[:],
                                                                     op0=ALU.mult, op1=ALU.add), reads=[rE, rtt, r_const], writes=[rtt])
                        S.op("dve", lambda h: h.scalar_tensor_tensor(out=tt[:], in0=Et[:, 0:nout], scalar=cwb[:, fb, 0:1], in1=tt[:],
                                                                     op0=ALU.mult, op1=ALU.add), reads=[rE, rtt, r_const], writes=[rtt])
                        S.op("act", lambda h: h.activation(out=tt[:], in_=tt[:], func=AF.Silu), reads=[rtt], writes=[rtt])
                        S.op("dve", lambda h: h.tensor_tensor(out=aT[:, fb, halo:halo + nout], in0=PB[bv][:, halo:halo + nout], in1=tt[:], op=ALU.mult),
                             reads=[rtt, RB[bv]], writes=[raT])
                        if conv_dst is not None:
                            S.op("pool", lambda h: h.tensor_copy(out=convc[:, fb, :], in_=Et[:, conv_cols:conv_cols + 2]), reads=[rE], writes=[r_convc])
                    for fb in range(NFF):
                        up_fb(fb)
                    if conv_dst is not None:
                        S.op("sp", lambda h: h.dma_start(out=conv_dst, in_=convc[:]), reads=[r_convc], writes=[r_scr["out"]], dsem=dl[3])
                    wdr = Ring(S, nc, st, "fb_wd", [128, 11, 512], BF16, 3)
                    wdv = wbf["down"].rearrange("(kc p) c -> p kc c", p=128)
                    tmpr = Ring(S, nc, st, "fb_tmp", [128, 512], F32, 2, with_dsem=False)
                    assert nb <= 4

                    def dthunk(nbk, pc):
                        def f():
                            wt, wres, wd = wdr.nxt()
                            S.op("sp", lambda h: h.dma_start(out=wt[:], in_=wdv[:, pc * 11:(pc + 1) * 11, nbk * 512:(nbk + 1) * 512]),
                                 reads=[r_scr["wbf"]], writes=[wres], dsem=wd)
                            return wt, wres
                        return f
                    dpre = Pre([dthunk(nbk, pc) for nbk in range(4) for pc in range(4)], 2)

                    def down_piece(nbk, pc, wt, wres):
                        for i, (b0, rows) in enumerate(blks):
                            bk = 4 + i

                            def df(h, bk=bk, b0=b0, rows=rows):
                                for kc in range(11):
                                    ins = h.matmul(PB[bk][:rows, :], aT[:, pc * 11 + kc, b0:b0 + rows], wt[:, kc, :], start=(pc == 0 and kc == 0),
                                                   stop=(pc == 3 and kc == 10))
                                return ins
                            S.op("pe", df, reads=[raT, wres], writes=[RB[bk]])

                    def down_evac(nbk, i, b0, rows):
                        bk = 4 + i
                        tt, rtt, _ = tmpr.nxt()
                        S.op("dve", lambda h: h.tensor_tensor(out=tt[:rows, :], in0=PB[bk][:rows, :], in1=gt2[:rows, nbk * 512:(nbk + 1) * 512], op=ALU.mult),
                             reads=[RB[bk], rgt2], writes=[rtt])
                        S.op("pool", lambda h: h.tensor_tensor(out=x1[i][:rows, nbk * 512:(nbk + 1) * 512], in0=x1[i][:rows, nbk * 512:(nbk + 1) * 512],
                                                               in1=tt[:rows, :], op=ALU.add), reads=[rtt, rx1[i]], writes=[rx1[i]])
                    for nbk in range(4):
                        for pc in range(4):
                            wt, wres = dpre.get(nbk * 4 + pc)
                            down_piece(nbk, pc, wt, wres)
                        for i, (b0, rows) in enumerate(blks):
                            down_evac(nbk, i, b0, rows)
                    junk = PT("fb_junk", [128, D], BF16, st)
                    sm = PT("fb_sm", [128, 8], F32, st)
                    rj, rsm = Res(), Res()

                    def fin(i, b0, rows):
                        xt = x1[i]
                        S.op("dve", lambda h: h.scalar_tensor_tensor(out=junk[:rows], in0=xt[:rows], scalar=1.0, in1=xt[:rows], op0=ALU.mult,
                                                                     op1=ALU.mult, accum_out=sm[:rows, 0:1]), reads=[rx1[i]], writes=[rj, rsm])
                        S.op("dve", lambda h: h.tensor_scalar(out=sm[:rows, 1:2], in0=sm[:rows, 0:1], scalar1=1.0 / D, scalar2=EPS, op0=ALU.mult,
                                                              op1=ALU.add), reads=[rsm], writes=[rsm])
                        S.op("act", lambda h: h.activation(out=sm[:rows, 2:3], in_=sm[:rows, 1:2], func=AF.Ln), reads=[rsm], writes=[rsm])
                        S.op("act", lambda h: h.activation(out=sm[:rows, 3:4], in_=sm[:rows, 2:3], func=AF.Exp, scale=-0.5), reads=[rsm], writes=[rsm])
                        S.op("dve", lambda h: h.scalar_tensor_tensor(out=xt[:rows], in0=xt[:rows], scalar=sm[:rows, 3:4], in1=gfin[:rows],
                                                                     op0=ALU.mult, op1=ALU.mult), reads=[rx1[i], rsm, rgfin], writes=[rx1[i]])
                        lo_ = max(b0, halo)
                        hi_ = min(b0 + rows, halo + nout)
                        if hi_ > lo_:
                            S.op("sp", lambda h: h.dma_start(out=ydst[lo_ - halo:hi_ - halo, :], in_=xt[lo_ - b0:hi_ - b0, :]),
                                 reads=[rx1[i]], writes=[r_scr["out"]], dsem=dl[4])
                    for i, (b0, rows) in enumerate(blks):
                        fin(i, b0, rows)
                    S.flush()
                    for rg in (wu, wdr):
                        rg.release()
                    S.putd(dl)

        import os as _os
        stop = int(_os.environ.get("MK_STOP", "1000"))
        stepc = [0]

        def go():
            stepc[0] += 1
            return stepc[0] <= stop
        if go():
            setup()
        for s in range(NS):
            if go():
                cache_import(s)
            if go():
                phaseA(ctx_s[s], xs[s], DSEQ, PAST, okv_s[s], 1 + s, DSEQ)
        if go():
            phaseA(ctx_p, xp, SEQ, 0, okv_p, 0, cfg.TT)

        kb_s = [(i * 128, 128) for i in range(PAST // 128)] + [(PAST, DSEQ)]
        for s in range(NS):
            if go():
                win_q(xs[s], DSEQ, 1 + s)
            if go():
                win_attn(ctx_s[s], DSEQ, kb_s, None, strip_s, cfg.ULs,
                         u0_of=lambda kbi: cfg.U0s - (128 * kbi - PAST),
                         near_of=lambda kbi: (128 * kbi - PAST) >= -NEAR - 127,
                         sbmask_of=lambda kbi: (sbmask_s_in[0:DSEQ, :] if kbi == PAST // 128 else None),
                         idxmask_of=lambda qi, kt: None)
            if go():
                win_ffn(xs[s], DSEQ, 1 + s, 0, y_s[s], DSEQ, cst[s], sconv[s], DSEQ, None)
        for m in range(NSLOT):
            kb_p = [(i * 128, min(128, cfg.kext[m] - i * 128)) for i in range(cfg.kextb[m])]
            if go():
                win_q(xw[m], ncols, 0)
            if go():
                win_attn(ctx_p, ncols, kb_p, m, strip_p, cfg.UL,
                         u0_of=(lambda m: lambda kbi: cfg.U0 - (128 * kbi - STRIDE * G * m + 2))(m),
                         near_of=(lambda m: lambda kbi: cfg.near(m, kbi))(m),
                         sbmask_of=(lambda m: lambda kbi: (sbmask_in[cfg.sbm_index[(m, kbi)], 0:kb_p_rows(cfg, m, kbi), :] if (m, kbi) in cfg.sbm_index else None))(m),
                         idxmask_of=(lambda m: lambda qi, kt: (idxmask_in[cfg.im_index[(m, qi, kt)]] if (m, qi, kt) in cfg.im_index else None))(m))
            last = (m == NSLOT - 1)
            ccol = (SEQ - 2) - (STRIDE * (G * m + G - 1) - 2)
            if go():
                win_ffn(xw[m], ncols, 0, 2, y_p[m], STRIDE, None, pconv if last else None, ccol, m)
        S.barrier()
        S.flush()
    return nc


def kb_p_rows(cfg, m, kbi):
    return min(128, cfg.kext[m] - kbi * 128)


def rel_bucket_np(rel):
    rel = np.asarray(rel, np.int64)
    nb = 16
    max_exact = 8
    ret = np.where(rel > 0, nb, 0)
    n = np.abs(rel)
    nf = np.maximum(n, 1).astype(np.float32)
    large = max_exact + (np.log(nf / np.float32(max_exact)) / np.float32(np.log(1024 / max_exact)) * np.float32(nb - max_exact)).astype(np.int32)
    large = np.minimum(large, nb - 1)
    return ret + np.where(n < max_exact, n, large)


def prep_cfg_tables(cfg):
    cfg.sbm_index = {}
    cfg.im_index = {}
    for m in range(cfg.NSLOT):
        for kb in range(cfg.kextb[m]):
            if cfg.sbmasked(m, kb):
                cfg.sbm_index[(m, kb)] = len(cfg.sbm_index)
        nqb = len(blocks_of(cfg.ncols))
        nkt = (cfg.kext[m] + 511) // 512
        for qi in range(nqb):
            for kt in range(nkt):
                if cfg.idxmasked(m, kt):
                    cfg.im_index[(m, qi, kt)] = len(cfg.im_index)
    cfg.n_sbm = max(1, len(cfg.sbm_index))
    cfg.n_im = max(1, len(cfg.im_index))


def core_tables(cfg, j):
    STRIDE, G, ncols = cfg.STRIDE, cfg.G, cfg.ncols
    bf = ml_dtypes.bfloat16
    sbm = np.zeros((cfg.n_sbm, 128, ncols), np.float32)
    for (m, kb), ix in cfg.sbm_index.items():
        qpos = STRIDE * (G * m + j) - 2 + np.arange(ncols)
        kpos = 128 * kb + np.arange(128)
        sbm[ix] = np.where(kpos[:, None] >= qpos[None, :], MASKV, 0.0)
    im = np.zeros((cfg.n_im, 128, 512), np.float32)
    qb = blocks_of(ncols)
    for (m, qi, kt), ix in cfg.im_index.items():
        q0, qrows = qb[qi]
        qpos = STRIDE * (G * m + j) - 2 + q0 + np.arange(128)
        lim = (qpos // 64 + 1) * 64
        kpos = 512 * kt + np.arange(512)
        im[ix] = np.where(kpos[None, :] >= lim[:, None], IMASKV, 0.0)
    i = np.arange(cfg.GL)
    r = (cfg.U0 + 127) - i - STRIDE * j
    b = rel_bucket_np(r)
    ohp = np.zeros((32, cfg.GL), np.float32)
    ohp[b, i] = 1.0
    i = np.arange(cfg.GLs)
    r = (cfg.U0s + 127) - i
    b = rel_bucket_np(r)
    ohs = np.zeros((32, cfg.GLs), np.float32)
    ohs[b, i] = 1.0
    sbs = np.zeros((128, DSEQ), np.float32)
    sbs[:DSEQ] = np.where(np.arange(DSEQ)[:, None] >= np.arange(DSEQ)[None, :], MASKV, 0.0)
    hflag = np.ones((128, cfg.NSLOT), np.float32)
    if j == 0:
        hflag[:, 0] = 0.0
    return {"sbmask": sbm.astype(bf), "idxmask": im.astype(bf), "oh_p": ohp, "oh_s": ohs, "sbmask_s": sbs.astype(bf), "hflag": hflag}


_NC_CACHE = {}


def run_cfg(cfg, inp):
    prep_cfg_tables(cfg)
    import os as _os
    key = (cfg.SEQ, cfg.NB, cfg.G, cfg.NSLOT, cfg.STRIDE, cfg.NS, cfg.TT, _os.environ.get('MK_STOP'), _os.environ.get('MK_QSKIP'))
    if key not in _NC_CACHE:
        _NC_CACHE[key] = build(cfg)
    nc = _NC_CACHE[key]
    SEQ, G, NSLOT, STRIDE, NS, ncols = cfg.SEQ, cfg.G, cfg.NSLOT, cfg.STRIDE, cfg.NS, cfg.ncols
    f32 = np.float32
    ident = np.eye(128, dtype=f32)
    tri = (np.arange(128)[:, None] >= np.arange(128)[None, :]).astype(f32)
    constf = np.stack([ident, -ident, tri, np.ones((128, 128), f32)], axis=1)
    cwb = np.concatenate([inp["conv_w"][0], inp["conv_b"][0][None]], axis=0)
    cwb = np.ascontiguousarray(cwb.reshape(4, NFF, 128).transpose(2, 1, 0))
    shared = {
        "w_ada": inp["w_ada"][0], "w_in": inp["w_in"][0], "w_gate": inp["w_gate"][0], "w_br_sb": inp["w_br_sb"][0],
        "w_br_sa": inp["w_br_sa"][0], "w_out": inp["w_out"][0], "w_up": inp["w_up"][0], "w_down": inp["w_down"][0],
        "g_mix": inp["g_mix"][0][None], "g_ffn": inp["g_ffn"][0][None], "g_final": inp["g_final"][None],
        "rel_table": inp["rel_table"], "cwb": cwb, "constf": constf,
        "wix": np.ascontiguousarray(inp["w_in"][0][:, C_WIX:C_WIX + 16].reshape(16, 128, 16).transpose(1, 0, 2)),
    }
    tabs = [core_tables(cfg, j) for j in range(G)]
    in_maps = []
    tot = STRIDE * G * NSLOT
    for c in range(cfg.ncores):
        b, j = c // G, c % G
        xpad = np.zeros((2 + max(tot, SEQ) + 2, D), f32)
        xpad[2:2 + SEQ] = inp["x_prompt"][b]
        xw = np.stack([xpad[STRIDE * (G * m + j):STRIDE * (G * m + j) + ncols] for m in range(NSLOT)])
        ss = slice(NS * c, NS * (c + 1))
        cvec = np.concatenate([inp["c_prompt"][b:b + 1], inp["c_sample"][ss]], axis=0)
        cT = np.ascontiguousarray(cvec.reshape(cfg.NR, 16, 128).transpose(2, 1, 0))
        st = inp["state_ffn_conv"][0, ss]
        cst = np.ascontiguousarray(st.reshape(NS, 2, NFF, 128).transpose(0, 3, 2, 1))
        m_ = dict(shared)
        m_.update(tabs[j])
        m_.update({
            "xp": inp["x_prompt"][b], "xw": xw, "xs": inp["x_sample"][ss],
            "c_sb_k": inp["cache_sb_k"][0, ss].reshape(NS, PAST, 1024), "c_sb_v": inp["cache_sb_v"][0, ss].reshape(NS, PAST, 1024),
            "c_sa_k": inp["cache_sa_k"][0, ss].reshape(NS, PAST, 1024), "c_sa_v": inp["cache_sa_v"][0, ss].reshape(NS, PAST, 1024),
            "c_ix": inp["cache_idx_k"][0, ss], "c_st": cst, "cT": cT,
            "b_ada_rows": np.repeat(inp["b_ada"][0][None], cfg.NR, axis=0),
        })
        in_maps.append({k: np.ascontiguousarray(v) for k, v in m_.items()})
    res = run_bass_kernel_spmd(nc, in_maps, core_ids=list(range(cfg.ncores))).results
    NB = cfg.NB
    y_prompt = np.zeros((NB, SEQ, D), f32)
    okv = np.zeros((NB, SEQ, KVC), f32)
    p_conv = np.zeros((1, NB, 2, DFF), f32)
    for c in range(cfg.ncores):
        b, j = c // G, c % G
        for m in range(NSLOT):
            p0 = STRIDE * (G * m + j)
            nv = min(STRIDE, SEQ - p0)
            if nv > 0:
                y_prompt[b, p0:p0 + nv] = res[c]["y_p"][m, :nv]
        q0, q1 = SEQ * j // G, SEQ * (j + 1) // G
        okv[b, q0:q1] = res[c]["okv_p"][q0:q1]
        if j == G - 1:
            p_conv[0, b] = res[c]["pconv"].transpose(2, 1, 0).reshape(2, DFF)
    nsb = cfg.ncores * NS
    y_sample = np.concatenate([res[c]["y_s"] for c in range(cfg.ncores)], axis=0)
    okvs = np.concatenate([res[c]["okv_s"] for c in range(cfg.ncores)], axis=0)
    s_conv = np.concatenate([res[c]["sconv"].transpose(0, 3, 2, 1).reshape(NS, 2, DFF) for c in range(cfg.ncores)], axis=0)[None]

    def split(o, L):
        nb_ = o.shape[0]
        return (o[..., 0:1024].reshape(1, nb_, L, NH, 128), o[..., 1024:2048].reshape(1, nb_, L, NH, 128),
                o[..., 2048:3072].reshape(1, nb_, L, NH, 128), o[..., 3072:4096].reshape(1, nb_, L, NH, 128),
                o[..., 4096:4160].reshape(1, nb_, L, 64))
    pk = split(okv, SEQ)
    sk = split(okvs, DSEQ)
    return (y_prompt, y_sample, pk[0], pk[1], pk[2], pk[3], pk[4], p_conv,
            sk[0], sk[1], sk[2], sk[3], sk[4], s_conv)


def kernel(**inputs):
    inp = {k: np.asarray(v) for k, v in inputs.items()}
    cfg = Cfg()
    out = run_cfg(cfg, inp)
    return tuple(np.ascontiguousarray(o, dtype=np.float32) for o in out)
```

```python
import numpy as np
import ml_dtypes
from contextlib import ExitStack
import concourse.bass as bass
import concourse.mybir as mybir
from concourse.bass_utils import run_bass_kernel_spmd

F32 = mybir.dt.float32
BF16 = mybir.dt.bfloat16
FP8 = mybir.dt.float8e4
AF = mybir.ActivationFunctionType
ALU = mybir.AluOpType
AX = mybir.AxisListType

D = 2048
DFF = 5632
NFF = DFF // 128
NH = 8
NHI = 16
INC = 7248
KVC = 4160
C_QSB, C_KSB, C_VSB, C_QSA, C_KSA, C_VSA, C_QIX, C_KIX, C_WIX = 0, 1024, 2048, 3072, 4096, 5120, 6144, 7168, 7232
EPS = 1e-6
PAST = 2048
DSEQ = 64
LKS = PAST + DSEQ
TOPK = 256
NBIS = 18
BRANGE = 32.0
NEAR = 640
SCALE = 128 ** -0.5
MASKV = -30000.0
IMASKV = -1024.0


class Cfg:
    def __init__(self, SEQ=16384, NB=2, G=4, NSLOT=9, STRIDE=456, NS=4, TT=1024):
        self.SEQ, self.NB, self.G, self.NSLOT, self.STRIDE, self.NS, self.TT = SEQ, NB, G, NSLOT, STRIDE, NS, TT
        self.ncols = STRIDE + 2
        self.ncores = NB * G
        self.NR = 1 + NS
        self.kext = [min(SEQ, STRIDE * (G * m + G)) for m in range(NSLOT)]
        self.kextb = [(k + 127) // 128 for k in self.kext]
        dmax = max(128 * (self.kextb[m] - 1) - STRIDE * G * m + 2 for m in range(NSLOT))
        self.U0 = dmax
        self.dmin = -NEAR - 127 - 128
        self.UL = self.U0 - self.dmin + self.ncols
        self.GL = self.UL + 128
        self.U0s = 128 * 16 - PAST
        self.dmins = -NEAR - 127 - 128
        self.ULs = self.U0s - self.dmins + DSEQ
        self.GLs = self.ULs + 128

    def near(self, m, kb):
        return 128 * kb - self.STRIDE * self.G * m + 2 >= -NEAR - 127

    def sbmasked(self, m, kb):
        return 128 * kb + 127 >= self.STRIDE * self.G * m - 2

    def idxmasked(self, m, kt):
        lim = ((self.STRIDE * self.G * m - 2) // 64 + 1) * 64
        return 512 * kt + 511 >= lim


def blocks_of(n):
    out = []
    o = 0
    while o < n:
        out.append((o, min(128, n - o)))
        o += 128
    return out


class Res:
    __slots__ = ("w", "r", "excl")

    def __init__(self, excl=False):
        self.w = None
        self.r = []
        self.excl = excl


class DSem:
    def __init__(self, sem):
        self.sem = sem
        self.count = 0


class Sched:
    ENGS = ("pe", "act", "dve", "pool", "sp")

    def __init__(self, nc, stack, ndsem):
        self.nc = nc
        self.ops = {e: [] for e in self.ENGS}
        self.sem = {}
        self.count = {e: 0 for e in self.ENGS}
        self.known = {e: {} for e in self.ENGS}
        for e in ("pe", "act", "dve", "pool"):
            self.sem[e] = stack.enter_context(nc.semaphore("sem_" + e))
        self.dsems = [DSem(stack.enter_context(nc.semaphore("dsem%d" % i))) for i in range(ndsem)]
        self.free = list(self.dsems[:-12])
        self.free_sw = list(self.dsems[-12:])
        self.dmap = {id(d.sem): d for d in self.dsems}

    def getd(self, sw=False):
        return self.free_sw.pop() if sw else self.free.pop()

    def putd(self, ds, sw=False):
        (self.free_sw if sw else self.free).extend(ds)

    def _deps(self, eng, reads, writes):
        deps = {}

        def add(tok):
            if tok is None:
                return
            key = id(tok[0])
            d = self.dmap.get(key)
            if d is not None:
                tok = (tok[0], d.count)
            if key not in deps or deps[key][1] < tok[1]:
                deps[key] = tok
        for r in reads:
            add(r.w)
            if r.excl:
                for t in r.r:
                    add(t)
        for w in writes:
            add(w.w)
            for t in w.r:
                add(t)
        out = []
        kn = self.known[eng]
        for key, tok in deps.items():
            if kn.get(key, 0) >= tok[1]:
                continue
            kn[key] = tok[1]
            out.append(tok)
        return out

    def op(self, eng, fn, reads=(), writes=(), dsem=None):
        waits = self._deps(eng, reads, writes)
        if dsem is not None and dsem.count > 0:
            key = id(dsem.sem)
            if self.known[eng].get(key, 0) < dsem.count:
                self.known[eng][key] = dsem.count
                waits = [w for w in waits if id(w[0]) != key] + [(dsem.sem, dsem.count)]
        if dsem is None:
            self.count[eng] += 1
            tok = (self.sem[eng], self.count[eng])
            inc = (self.sem[eng], 1)
        else:
            dsem.count += 16
            tok = (dsem.sem, dsem.count)
            inc = (dsem.sem, 16)
        self.ops[eng].append((waits, fn, inc))
        for r in reads:
            if len(r.r) > 24:
                r.r = r.r[-24:]
            r.r.append(tok)
        for w in writes:
            w.w = tok
            w.r = []
        return tok

    def barrier(self):
        toks = [(self.sem[e], self.count[e]) for e in ("pe", "act", "dve", "pool") if self.count[e] > 0]
        toks += [(d.sem, d.count) for d in self.dsems if d.count > 0]
        for e in self.ENGS:
            kn = self.known[e]
            waits = []
            for tok in toks:
                key = id(tok[0])
                if kn.get(key, 0) >= tok[1]:
                    continue
                kn[key] = tok[1]
                waits.append(tok)
            if waits:
                self.ops[e].append((waits, None, None))

    def flush(self):
        self.barrier()
        ops = self.ops
        self.ops = {e: [] for e in self.ENGS}

        def mk(e):
            def body(h):
                for waits, fn, inc in ops[e]:
                    for (s, v) in waits:
                        h.wait_ge(s, v)
                    if fn is not None:
                        inst = fn(h)
                        inst.then_inc(inc[0], inc[1])
            return body
        with self.nc.Block() as block:
            block.tensor(mk("pe"))
            block.scalar(mk("act"))
            block.vector(mk("dve"))
            block.gpsimd(mk("pool"))
            block.sync(mk("sp"))


class Ring:
    uid = 0

    def __init__(self, S, nc, st, name, shape, dt, n, with_dsem=True, sw_dsem=False):
        self.S = S
        self.d2 = [S.getd(sw=True) for _ in range(n)] if sw_dsem else None
        Ring.uid += 1
        self.t = [st.enter_context(nc.sbuf_tensor("%s%d_r%d" % (name, i, Ring.uid), shape, dt)) for i in range(n)]
        self.r = [Res() for _ in range(n)]
        self.d = [S.getd() for _ in range(n)] if with_dsem else None
        self.n = n
        self.i = 0

    def nxt(self):
        i = self.i % self.n
        self.i += 1
        return self.t[i], self.r[i], (self.d[i] if self.d else None)

    def release(self):
        if self.d:
            self.S.putd(self.d)
        if self.d2:
            self.S.putd(self.d2, sw=True)


class Pre:
    def __init__(self, thunks, depth):
        self.th = thunks
        self.depth = depth
        self.nxt_ = 0
        self.got = {}

    def get(self, i):
        while self.nxt_ < len(self.th) and self.nxt_ <= i + self.depth:
            self.got[self.nxt_] = self.th[self.nxt_]()
            self.nxt_ += 1
        return self.got.pop(i)


def build(cfg):
    nc = bass.Bass("TRN2", target_bir_lowering=False)
    SEQ, G, NSLOT, STRIDE, NS, NR, ncols = cfg.SEQ, cfg.G, cfg.NSLOT, cfg.STRIDE, cfg.NS, cfg.NR, cfg.ncols

    def din(name, shape, dt=F32):
        return nc.dram_tensor(name, list(shape), dt, kind="ExternalInput").ap()

    def dout(name, shape, dt=F32):
        return nc.dram_tensor(name, list(shape), dt, kind="ExternalOutput").ap()

    def dscr(name, shape, dt=BF16):
        return nc.dram_tensor(name, list(shape), dt, kind="Internal").ap()

    xp = din("xp", [SEQ, D])
    xw = din("xw", [NSLOT, ncols, D])
    xs = din("xs", [NS, DSEQ, D])
    cache = [din(n, [NS, PAST, 1024]) for n in ("c_sb_k", "c_sb_v", "c_sa_k", "c_sa_v")]
    cix = din("c_ix", [NS, PAST, 64])
    cst = din("c_st", [NS, 128, NFF, 2])
    cT = din("cT", [128, 16, NR])
    w_f32 = {
        "ada": din("w_ada", [D, 6 * D]), "in": din("w_in", [D, INC]), "gate": din("w_gate", [D, 2 * D]),
        "brsb": din("w_br_sb", [1024, D]), "brsa": din("w_br_sa", [1024, D]), "out": din("w_out", [D, D]),
        "up": din("w_up", [D, 2 * DFF]), "down": din("w_down", [DFF, D]),
    }
    b_ada_rows = din("b_ada_rows", [NR, 6 * D])
    g_mix = din("g_mix", [1, D])
    g_ffn = din("g_ffn", [1, D])
    g_final = din("g_final", [1, D])
    rel_table = din("rel_table", [32, 8])
    cwb_in = din("cwb", [128, NFF, 4])
    constf = din("constf", [128, 4, 128])
    sbmask_in = din("sbmask", [cfg.n_sbm, 128, ncols], BF16)
    idxmask_in = din("idxmask", [cfg.n_im, 128, 512], BF16)
    sbmask_s_in = din("sbmask_s", [128, DSEQ], BF16)
    oh_p = din("oh_p", [32, cfg.GL])
    oh_s = din("oh_s", [32, cfg.GLs])
    hflag_in = din("hflag", [128, NSLOT])
    wix_in = din("wix", [128, 16, 16])

    y_p = dout("y_p", [NSLOT, STRIDE, D])
    y_s = dout("y_s", [NS, DSEQ, D])
    okv_p = dout("okv_p", [SEQ, KVC])
    okv_s = dout("okv_s", [NS, DSEQ, KVC])
    pconv = dout("pconv", [128, NFF, 2])
    sconv = dout("sconv", [NS, 128, NFF, 2])

    wbf = {k: dscr("wbf_" + k, v.shape) for k, v in w_f32.items()}
    modrow = dscr("modrow", [NR, 6 * D], F32)

    def mkctx(name, Lk):
        return {"KT": [dscr(name + "_ktsb", [NH, 128, Lk]), dscr(name + "_ktsa", [NH, 128, Lk])],
                "V": [dscr(name + "_vsb", [Lk, 1024]), dscr(name + "_vsa", [Lk, 1024])],
                "KI2": dscr(name + "_ki2", [128, Lk]), "Lk": Lk}
    ctx_p = mkctx("cp", SEQ)
    ctx_s = [mkctx("cs%d" % s, LKS) for s in range(NS)]
    gvec_p = dscr("gvec_p", [8, cfg.GL])
    gvec_s = dscr("gvec_s", [8, cfg.GLs])
    strip_p = dscr("strip_p", [8, 128, cfg.UL])
    strip_s = dscr("strip_s", [8, 128, cfg.ULs])
    QTs = dscr("QTs", [4, 8, 128, ncols])
    hTs = dscr("hTs", [16, 128, ncols])
    OTs = dscr("OTs", [2, 8, 128, ncols])

    with ExitStack() as gst:
        S = Sched(nc, gst, 80)

        uid = [0]

        def PT(name, shape, dt=F32, st=gst):
            uid[0] += 1
            return st.enter_context(nc.sbuf_tensor("%s_u%d" % (name, uid[0]), list(shape), dt))

        identf = PT("identf", [128, 128])
        cbf = PT("cbf", [128, 4, 128], BF16)
        cwb = PT("cwb_t", [128, NFF, 4])
        CH = PT("CH", [128, 8])
        hflag = PT("hflag_t", [128, NSLOT])
        wraw = PT("wraw", [128, 4, NHI])
        convc = PT("convc", [128, NFF, 2])
        r_const = Res()
        r_wraw = Res()
        r_convc = Res()
        identb = cbf[:, 0, :]
        nidentb = cbf[:, 1, :]
        trib = cbf[:, 2, :]
        onesb = cbf[:, 3, :]
        PB = [gst.enter_context(nc.psum_tensor("pb%d" % i, [128, 512], F32)) for i in range(8)]
        RB = [Res(excl=True) for _ in range(8)]
        r_scr = {"modrow": Res(), "wbf": Res(), "strip": Res(), "QT": Res(), "hT": Res(), "OT": Res(), "ctx": Res(),
                 "out": Res()}

        def setup():
            with ExitStack() as st:
                dl = [S.getd() for _ in range(6)]
                dsw = S.getd(sw=True)
                for k in ("in", "ada", "gate", "brsb", "brsa", "out", "up", "down"):
                    src = w_f32[k]
                    n = src.shape[0] * src.shape[1] // 2048
                    s2 = src.rearrange("r c -> (r c)").rearrange("(n k) -> n k", k=2048)
                    d2 = wbf[k].rearrange("r c -> (r c)").rearrange("(n k) -> n k", k=2048)
                    o = 0
                    while o < n:
                        m = min(4096, n - o)
                        S.op("pool", (lambda a, b: lambda h: h.dma_start(out=a, in_=b))(d2[o:o + m], s2[o:o + m]),
                             writes=[Res()], dsem=dsw)
                        o += m
                ctmp = PT("ctmp", [128, 4, 128], F32, st)
                rt = Res()
                S.op("sp", lambda h: h.dma_start(out=ctmp[:], in_=constf), writes=[rt], dsem=dl[1])
                S.op("sp", lambda h: h.dma_start(out=identf[:], in_=constf[:, 0, :]), writes=[r_const], dsem=dl[2])
                S.op("sp", lambda h: h.dma_start(out=cwb[:], in_=cwb_in), writes=[r_const], dsem=dl[2])
                S.op("sp", lambda h: h.dma_start(out=hflag[:], in_=hflag_in), writes=[r_const], dsem=dl[2])
                S.op("sp", lambda h: h.dma_start(out=CH[:], in_=rel_table[15:16, :].to_broadcast([128, 8])),
                     writes=[r_const], dsem=dl[2])
                S.op("dve", lambda h: h.tensor_copy(out=cbf[:], in_=ctmp[:]), reads=[rt], writes=[r_const])
                cTt = PT("cTt", [128, 16, NR], F32, st)
                sT = PT("sT", [128, 16, NR], BF16, st)
                rc, rs = Res(), Res()
                S.op("sp", lambda h: h.dma_start(out=cTt[:], in_=cT), writes=[rc], dsem=dl[1])
                S.op("act", lambda h: h.activation(out=sT[:], in_=cTt[:], func=AF.Silu), reads=[rc], writes=[rs])
                relt = PT("relt", [32, 8], F32, st)
                rr = Res()
                S.op("sp", lambda h: h.dma_start(out=relt[:], in_=rel_table), writes=[rr], dsem=dl[1])
                def mkstrip(oh, gv, GLx, strip, ULx, nm):
                    oht = PT("oht" + nm, [32, GLx], F32, st)
                    gst_t = PT("gst" + nm, [8, GLx], BF16, st)
                    ro, rg, rgv = Res(), Res(), Res()
                    S.op("sp", (lambda a, b: lambda h: h.dma_start(out=a, in_=b))(oht[:], oh), writes=[ro], dsem=dl[1])
                    o = 0
                    i = 0
                    while o < GLx:
                        m = min(512, GLx - o)
                        bk = 6 + (i % 2)
                        S.op("pe", (lambda bk, o, m: lambda h: h.matmul(PB[bk][:8, :m], relt[:], oht[:, o:o + m],
                                                                         start=True, stop=True))(bk, o, m),
                             reads=[rr, ro], writes=[RB[bk]])
                        S.op("act", (lambda bk, o, m: lambda h: h.activation(out=gst_t[:, o:o + m], in_=PB[bk][:8, :m],
                                                                             func=AF.Copy))(bk, o, m),
                             reads=[RB[bk]], writes=[rg])
                        o += m
                        i += 1
                    S.op("sp", (lambda a, b: lambda h: h.dma_start(out=a, in_=b))(gv, gst_t[:]), reads=[rg], writes=[rgv],
                         dsem=dl[3])
                    for p in range(128):
                        S.op("sp", (lambda p, strip, gv, ULx: lambda h: h.dma_start(
                            out=strip[:, p, :], in_=gv[:, 127 - p:127 - p + ULx]))(p, strip, gv, ULx),
                            reads=[rgv], writes=[Res()], dsem=dl[4])
                mkstrip(oh_p, gvec_p, cfg.GL, strip_p, cfg.UL, "p")
                mkstrip(oh_s, gvec_s, cfg.GLs, strip_s, cfg.ULs, "s")
                S.flush()
                wr = Ring(S, nc, st, "wada", [128, 16, 512], BF16, 2)
                br = Ring(S, nc, st, "bada", [NR, 512], F32, 2)
                ms = Ring(S, nc, st, "mst", [NR, 512], F32, 2)
                wv = wbf["ada"].rearrange("(kc p) c -> p kc c", p=128)
                for pc in range(24):
                    wt, wres, wd = wr.nxt()
                    bt, bres, bd = br.nxt()
                    mt, mres, md = ms.nxt()
                    S.op("sp", (lambda wt, pc: lambda h: h.dma_start(out=wt[:], in_=wv[:, :, pc * 512:(pc + 1) * 512]))(wt, pc),
                         writes=[wres], dsem=wd)
                    S.op("sp", (lambda bt, pc: lambda h: h.dma_start(out=bt[:], in_=b_ada_rows[:, pc * 512:(pc + 1) * 512]))(bt, pc),
                         writes=[bres], dsem=bd)
                    bk = pc % 2

                    def mmf(h, wt=wt, bk=bk):
                        for kc in range(16):
                            ins = h.matmul(PB[bk][:NR, :], sT[:, kc, :], wt[:, kc, :], start=(kc == 0), stop=(kc == 15))
                        return ins
                    S.op("pe", mmf, reads=[rs, wres], writes=[RB[bk]])
                    S.op("dve", (lambda mt, bt, bk: lambda h: h.tensor_tensor(out=mt[:], in0=PB[bk][:NR, :], in1=bt[:],
                                                                             op=ALU.add))(mt, bt, bk),
                         reads=[RB[bk], bres], writes=[mres])
                    S.op("sp", (lambda mt, pc: lambda h: h.dma_start(out=modrow[:, pc * 512:(pc + 1) * 512], in_=mt[:]))(mt, pc),
                         reads=[mres], writes=[Res()], dsem=md)
                S.flush()
                wr.release(); br.release(); ms.release()
                S.putd(dl)
                S.putd([dsw], sw=True)

        def load_mod(st, dl, r, which):
            A = PT("modA", [128, D], F32, st)
            Bt = PT("modB", [128, D], F32, st)
            Gt = PT("modG", [128, D], F32, st)
            rA, rB, rG = Res(), Res(), Res()
            base = 0 if which == 1 else 3 * D
            g = g_mix if which == 1 else g_ffn
            S.op("sp", lambda h: h.dma_start(out=A[:], in_=modrow[r:r + 1, base + D:base + 2 * D].to_broadcast([128, D])),
                 writes=[rA], dsem=dl[0])
            S.op("sp", lambda h: h.dma_start(out=Bt[:], in_=modrow[r:r + 1, base:base + D].to_broadcast([128, D])),
                 writes=[rB], dsem=dl[1])
            S.op("sp", lambda h: h.dma_start(out=Gt[:], in_=g.to_broadcast([128, D])), writes=[rG], dsem=dl[2])
            S.op("dve", lambda h: h.scalar_tensor_tensor(out=A[:], in0=A[:], scalar=1.0, in1=Gt[:], op0=ALU.add, op1=ALU.mult),
                 reads=[rA, rG], writes=[rA])
            return A, Bt, rA, rB

        def load_row_rep(st, dl, name, src_row):
            t = PT(name, [128, D], F32, st)
            rr = Res()
            S.op("sp", lambda h: h.dma_start(out=t[:], in_=src_row.to_broadcast([128, D])), writes=[rr], dsem=dl)
            return t, rr

        class NormT:
            def __init__(self, st, name):
                self.junk = PT(name + "_junk", [128, D], BF16, st)
                self.hb = [PT(name + "_hb%d" % i, [128, D], F32, st) for i in range(2)]
                self.rhb = [Res(), Res()]
                self.sm = PT(name + "_sm", [128, 8], F32, st)
                self.rj = Res()
                self.rsm = Res()
                self.k = 0

            def run(self, xt, rx, rows, A, Bt, rA, rB, dst, rdst, c0):
                k = self.k
                self.k += 1
                hb, rhb = self.hb[k % 2], self.rhb[k % 2]
                sm, rsm, junk, rj = self.sm, self.rsm, self.junk, self.rj
                S.op("dve", lambda h: h.scalar_tensor_tensor(out=junk[:rows], in0=xt[:rows], scalar=1.0, in1=xt[:rows],
                                                             op0=ALU.mult, op1=ALU.mult, accum_out=sm[:rows, 0:1]),
                     reads=[rx], writes=[rj, rsm])
                S.op("dve", lambda h: h.tensor_scalar(out=sm[:rows, 1:2], in0=sm[:rows, 0:1], scalar1=1.0 / D, scalar2=EPS,
                                                      op0=ALU.mult, op1=ALU.add), reads=[rsm], writes=[rsm])
                S.op("act", lambda h: h.activation(out=sm[:rows, 2:3], in_=sm[:rows, 1:2], func=AF.Ln), reads=[rsm], writes=[rsm])
                S.op("act", lambda h: h.activation(out=sm[:rows, 3:4], in_=sm[:rows, 2:3], func=AF.Exp, scale=-0.5),
                     reads=[rsm], writes=[rsm])
                S.op("dve", lambda h: h.scalar_tensor_tensor(out=hb[:rows], in0=xt[:rows], scalar=sm[:rows, 3:4], in1=A[:rows],
                                                             op0=ALU.mult, op1=ALU.mult), reads=[rx, rsm, rA], writes=[rhb])
                S.op("pool", lambda h: h.tensor_tensor(out=hb[:rows], in0=hb[:rows], in1=Bt[:rows], op=ALU.add),
                     reads=[rhb, rB], writes=[rhb])
                for g4 in range(4):
                    bk = 6 + (g4 % 2)

                    def tp(h, g4=g4, bk=bk):
                        for q in range(4):
                            fc = g4 * 4 + q
                            ins = h.transpose(out=PB[bk][:, q * 128:q * 128 + rows], in_=hb[:rows, fc * 128:(fc + 1) * 128],
                                              identity=identf[:rows, :rows])
                        return ins
                    S.op("pe", tp, reads=[rhb, r_const], writes=[RB[bk]])
                    src = PB[bk][:, :].rearrange("p (q c) -> p q c", q=4)[:, :, 0:rows]
                    S.op("act", (lambda g4, src: lambda h: h.activation(out=dst[:, g4 * 4:(g4 + 1) * 4, c0:c0 + rows], in_=src,
                                                                        func=AF.Copy))(g4, src),
                         reads=[RB[bk]], writes=[rdst])

        KVBLK = [(C_KSB, 512, "k", 0, 0, 0), (C_KSB + 512, 512, "k", 0, 4, 512),
                 (C_VSB, 512, "v", 0, 0, 1024), (C_VSB + 512, 512, "v", 0, 4, 1536),
                 (C_KSA, 512, "k", 1, 0, 2048), (C_KSA + 512, 512, "k", 1, 4, 2560),
                 (C_VSA, 512, "v", 1, 0, 3072), (C_VSA + 512, 512, "v", 1, 4, 3584),
                 (C_KIX, 64, "i", 0, 0, 4096)]

        def phaseA(ctx, xsrc, ntok, tok0, okv, r, TT):
            with ExitStack() as st:
                dl = [S.getd() for _ in range(4)]
                A, Bt, rA, rB = load_mod(st, dl, r, 1)
                nt = NormT(st, "na")
                ncol_t = min(TT, ntok)
                hT = PT("a_hT", [128, 16, ncol_t], BF16, st)
                rhT = Res()
                xr = Ring(S, nc, st, "a_x", [128, D], F32, 3)
                wr = Ring(S, nc, st, "a_w", [128, 16, 512], BF16, 3)
                sf = Ring(S, nc, st, "a_sf", [128, 512], F32, 4, sw_dsem=True)
                kts = Ring(S, nc, st, "a_kt", [128, 4, ncol_t], BF16, 2)
                ki2 = Ring(S, nc, st, "a_ki2", [128, ncol_t], BF16, 2)
                kid = Ring(S, nc, st, "a_kid", [128, 128], F32, 2, with_dsem=False)
                win = wbf["in"].rearrange("(kc p) c -> p kc c", p=128)
                mmk = 0
                ntiles = (ntok + TT - 1) // TT

                def wthunk(wc0, wn):
                    def f():
                        wt, wres, wd = wr.nxt()
                        S.op("sp", lambda h: h.dma_start(out=wt[:, :, :wn], in_=win[:, :, wc0:wc0 + wn]),
                             reads=[r_scr["wbf"]], writes=[wres], dsem=wd)
                        return wt, wres
                    return f
                wpre = Pre([wthunk(b[0], b[1]) for _ in range(ntiles) for b in KVBLK], 1)
                wi = 0
                for t0 in range(0, ntok, TT):
                    n = min(TT, ntok - t0)
                    blks = blocks_of(n)
                    for (b0, rows) in blks:
                        xt, rx, xd = xr.nxt()
                        S.op("sp", (lambda xt, a, rows: lambda h: h.dma_start(out=xt[:rows], in_=a))(xt, xsrc[t0 + b0:t0 + b0 + rows, :], rows),
                             writes=[rx], dsem=xd)
                        nt.run(xt, rx, rows, A, Bt, rA, rB, hT, rhT, b0)
                    for (wc0, wn, kind, which, head0, oc0) in KVBLK:
                        wt, wres = wpre.get(wi)
                        wi += 1
                        if kind == "k":
                            kt, rkt, kd = kts.nxt()
                        if kind == "i":
                            k2, rk2, k2d = ki2.nxt()
                        for (b0, rows) in blks:
                            bk = mmk % 3
                            mmk += 1

                            def mmf(h, wt=wt, bk=bk, b0=b0, rows=rows, wn=wn):
                                for kc in range(16):
                                    ins = h.matmul(PB[bk][:rows, :wn], hT[:, kc, b0:b0 + rows], wt[:, kc, :wn],
                                                   start=(kc == 0), stop=(kc == 15))
                                return ins
                            S.op("pe", mmf, reads=[rhT, wres], writes=[RB[bk]])
                            sft, rsf, sfd = sf.nxt()
                            sfd2 = sf.d2[(sf.i - 1) % sf.n]
                            eng = "act" if (mmk % 2) else "dve"
                            if eng == "act":
                                S.op("act", (lambda sft, bk, rows, wn: lambda h: h.activation(out=sft[:rows, :wn], in_=PB[bk][:rows, :wn],
                                                                                              func=AF.Copy))(sft, bk, rows, wn),
                                     reads=[RB[bk]], writes=[rsf])
                            else:
                                S.op("dve", (lambda sft, bk, rows, wn: lambda h: h.tensor_copy(out=sft[:rows, :wn], in_=PB[bk][:rows, :wn]))(sft, bk, rows, wn),
                                     reads=[RB[bk]], writes=[rsf])
                            S.op("sp", (lambda sft, rows, wn, a: lambda h: h.dma_start(out=a, in_=sft[:rows, :wn]))(
                                sft, rows, wn, okv[t0 + b0:t0 + b0 + rows, oc0:oc0 + wn]),
                                reads=[rsf], writes=[Res()], dsem=sfd)
                            if kind == "v":
                                vdst = ctx["V"][which][tok0 + t0 + b0:tok0 + t0 + b0 + rows, head0 * 128:head0 * 128 + 512]
                                S.op("pool", (lambda sft, rows, a: lambda h: h.dma_start(out=a, in_=sft[:rows, :]))(sft, rows, vdst),
                                     reads=[rsf], writes=[Res()], dsem=sfd2)
                            elif kind == "k":
                                bk2 = 3 + (mmk % 2)

                                def tpf(h, sft=sft, bk2=bk2, rows=rows):
                                    for q in range(4):
                                        ins = h.transpose(out=PB[bk2][:, q * 128:q * 128 + rows], in_=sft[:rows, q * 128:(q + 1) * 128],
                                                          identity=identf[:rows, :rows])
                                    return ins
                                S.op("pe", tpf, reads=[rsf, r_const], writes=[RB[bk2]])
                                src = PB[bk2][:, :].rearrange("p (q c) -> p q c", q=4)[:, :, 0:rows]
                                S.op("dve", (lambda kt, src, b0, rows: lambda h: h.tensor_copy(out=kt[:, :, b0:b0 + rows], in_=src))(kt, src, b0, rows),
                                     reads=[RB[bk2]], writes=[rkt])
                            else:
                                kdt, rkd, _ = kid.nxt()
                                S.op("pool", (lambda kdt, sft, rows: lambda h: h.tensor_copy(out=kdt[:rows, 0:64], in_=sft[:rows, 0:64]))(kdt, sft, rows),
                                     reads=[rsf], writes=[rkd])
                                S.op("pool", (lambda kdt, sft, rows: lambda h: h.tensor_copy(out=kdt[:rows, 64:128], in_=sft[:rows, 0:64]))(kdt, sft, rows),
                                     reads=[rsf, rkd], writes=[rkd])
                                bk2 = 5
                                S.op("pe", (lambda kdt, rows: lambda h: h.transpose(out=PB[5][:, :rows], in_=kdt[:rows, :],
                                                                                    identity=identf[:rows, :rows]))(kdt, rows),
                                     reads=[rkd, r_const], writes=[RB[5]])
                                S.op("dve", (lambda k2, b0, rows: lambda h: h.tensor_copy(out=k2[:, b0:b0 + rows], in_=PB[5][:, :rows]))(k2, b0, rows),
                                     reads=[RB[5]], writes=[rk2])
                        if kind == "k":
                            S.op("sp", (lambda kt, a, n: lambda h: h.dma_start(out=a, in_=kt[:, :, :n]))(
                                kt, ctx["KT"][which][head0:head0 + 4, :, tok0 + t0:tok0 + t0 + n].rearrange("q p c -> p q c"), n),
                                reads=[rkt], writes=[Res()], dsem=kd)
                        if kind == "i":
                            S.op("sp", (lambda k2, a, n: lambda h: h.dma_start(out=a, in_=k2[:, :n]))(
                                k2, ctx["KI2"][:, tok0 + t0:tok0 + t0 + n], n), reads=[rk2], writes=[Res()], dsem=k2d)
                S.flush()
                for rg in (xr, wr, sf, kts, ki2):
                    rg.release()
                S.putd(dl)

        def cache_import(s):
            ctx = ctx_s[s]
            with ExitStack() as st:
                dl = [S.getd() for _ in range(2)]
                dsw = [S.getd(sw=True) for _ in range(2)]
                for which, ci in ((0, 1), (1, 3)):
                    S.op("pool", (lambda a, b: lambda h: h.dma_start(out=a, in_=b))(ctx["V"][which][0:PAST, :], cache[ci][s]),
                         writes=[Res()], dsem=dsw[which])
                cr = Ring(S, nc, st, "ci_c", [128, 1024], F32, 3)
                kst = Ring(S, nc, st, "ci_k", [128, 8, 512], BF16, 2)
                for which, ci in ((0, 0), (1, 2)):
                    for g4 in range(PAST // 512):
                        kt, rkt, kd = kst.nxt()
                        for bb in range(4):
                            tb = g4 * 4 + bb
                            ct, rct, cd = cr.nxt()
                            S.op("sp", (lambda ct, a: lambda h: h.dma_start(out=ct[:], in_=a))(ct, cache[ci][s, tb * 128:(tb + 1) * 128, :]),
                                 writes=[rct], dsem=cd)
                            for hh in range(2):
                                bk = (tb * 2 + hh) % 4

                                def tpf(h, ct=ct, bk=bk, hh=hh):
                                    for q in range(4):
                                        hd = hh * 4 + q
                                        ins = h.transpose(out=PB[bk][:, q * 128:(q + 1) * 128], in_=ct[:, hd * 128:(hd + 1) * 128],
                                                          identity=identf[:])
                                    return ins
                                S.op("pe", tpf, reads=[rct, r_const], writes=[RB[bk]])
                                src = PB[bk][:, :].rearrange("p (q c) -> p q c", q=4)
                                eng = "act" if hh else "dve"
                                if eng == "act":
                                    S.op("act", (lambda kt, src, hh, bb: lambda h: h.activation(out=kt[:, hh * 4:hh * 4 + 4, bb * 128:(bb + 1) * 128],
                                                                                               in_=src, func=AF.Copy))(kt, src, hh, bb),
                                         reads=[RB[bk]], writes=[rkt])
                                else:
                                    S.op("dve", (lambda kt, src, hh, bb: lambda h: h.tensor_copy(out=kt[:, hh * 4:hh * 4 + 4, bb * 128:(bb + 1) * 128],
                                                                                                in_=src))(kt, src, hh, bb),
                                         reads=[RB[bk]], writes=[rkt])
                        S.op("sp", (lambda kt, a: lambda h: h.dma_start(out=a, in_=kt[:]))(
                            kt, ctx["KT"][which][:, :, g4 * 512:(g4 + 1) * 512].rearrange("q p c -> p q c")), reads=[rkt], writes=[Res()], dsem=kd)
                ir = Ring(S, nc, st, "ci_i", [128, 128], F32, 3)
                i2 = Ring(S, nc, st, "ci_i2", [128, 512], BF16, 2)
                for g4 in range(PAST // 512):
                    k2, rk2, k2d = i2.nxt()
                    for bb in range(4):
                        tb = g4 * 4 + bb
                        it, rit, idd = ir.nxt()
                        S.op("sp", (lambda it, a: lambda h: h.dma_start(out=it[:, 0:64], in_=a))(it, cix[s, tb * 128:(tb + 1) * 128, :]),
                             writes=[rit], dsem=idd)
                        S.op("sp", (lambda it, a: lambda h: h.dma_start(out=it[:, 64:128], in_=a))(it, cix[s, tb * 128:(tb + 1) * 128, :]),
                             writes=[rit], dsem=idd)
                        S.op("pe", (lambda it: lambda h: h.transpose(out=PB[5][:, :128], in_=it[:], identity=identf[:]))(it),
                             reads=[rit, r_const], writes=[RB[5]])
                        S.op("dve", (lambda k2, bb: lambda h: h.tensor_copy(out=k2[:, bb * 128:(bb + 1) * 128], in_=PB[5][:, :128]))(k2, bb),
                             reads=[RB[5]], writes=[rk2])
                    S.op("sp", (lambda k2, a: lambda h: h.dma_start(out=a, in_=k2[:]))(k2, ctx["KI2"][:, g4 * 512:(g4 + 1) * 512]),
                         reads=[rk2], writes=[Res()], dsem=k2d)
                S.flush()
                for rg in (cr, kst, ir, i2):
                    rg.release()
                S.putd(dl)
                S.putd(dsw, sw=True)

        def win_q(xsrc, n, r):
            with ExitStack() as st:
                dl = [S.getd() for _ in range(4)]
                A, Bt, rA, rB = load_mod(st, dl, r, 1)
                nt = NormT(st, "nq")
                hT = PT("q_hT", [128, 16, n], BF16, st)
                rhT = Res()
                xr = Ring(S, nc, st, "q_x", [128, D], F32, 3)
                wr = Ring(S, nc, st, "q_w", [128, 16, 512], BF16, 2)
                qs = Ring(S, nc, st, "q_s", [128, n], BF16, 4)
                win = wbf["in"].rearrange("(kc p) c -> p kc c", p=128)
                blks = blocks_of(n)
                for (b0, rows) in blks:
                    xt, rx, xd = xr.nxt()
                    S.op("sp", (lambda xt, a, rows: lambda h: h.dma_start(out=xt[:rows], in_=a))(xt, xsrc[b0:b0 + rows, :], rows),
                         writes=[rx], dsem=xd)
                    nt.run(xt, rx, rows, A, Bt, rA, rB, hT, rhT, b0)
                import os as _os
                qskip = int(_os.environ.get("MK_QSKIP", "0"))
                if not (qskip & 1):
                    S.op("sp", lambda h: h.dma_start(out=hTs[:, :, :n].rearrange("a p c -> p a c"), in_=hT[:]), reads=[rhT],
                         writes=[r_scr["hT"]], dsem=dl[3])
                k = 0
                for (kind, c0, scale) in (((0, C_QSB, SCALE), (2, C_QSA, SCALE), (3, C_QIX, 0.125)) if not (qskip & 2) else ()):
                    for hg in range(2):
                        wt, wres, wd = wr.nxt()
                        S.op("sp", (lambda wt, c: lambda h: h.dma_start(out=wt[:], in_=win[:, :, c:c + 512]))(wt, c0 + hg * 512),
                             reads=[r_scr["wbf"]], writes=[wres], dsem=wd)
                        for h4 in range(4):
                            hd = hg * 4 + h4
                            bk = k % 3
                            k += 1

                            def mmf(h, wt=wt, bk=bk, h4=h4):
                                for kc in range(16):
                                    ins = h.matmul(PB[bk][:, :n], wt[:, kc, h4 * 128:(h4 + 1) * 128], hT[:, kc, :], start=(kc == 0), stop=(kc == 15))
                                return ins
                            S.op("pe", mmf, reads=[rhT, wres], writes=[RB[bk]])
                            qt, rq, qd = qs.nxt()
                            S.op("act", (lambda qt, bk, scale: lambda h: h.activation(out=qt[:], in_=PB[bk][:, :n], func=AF.Copy, scale=scale))(qt, bk, scale),
                                 reads=[RB[bk]], writes=[rq])
                            if not (qskip & 8):
                                S.op("sp", (lambda qt, kind, hd: lambda h: h.dma_start(out=QTs[kind, hd, :, :n], in_=qt[:]))(qt, kind, hd),
                                     reads=[rq], writes=[r_scr["QT"]], dsem=qd)
                            if kind == 0 and not (qskip & 16):
                                qt2, rq2, qd2 = qs.nxt()
                                S.op("dve", (lambda qt2, qt: lambda h: h.tensor_scalar(out=qt2[:], in0=qt[:], scalar1=-1.0, scalar2=None,
                                                                                      op0=ALU.mult))(qt2, qt), reads=[rq], writes=[rq2])
                                S.op("sp", (lambda qt2, hd: lambda h: h.dma_start(out=QTs[1, hd, :, :n], in_=qt2[:]))(qt2, hd),
                                     reads=[rq2], writes=[r_scr["QT"]], dsem=qd2)
                wx = PT("q_wx", [128, 16, 16], BF16, st)
                wxf = PT("q_wxf", [128, 16, 16], F32, st)
                rwx, rwxf = Res(), Res()
                S.op("sp", lambda h: h.dma_start(out=wxf[:], in_=wix_in), writes=[rwxf], dsem=dl[3])
                S.op("dve", lambda h: h.tensor_copy(out=wx[:], in_=wxf[:]), reads=[rwxf], writes=[rwx])
                for i, (b0, rows) in enumerate(blks if not (qskip & 4) else []):
                    def mmw(h, b0=b0, rows=rows):
                        for kc in range(16):
                            ins = h.matmul(PB[4][:rows, :16], hT[:, kc, b0:b0 + rows], wx[:, kc, :], start=(kc == 0), stop=(kc == 15))
                        return ins
                    S.op("pe", mmw, reads=[rhT, rwx], writes=[RB[4]])
                    S.op("dve", (lambda i, rows: lambda h: h.tensor_copy(out=wraw[:rows, i, :], in_=PB[4][:rows, :16]))(i, rows),
                         reads=[RB[4]], writes=[r_wraw])
                S.flush()
                for rg in (xr, wr, qs):
                    rg.release()
                S.putd(dl)

        def win_attn(ctx, n, kblocks, slot, strip, ULx, u0_of, near_of, sbmask_of, idxmask_of):
            KE = kblocks[-1][0] + kblocks[-1][1]
            qblks = blocks_of(n)
            nqb = len(qblks)
            nkb = len(kblocks)
            with ExitStack() as st:
                dl = [S.getd() for _ in range(6)]
                scores = PT("at_sc", [128, KE], F32, st)
                masks = [PT("at_m%d" % i, [128, KE], FP8, st) for i in range(nqb)]
                rsc = Res()
                rmk = [Res() for _ in range(nqb)]
                qix = PT("at_qix", [128, 8, n], BF16, st)
                rqix = Res()
                S.op("sp", lambda h: h.dma_start(out=qix[:], in_=QTs[3, :, :, :n].rearrange("a p c -> p a c")), reads=[r_scr["QT"]],
                     writes=[rqix], dsem=dl[0])
                CK = 1024
                kvr_k = Ring(S, nc, st, "at_k", [128, CK], BF16, 3)
                kvr_v = Ring(S, nc, st, "at_v", [128, CK // 128, 128], BF16, 3)
                kir = Ring(S, nc, st, "at_ki", [128, 512], BF16, 3)
                imr = Ring(S, nc, st, "at_im", [128, 512], BF16, 2)
                smr = Ring(S, nc, st, "at_sm", [128, n], BF16, 4)
                qr = Ring(S, nc, st, "at_q", [128, n], BF16, 4)
                er = Ring(S, nc, st, "at_e", [128, n], F32, 3, with_dsem=False)
                spr = Ring(S, nc, st, "at_sp", [128, n], BF16, 4, with_dsem=False)
                ar = Ring(S, nc, st, "at_a", [128, n], BF16, 4, with_dsem=False)
                rr_ = Ring(S, nc, st, "at_r", [128, 512], BF16, 6, with_dsem=False)
                osr = Ring(S, nc, st, "at_os", [128, n], BF16, 2)
                Sacc = PT("at_S", [128, n], BF16, st)
                rS = Res()
                Dh = PT("at_Dh", [128, NHI, 128], BF16, st)
                rDh = Res()
                wsm = PT("at_wsm", [128, 2, NHI], F32, st)
                rws = Res()
                bs = PT("at_bs", [128, 16], F32, st)
                rbs = Res()
                stp = PT("at_strip", [128, ULx], BF16, st)
                rstp = Res()
                dstp = dl[1]
                rden = PT("at_rden", [128, n], F32, st)
                rrden = Res()

                chunks = []
                i = 0
                while i < nkb:
                    j = i
                    while j < nkb and kblocks[j][0] + kblocks[j][1] <= kblocks[i][0] + CK:
                        j += 1
                    chunks.append((i, j))
                    i = j

                def load_kv(which, hd, ci):
                    i0, i1 = chunks[ci]
                    k0 = kblocks[i0][0]
                    kn = kblocks[i1 - 1][0] + kblocks[i1 - 1][1] - k0
                    kt, rk, kd = kvr_k.nxt()
                    vt, rv, vd = kvr_v.nxt()
                    S.op("sp", (lambda kt, a, kn: lambda h: h.dma_start(out=kt[:, :kn], in_=a))(kt, ctx["KT"][which][hd, :, k0:k0 + kn], kn),
                         reads=[r_scr["ctx"]], writes=[rk], dsem=kd)
                    nfull = kn // 128
                    if nfull:
                        S.op("sp", (lambda vt, a, nfull: lambda h: h.dma_start(out=vt[:, :nfull, :], in_=a))(
                            vt, ctx["V"][which][k0:k0 + nfull * 128, hd * 128:(hd + 1) * 128].rearrange("(b p) d -> p b d", p=128), nfull),
                            reads=[r_scr["ctx"]], writes=[rv], dsem=vd)
                    rem = kn - nfull * 128
                    if rem:
                        S.op("sp", (lambda vt, a, nfull, rem: lambda h: h.dma_start(out=vt[:rem, nfull, :], in_=a))(
                            vt, ctx["V"][which][k0 + nfull * 128:k0 + kn, hd * 128:(hd + 1) * 128], nfull, rem),
                            reads=[r_scr["ctx"]], writes=[rv], dsem=vd)
                    return kt, rk, vt, rv, k0

                def idx(qi):
                    q0, qrows = qblks[qi]
                    S.op("dve", lambda h: h.tensor_scalar(out=wsm[:qrows, 0, :], in0=wraw[:qrows, qi, :], scalar1=-0.25, scalar2=None,
                                                          op0=ALU.mult), reads=[r_wraw], writes=[rws])
                    S.op("dve", lambda h: h.scalar_tensor_tensor(out=wsm[:qrows, 0, :], in0=wraw[:qrows, qi, :], scalar=0.25, in1=wsm[:qrows, 0, :],
                                                                 op0=ALU.mult, op1=ALU.max), reads=[r_wraw, rws], writes=[rws])
                    S.op("act", lambda h: h.activation(out=wsm[:qrows, 1, :], in_=wraw[:qrows, qi, :], func=AF.Sign), reads=[r_wraw, rws],
                         writes=[rws])
                    for hh in range(NHI):
                        eng = "dve" if hh % 2 else "pool"
                        S.op(eng, (lambda hh: lambda h: h.tensor_scalar(out=Dh[:qrows, hh, :qrows], in0=identb[:qrows, :qrows],
                                                                       scalar1=wsm[:qrows, 1, hh:hh + 1], scalar2=None, op0=ALU.mult))(hh),
                             reads=[rws, r_const], writes=[rDh])
                    nkt = (KE + 511) // 512

                    def kithunk(kt_i):
                        def f():
                            k0 = kt_i * 512
                            kw = min(512, KE - k0)
                            kit, rki, kid_ = kir.nxt()
                            S.op("sp", lambda h: h.dma_start(out=kit[:, :kw], in_=ctx["KI2"][:, k0:k0 + kw]),
                                 reads=[r_scr["ctx"]], writes=[rki], dsem=kid_)
                            return kit, rki
                        return f
                    kpre = Pre([kithunk(k) for k in range(nkt)], 1)
                    def ktile(kt_i):
                        k0 = kt_i * 512
                        kw = min(512, KE - k0)
                        kit, rki = kpre.get(kt_i)
                        sb = 4 + (kt_i % 2)
                        pend = []
                        for step in range(NHI + 3):
                            if step < NHI:
                                hh = step
                                bk = hh % 4
                                base = 64 * (hh % 2)
                                S.op("pe", (lambda hh, bk, base: lambda h: h.matmul(PB[bk][:qrows, :kw], qix[base:base + 64, hh // 2, q0:q0 + qrows],
                                                                                  kit[base:base + 64, :kw], start=True, stop=True))(hh, bk, base),
                                     reads=[rqix, rki], writes=[RB[bk]])
                                rt, rrt, _ = rr_.nxt()
                                if hh % 2 == 0:
                                    S.op("act", (lambda rt, bk, hh: lambda h: h.activation(out=rt[:qrows, :kw], in_=PB[bk][:qrows, :kw], func=AF.Relu,
                                                                                          scale=wsm[:qrows, 0, hh:hh + 1]))(rt, bk, hh),
                                         reads=[RB[bk], rws], writes=[rrt])
                                else:
                                    S.op("dve", (lambda rt, bk, hh: lambda h: h.tensor_scalar(out=rt[:qrows, :kw], in0=PB[bk][:qrows, :kw],
                                                                                             scalar1=wsm[:qrows, 0, hh:hh + 1], scalar2=0.0,
                                                                                             op0=ALU.mult, op1=ALU.max))(rt, bk, hh),
                                         reads=[RB[bk], rws], writes=[rrt])
                                pend.append((rt, rrt))
                            if step >= 3:
                                h2 = step - 3
                                rt, rrt = pend[h2]
                                S.op("pe", (lambda rt, h2: lambda h: h.matmul(PB[sb][:qrows, :kw], Dh[:qrows, h2, :qrows], rt[:qrows, :kw],
                                                                             start=(h2 == 0), stop=(h2 == NHI - 1)))(rt, h2),
                                     reads=[rrt, rDh], writes=[RB[sb]])
                        ima = idxmask_of(qi, kt_i)
                        if ima is not None:
                            imt, rim, imd = imr.nxt()
                            S.op("sp", (lambda imt, ima: lambda h: h.dma_start(out=imt[:], in_=ima))(imt, ima), writes=[rim], dsem=imd)
                            S.op("dve", (lambda imt: lambda h: h.tensor_tensor(out=scores[:qrows, k0:k0 + kw], in0=PB[sb][:qrows, :kw],
                                                                              in1=imt[:qrows, :kw], op=ALU.add))(imt),
                                 reads=[RB[sb], rim], writes=[rsc])
                        else:
                            S.op("act", lambda h: h.activation(out=scores[:qrows, k0:k0 + kw], in_=PB[sb][:qrows, :kw],
                                                               func=AF.Copy), reads=[RB[sb]], writes=[rsc])
                    for kt_i in range(nkt):
                        ktile(kt_i)

                def bisect(qi):
                    q0, qrows = qblks[qi]
                    mk = masks[qi]
                    lo, hi, mid, cnt, ge, d1 = (bs[:qrows, c:c + 1] for c in range(6))
                    S.op("dve", lambda h: h.reduce_max(out=hi, in_=scores[:qrows, :KE], axis=AX.X), reads=[rsc], writes=[rbs])
                    S.op("dve", lambda h: h.tensor_scalar(out=lo, in0=hi, scalar1=-BRANGE, scalar2=None, op0=ALU.add), reads=[rbs], writes=[rbs])
                    S.op("dve", lambda h: h.tensor_scalar(out=hi, in0=hi, scalar1=1e-3, scalar2=None, op0=ALU.add), reads=[rbs], writes=[rbs])
                    for it in range(NBIS):
                        S.op("dve", lambda h: h.tensor_scalar(out=mid, in0=lo, scalar1=hi, scalar2=0.5, op0=ALU.add, op1=ALU.mult),
                             reads=[rbs], writes=[rbs])
                        S.op("dve", lambda h: h.tensor_scalar(out=mk[:qrows, :KE], in0=scores[:qrows, :KE], scalar1=mid, scalar2=0.0,
                                                              op0=ALU.is_ge, op1=ALU.add, accum_out=cnt, saturate=False),
                             reads=[rsc, rbs], writes=[rmk[qi], rbs])
                        S.op("dve", lambda h: h.tensor_scalar(out=ge, in0=cnt, scalar1=TOPK - 0.5, scalar2=None, op0=ALU.is_ge), reads=[rbs],
                             writes=[rbs])
                        S.op("dve", lambda h: h.tensor_tensor(out=d1, in0=mid, in1=lo, op=ALU.subtract), reads=[rbs], writes=[rbs])
                        S.op("dve", lambda h: h.scalar_tensor_tensor(out=lo, in0=d1, scalar=ge, in1=lo, op0=ALU.mult, op1=ALU.add), reads=[rbs],
                             writes=[rbs])
                        S.op("dve", lambda h: h.tensor_tensor(out=d1, in0=hi, in1=mid, op=ALU.subtract), reads=[rbs], writes=[rbs])
                        S.op("dve", lambda h: h.scalar_tensor_tensor(out=hi, in0=d1, scalar=ge, in1=mid, op0=ALU.mult, op1=ALU.add), reads=[rbs],
                             writes=[rbs])
                    S.op("dve", lambda h: h.tensor_scalar(out=mk[:qrows, :KE], in0=scores[:qrows, :KE], scalar1=lo, scalar2=-240.0,
                                                          op0=ALU.is_lt, op1=ALU.mult, saturate=False), reads=[rsc, rbs], writes=[rmk[qi]])

                def sb_head(hd):
                    qt, rq, qd = qr.nxt()
                    qn, rqn, qnd = qr.nxt()
                    S.op("sp", lambda h: h.dma_start(out=qt[:], in_=QTs[0, hd, :, :n]), reads=[r_scr["QT"]], writes=[rq], dsem=qd)
                    S.op("sp", lambda h: h.dma_start(out=qn[:], in_=QTs[1, hd, :, :n]), reads=[r_scr["QT"]], writes=[rqn], dsem=qnd)
                    S.op("pool", lambda h: h.memset(Sacc[:], 0.0), writes=[rS])
                    corder = list(reversed(range(len(chunks))))
                    kvpre = Pre([(lambda ci: lambda: load_kv(0, hd, ci))(ci) for ci in corder], 1)
                    kvc = {}
                    tiles = []
                    for pos, ci in enumerate(corder):
                        i0, i1 = chunks[ci]
                        for kbi in reversed(range(i0, i1)):
                            tiles.append((pos, ci, kbi))
                    T_ = len(tiles)
                    stt = [None] * T_

                    def Z(i):
                        pos, ci, kbi = tiles[i]
                        if ci not in kvc:
                            kvc[ci] = kvpre.get(pos)
                        kt, rk, vt, rv, kc0 = kvc[ci]
                        k0, rows = kblocks[kbi]
                        o = k0 - kc0
                        d = dict(kt=kt, rk=rk, vt=vt, rv=rv, o=o, vb=o // 128, rows=rows, zb=(0, 1, 4)[i % 3], lb=(2, 3, 5)[i % 3])
                        sma = sbmask_of(kbi)
                        d["sma"] = sma
                        if sma is not None:
                            smt, rsm, smd = smr.nxt()
                            S.op("sp", lambda h: h.dma_start(out=smt[:rows, :], in_=sma), writes=[rsm], dsem=smd)
                            d["smt"], d["rsm"] = smt, rsm
                        zb = d["zb"]

                        def zf(h):
                            ins = h.matmul(PB[zb][:rows, :n], kt[:, o:o + rows], qt[:], start=True, stop=(sma is None))
                            if sma is not None:
                                ins = h.matmul(PB[zb][:rows, :n], identb[:rows, :rows], d["smt"][:rows, :], start=False, stop=True)
                            return ins
                        S.op("pe", zf, reads=[rk, rq, r_const] + ([d["rsm"]] if sma is not None else []), writes=[RB[zb]])
                        et, ret, _ = er.nxt()
                        spt, rspt, _ = spr.nxt()
                        S.op("act", lambda h: h.activation(out=et[:rows, :], in_=PB[zb][:rows, :n], func=AF.Exp), reads=[RB[zb]], writes=[ret])
                        S.op("act", lambda h: h.activation(out=spt[:rows, :], in_=et[:rows, :], func=AF.Ln, bias=1.0), reads=[ret], writes=[rspt])
                        d["spt"], d["rspt"] = spt, rspt
                        stt[i] = d

                    def L(i):
                        d = stt[i]
                        kt, o, rows, lb, sma, spt = d["kt"], d["o"], d["rows"], d["lb"], d["sma"], d["spt"]

                        def lf(h):
                            h.matmul(PB[lb][:rows, :n], kt[:, o:o + rows], qn[:], start=True, stop=False)
                            if sma is not None:
                                h.matmul(PB[lb][:rows, :n], nidentb[:rows, :rows], d["smt"][:rows, :], start=False, stop=False)
                            h.matmul(PB[lb][:rows, :n], trib[:rows, :rows], spt[:rows, :], start=False, stop=False)
                            return h.matmul(PB[lb][:rows, :n], onesb[:, :rows], Sacc[:], start=False, stop=True)
                        S.op("pe", lf, reads=[d["rk"], rqn, r_const, d["rspt"], rS] + ([d["rsm"]] if sma is not None else []), writes=[RB[lb]])
                        at, rat, _ = ar.nxt()
                        S.op("act", lambda h: h.activation(out=at[:rows, :], in_=PB[lb][:rows, :n], func=AF.Exp, scale=-1.0), reads=[RB[lb]], writes=[rat])
                        S.op("pool", lambda h: h.tensor_tensor(out=Sacc[:rows, :], in0=Sacc[:rows, :], in1=spt[:rows, :], op=ALU.add),
                             reads=[d["rspt"], rS], writes=[rS])
                        d["at"], d["rat"] = at, rat

                    def O(i):
                        d = stt[i]
                        vt, vb, rows, at = d["vt"], d["vb"], d["rows"], d["at"]
                        S.op("pe", lambda h: h.matmul(PB[6][:, :n], vt[:rows, vb, :], at[:rows, :], start=(i == 0), stop=(i == T_ - 1)),
                             reads=[d["rv"], d["rat"]], writes=[RB[6]])
                        stt[i] = None
                    for i in range(T_ + 3):
                        if i < T_:
                            Z(i)
                        if 0 <= i - 2 < T_:
                            L(i - 2)
                        if 0 <= i - 3 < T_:
                            O(i - 3)
                    ost, ros, osd = osr.nxt()
                    S.op("act", lambda h: h.activation(out=ost[:], in_=PB[6][:, :n], func=AF.Copy), reads=[RB[6]], writes=[ros])
                    S.op("sp", lambda h: h.dma_start(out=OTs[0, hd, :, :n], in_=ost[:]), reads=[ros], writes=[r_scr["OT"]], dsem=osd)

                def dsa_head(hd):
                    qt, rq, qd = qr.nxt()
                    S.op("sp", lambda h: h.dma_start(out=qt[:], in_=QTs[2, hd, :, :n]), reads=[r_scr["QT"]], writes=[rq], dsem=qd)
                    S.op("sp", lambda h: h.dma_start(out=stp[:], in_=strip[hd]), reads=[r_scr["strip"]], writes=[rstp], dsem=dstp)
                    corder = list(range(len(chunks)))
                    kvpre = Pre([(lambda ci: lambda: load_kv(1, hd, ci))(ci) for ci in corder], 1)
                    kvc = {}
                    tiles = []
                    for pos, ci in enumerate(corder):
                        i0, i1 = chunks[ci]
                        for kbi in range(i0, i1):
                            tiles.append((pos, ci, kbi))
                    T_ = len(tiles)
                    stt = [None] * T_

                    def LT(i):
                        pos, ci, kbi = tiles[i]
                        if ci not in kvc:
                            kvc[ci] = kvpre.get(pos)
                        kt, rk, vt, rv, kc0 = kvc[ci]
                        k0, rows = kblocks[kbi]
                        o = k0 - kc0
                        lb = i % 4
                        nearb = near_of(kbi)
                        u0 = u0_of(kbi) if nearb else 0

                        def lf(h):
                            ins = h.matmul(PB[lb][:rows, :n], kt[:, o:o + rows], qt[:], start=True, stop=False)
                            for qi, (q0, qrows) in enumerate(qblks):
                                lastm = (qi == nqb - 1) and not nearb
                                ins = h.matmul(PB[lb][:rows, q0:q0 + qrows], masks[qi][:qrows, k0:k0 + rows], identb[:qrows, :qrows],
                                               start=False, stop=lastm)
                            if nearb:
                                ins = h.matmul(PB[lb][:rows, :n], identb[:rows, :rows], stp[:rows, u0:u0 + n], start=False, stop=True)
                            return ins
                        S.op("pe", lf, reads=[rk, rq, r_const, rstp] + rmk, writes=[RB[lb]])
                        pt, rpt, _ = ar.nxt()
                        if nearb:
                            S.op("act", lambda h: h.activation(out=pt[:rows, :], in_=PB[lb][:rows, :n], func=AF.Exp), reads=[RB[lb]], writes=[rpt])
                        else:
                            S.op("act", lambda h: h.activation(out=pt[:rows, :], in_=PB[lb][:rows, :n], func=AF.Exp, bias=CH[:rows, hd:hd + 1]),
                                 reads=[RB[lb], r_const], writes=[rpt])
                        stt[i] = dict(vt=vt, rv=rv, vb=o // 128, rows=rows, pt=pt, rpt=rpt)

                    def O(i):
                        d = stt[i]
                        vt, vb, rows, pt = d["vt"], d["vb"], d["rows"], d["pt"]

                        def of(h):
                            h.matmul(PB[7][:, :n], vt[:rows, vb, :], pt[:rows, :], start=(i == 0), stop=(i == T_ - 1))
                            return h.matmul(PB[5][:, :n], onesb[:rows, :], pt[:rows, :], start=(i == 0), stop=(i == T_ - 1))
                        S.op("pe", of, reads=[d["rv"], d["rpt"], r_const], writes=[RB[7], RB[5]])
                        stt[i] = None
                    for i in range(T_ + 1):
                        if i < T_:
                            LT(i)
                        if 0 <= i - 1 < T_:
                            O(i - 1)
                    S.op("dve", lambda h: h.tensor_scalar(out=rden[:], in0=PB[5][:, :n], scalar1=1e-30, scalar2=None, op0=ALU.max), reads=[RB[5]],
                         writes=[rrden])
                    S.op("dve", lambda h: h.reciprocal(out=rden[:], in_=rden[:]), reads=[rrden], writes=[rrden])
                    ost, ros, osd = osr.nxt()
                    S.op("dve", lambda h: h.tensor_tensor(out=ost[:], in0=PB[7][:, :n], in1=rden[:], op=ALU.mult),
                         reads=[RB[7], rrden], writes=[ros])
                    S.op("sp", lambda h: h.dma_start(out=OTs[1, hd, :, :n], in_=ost[:]), reads=[ros], writes=[r_scr["OT"]], dsem=osd)

                hpq = (NH + nqb - 1) // nqb
                hd_next = 0
                for qi in range(nqb):
                    idx(qi)
                    bisect(qi)
                    for _ in range(hpq):
                        if hd_next < NH:
                            sb_head(hd_next)
                            hd_next += 1
                while hd_next < NH:
                    sb_head(hd_next)
                    hd_next += 1
                for hd in range(NH):
                    dsa_head(hd)
                S.flush()
                for rg in (kvr_k, kvr_v, kir, imr, smr, qr, osr):
                    rg.release()
                S.putd(dl)

        def win_ffn(xsrc, n, r, halo, ydst, nout, prev_src, conv_dst, conv_cols, slot_flag):
            blks = blocks_of(n)
            nb = len(blks)
            with ExitStack() as st0:
                x1 = [PT("f_x1_%d" % i, [128, D], F32, st0) for i in range(nb)]
                rx1 = [Res() for _ in range(nb)]
                h2T = PT("f_h2T", [128, 16, n], BF16, st0)
                rh2 = Res()
                with ExitStack() as st:
                    dl = [S.getd() for _ in range(4)]
                    hT = PT("fa_hT", [128, 16, n], BF16, st)
                    oT = PT("fa_oT", [128, 16, n], BF16, st)
                    mT = PT("fa_mT", [128, 16, n], BF16, st)
                    rhT, roT, rmT = Res(), Res(), Res()
                    S.op("sp", lambda h: h.dma_start(out=hT[:], in_=hTs[:, :, :n].rearrange("a p c -> p a c")), reads=[r_scr["hT"]], writes=[rhT], dsem=dl[0])
                    for t in range(2):
                        S.op("sp", (lambda t: lambda h: h.dma_start(out=oT[:, t * 8:(t + 1) * 8, :], in_=OTs[t, :, :, :n].rearrange("a p c -> p a c")))(t),
                             reads=[r_scr["OT"]], writes=[roT], dsem=dl[1])
                    for i, (b0, rows) in enumerate(blks):
                        S.op("sp", (lambda i, b0, rows: lambda h: h.dma_start(out=x1[i][:rows], in_=xsrc[b0:b0 + rows, :]))(i, b0, rows), writes=[rx1[i]],
                             dsem=dl[2])
                    gt1, rgt1 = load_row_rep(st, dl[3], "fa_gt1", modrow[r:r + 1, 2 * D:3 * D])
                    wg = Ring(S, nc, st, "fa_wg", [128, 16, 256], BF16, 2)
                    wb = Ring(S, nc, st, "fa_wb", [128, 16, 128], BF16, 2)
                    gs = Ring(S, nc, st, "fa_g", [128, n], F32, 4, with_dsem=False)
                    wgv = wbf["gate"].rearrange("(kc p) c -> p kc c", p=128)
                    wbv = [wbf["brsb"].rearrange("(kc p) c -> p kc c", p=128), wbf["brsa"].rearrange("(kc p) c -> p kc c", p=128)]

                    def gthunk(fb):
                        def f():
                            wt, wres, wd = wg.nxt()
                            S.op("sp", lambda h: h.dma_start(out=wt[:, :, 0:128], in_=wgv[:, :, fb * 128:(fb + 1) * 128]),
                                 reads=[r_scr["wbf"]], writes=[wres], dsem=wd)
                            S.op("sp", lambda h: h.dma_start(out=wt[:, :, 128:256], in_=wgv[:, :, D + fb * 128:D + (fb + 1) * 128]),
                                 reads=[r_scr["wbf"]], writes=[wres], dsem=wd)
                            wt2, wres2, wd2 = wb.nxt()
                            for t in range(2):
                                S.op("sp", (lambda t: lambda h: h.dma_start(out=wt2[:, t * 8:(t + 1) * 8, :], in_=wbv[t][:, :, fb * 128:(fb + 1) * 128]))(t),
                                     reads=[r_scr["wbf"]], writes=[wres2], dsem=wd2)
                            return wt, wres, wt2, wres2
                        return f
                    gpre = Pre([gthunk(fb) for fb in range(16)], 1)

                    def gate_fb(fb):
                        wt, wres, wt2, wres2 = gpre.get(fb)
                        gts = []
                        for t in range(2):
                            bk = t

                            def gf(h, bk=bk, t=t):
                                for kc in range(16):
                                    ins = h.matmul(PB[bk][:, :n], wt[:, kc, t * 128:(t + 1) * 128], hT[:, kc, :], start=(kc == 0), stop=(kc == 15))
                                return ins
                            S.op("pe", gf, reads=[rhT, wres], writes=[RB[bk]])
                            g_, rg_, _ = gs.nxt()
                            S.op("act", (lambda g_, bk: lambda h: h.activation(out=g_[:], in_=PB[bk][:, :n], func=AF.Sigmoid))(g_, bk), reads=[RB[bk]],
                                 writes=[rg_])
                            gts.append((g_, rg_))
                        for t in range(2):
                            bk = 2 + t

                            def bf_(h, bk=bk, t=t):
                                for kc in range(8):
                                    ins = h.matmul(PB[bk][:, :n], wt2[:, t * 8 + kc, :], oT[:, t * 8 + kc, :], start=(kc == 0), stop=(kc == 7))
                                return ins
                            S.op("pe", bf_, reads=[roT, wres2], writes=[RB[bk]])
                        g0, rg0 = gts[0]
                        g1, rg1 = gts[1]
                        S.op("dve", lambda h: h.tensor_tensor(out=g0[:], in0=PB[2][:, :n], in1=g0[:], op=ALU.mult), reads=[RB[2], rg0], writes=[rg0])
                        S.op("dve", lambda h: h.tensor_tensor(out=g1[:], in0=PB[3][:, :n], in1=g1[:], op=ALU.mult), reads=[RB[3], rg1], writes=[rg1])
                        S.op("pool", lambda h: h.tensor_tensor(out=mT[:, fb, :], in0=g0[:], in1=g1[:], op=ALU.add), reads=[rg0, rg1], writes=[rmT])
                    for fb in range(16):
                        gate_fb(fb)
                    wo = Ring(S, nc, st, "fa_wo", [128, 16, 512], BF16, 2)
                    tmpr = Ring(S, nc, st, "fa_tmp", [128, 512], F32, 2, with_dsem=False)
                    wov = wbf["out"].rearrange("(kc p) c -> p kc c", p=128)

                    def othunk(nbk):
                        def f():
                            wt, wres, wd = wo.nxt()
                            S.op("sp", lambda h: h.dma_start(out=wt[:], in_=wov[:, :, nbk * 512:(nbk + 1) * 512]),
                                 reads=[r_scr["wbf"]], writes=[wres], dsem=wd)
                            return wt, wres
                        return f
                    opre = Pre([othunk(k_) for k_ in range(4)], 1)

                    def out_blk(nbk, i, b0, rows, bk, wt, wres):
                        def of(h):
                            for kc in range(16):
                                ins = h.matmul(PB[bk][:rows, :], mT[:, kc, b0:b0 + rows], wt[:, kc, :], start=(kc == 0), stop=(kc == 15))
                            return ins
                        S.op("pe", of, reads=[rmT, wres], writes=[RB[bk]])
                        tt, rtt, _ = tmpr.nxt()
                        S.op("dve", lambda h: h.tensor_tensor(out=tt[:rows, :], in0=PB[bk][:rows, :], in1=gt1[:rows, nbk * 512:(nbk + 1) * 512], op=ALU.mult),
                             reads=[RB[bk], rgt1], writes=[rtt])
                        S.op("pool", lambda h: h.tensor_tensor(out=x1[i][:rows, nbk * 512:(nbk + 1) * 512], in0=x1[i][:rows, nbk * 512:(nbk + 1) * 512],
                                                               in1=tt[:rows, :], op=ALU.add), reads=[rtt, rx1[i]], writes=[rx1[i]])
                    k = 0
                    for nbk in range(4):
                        wt, wres = opre.get(nbk)
                        for i, (b0, rows) in enumerate(blks):
                            out_blk(nbk, i, b0, rows, 4 + (k % 2), wt, wres)
                            k += 1
                    S.flush()
                    for rg in (wg, wb, wo):
                        rg.release()
                    S.putd(dl)
                with ExitStack() as st:
                    dl = [S.getd() for _ in range(3)]
                    A2, B2, rA2, rB2 = load_mod(st, dl, r, 2)
                    nt = NormT(st, "nf")
                    for i, (b0, rows) in enumerate(blks):
                        nt.run(x1[i], rx1[i], rows, A2, B2, rA2, rB2, h2T, rh2, b0)
                    if slot_flag is not None and halo:
                        S.op("dve", lambda h: h.tensor_scalar(out=h2T[:, :, 0:halo], in0=h2T[:, :, 0:halo], scalar1=hflag[:, slot_flag:slot_flag + 1],
                                                              scalar2=None, op0=ALU.mult), reads=[rh2, r_const], writes=[rh2])
                    S.flush()
                    S.putd(dl)
                with ExitStack() as st:
                    dl = [S.getd() for _ in range(6)]
                    aT = PT("fb_aT", [128, NFF, n], BF16, st)
                    raT = Res()
                    if halo:
                        S.op("pool", lambda h: h.memset(aT[:, :, 0:halo], 0.0), writes=[raT])
                    gt2, rgt2 = load_row_rep(st, dl[0], "fb_gt2", modrow[r:r + 1, 5 * D:6 * D])
                    gfin, rgfin = load_row_rep(st, dl[1], "fb_gf", g_final[0:1, :])
                    ne = nout + 2
                    E = Ring(S, nc, st, "fb_E", [128, ne], F32, 2, with_dsem=False)
                    tr_ = Ring(S, nc, st, "fb_t", [128, nout], F32, 2, with_dsem=False)
                    wu = Ring(S, nc, st, "fb_wu", [128, 16, 256], BF16, 3)
                    wuv = wbf["up"].rearrange("(kc p) c -> p kc c", p=128)
                    prevt = None
                    rprev = Res()
                    if prev_src is not None:
                        prevt = PT("fb_prev", [128, NFF, 2], F32, st)
                        S.op("sp", lambda h: h.dma_start(out=prevt[:], in_=prev_src), writes=[rprev], dsem=dl[2])

                    def uthunk(fb):
                        def f():
                            wt, wres, wd = wu.nxt()
                            S.op("sp", lambda h: h.dma_start(out=wt[:, :, 0:128], in_=wuv[:, :, fb * 128:(fb + 1) * 128]),
                                 reads=[r_scr["wbf"]], writes=[wres], dsem=wd)
                            S.op("sp", lambda h: h.dma_start(out=wt[:, :, 128:256], in_=wuv[:, :, DFF + fb * 128:DFF + (fb + 1) * 128]),
                                 reads=[r_scr["wbf"]], writes=[wres], dsem=wd)
                            return wt, wres
                        return f
                    upre = Pre([uthunk(fb) for fb in range(NFF)], 2)

                    def up_fb(fb):
                        wt, wres = upre.get(fb)
                        for t in range(2):
                            bk = (fb % 2) * 2 + t

                            def uf(h, bk=bk, t=t):
                                for kc in range(16):
                                    ins = h.matmul(PB[bk][:, :n], wt[:, kc, t * 128:(t + 1) * 128], h2T[:, kc, :], start=(kc == 0), stop=(kc == 15))
                                return ins
                            S.op("pe", uf, reads=[rh2, wres], writes=[RB[bk]])
                        bg = (fb % 2) * 2
                        bv = bg + 1
                        Et, rE, _ = E.nxt()
                        tt, rtt, _ = tr_.nxt()
                        if prevt is None:
                            S.op("act", lambda h: h.activation(out=Et[:, :n], in_=PB[bg][:, :n], func=AF.Copy), reads=[RB[bg]], writes=[rE])
                        else:
                            S.op("act", lambda h: h.activation(out=Et[:, 2:2 + n], in_=PB[bg][:, :n], func=AF.Copy), reads=[RB[bg]], writes=[rE])
                            S.op("pool", lambda h: h.tensor_copy(out=Et[:, 0:2], in_=prevt[:, fb, :]), reads=[rprev, rE], writes=[rE])
                        S.op("dve", lambda h: h.tensor_scalar(out=tt[:], in0=Et[:, 2:2 + nout], scalar1=cwb[:, fb, 2:3], scalar2=cwb[:, fb, 3:4],
                                                              op0=ALU.mult, op1=ALU.add), reads=[rE, r_const], writes=[rtt])
                        S.op("dve", lambda h: h.scalar_tensor_tensor(out=tt[:], in0=Et[:, 1:1 + nout], scalar=cwb[:, fb, 1:2], in1=tt[:],
                                                                     op0=ALU.mult, op1=ALU.add), reads=[rE, rtt, r_const], writes=[rtt])
                        S.op("dve", lambda h: h.scalar_tensor_tensor(out=tt[:], in0=Et[:, 0:nout], scalar=cwb[:, fb, 0:1], in1=tt[:],
                                                                     op0=ALU.mult, op1=ALU.add), reads=[rE, rtt, r_const], writes=[rtt])
                        S.op("act", lambda h: h.activation(out=tt[:], in_=tt[:], func=AF.Silu), reads=[rtt], writes=[rtt])
                        S.op("dve", lambda h: h.tensor_tensor(out=aT[:, fb, halo:halo + nout], in0=PB[bv][:, halo:halo + nout], in1=tt[:], op=ALU.mult),
                             reads=[rtt, RB[bv]], writes=[raT])
                        if conv_dst is not None:
                            S.op("pool", lambda h: h.tensor_copy(out=convc[:, fb, :], in_=Et[:, conv_cols:conv_cols + 2]), reads=[rE], writes=[r_convc])
                    for fb in range(NFF):
                        up_fb(fb)
                    if conv_dst is not None:
                        S.op("sp", lambda h: h.dma_start(out=conv_dst, in_=convc[:]), reads=[r_convc], writes=[r_scr["out"]], dsem=dl[3])
                    wdr = Ring(S, nc, st, "fb_wd", [128, 11, 512], BF16, 3)
                    wdv = wbf["down"].rearrange("(kc p) c -> p kc c", p=128)
                    tmpr = Ring(S, nc, st, "fb_tmp", [128, 512], F32, 2, with_dsem=False)
                    assert nb <= 4

                    def dthunk(nbk, pc):
                        def f():
                            wt, wres, wd = wdr.nxt()
                            S.op("sp", lambda h: h.dma_start(out=wt[:], in_=wdv[:, pc * 11:(pc + 1) * 11, nbk * 512:(nbk + 1) * 512]),
                                 reads=[r_scr["wbf"]], writes=[wres], dsem=wd)
                            return wt, wres
                        return f
                    dpre = Pre([dthunk(nbk, pc) for nbk in range(4) for pc in range(4)], 2)

                    def down_piece(nbk, pc, wt, wres):
                        for i, (b0, rows) in enumerate(blks):
                            bk = 4 + i

                            def df(h, bk=bk, b0=b0, rows=rows):
                                for kc in range(11):
                                    ins = h.matmul(PB[bk][:rows, :], aT[:, pc * 11 + kc, b0:b0 + rows], wt[:, kc, :], start=(pc == 0 and kc == 0),
                                                   stop=(pc == 3 and kc == 10))
                                return ins
                            S.op("pe", df, reads=[raT, wres], writes=[RB[bk]])

                    def down_evac(nbk, i, b0, rows):
                        bk = 4 + i
                        tt, rtt, _ = tmpr.nxt()
                        S.op("dve", lambda h: h.tensor_tensor(out=tt[:rows, :], in0=PB[bk][:rows, :], in1=gt2[:rows, nbk * 512:(nbk + 1) * 512], op=ALU.mult),
                             reads=[RB[bk], rgt2], writes=[rtt])
                        S.op("pool", lambda h: h.tensor_tensor(out=x1[i][:rows, nbk * 512:(nbk + 1) * 512], in0=x1[i][:rows, nbk * 512:(nbk + 1) * 512],
                                                               in1=tt[:rows, :], op=ALU.add), reads=[rtt, rx1[i]], writes=[rx1[i]])
                    for nbk in range(4):
                        for pc in range(4):
                            wt, wres = dpre.get(nbk * 4 + pc)
                            down_piece(nbk, pc, wt, wres)
                        for i, (b0, rows) in enumerate(blks):
                            down_evac(nbk, i, b0, rows)
                    junk = PT("fb_junk", [128, D], BF16, st)
                    sm = PT("fb_sm", [128, 8], F32, st)
                    rj, rsm = Res(), Res()

                    def fin(i, b0, rows):
                        xt = x1[i]
                        S.op("dve", lambda h: h.scalar_tensor_tensor(out=junk[:rows], in0=xt[:rows], scalar=1.0, in1=xt[:rows], op0=ALU.mult,
                                                                     op1=ALU.mult, accum_out=sm[:rows, 0:1]), reads=[rx1[i]], writes=[rj, rsm])
                        S.op("dve", lambda h: h.tensor_scalar(out=sm[:rows, 1:2], in0=sm[:rows, 0:1], scalar1=1.0 / D, scalar2=EPS, op0=ALU.mult,
                                                              op1=ALU.add), reads=[rsm], writes=[rsm])
                        S.op("act", lambda h: h.activation(out=sm[:rows, 2:3], in_=sm[:rows, 1:2], func=AF.Ln), reads=[rsm], writes=[rsm])
                        S.op("act", lambda h: h.activation(out=sm[:rows, 3:4], in_=sm[:rows, 2:3], func=AF.Exp, scale=-0.5), reads=[rsm], writes=[rsm])
                        S.op("dve", lambda h: h.scalar_tensor_tensor(out=xt[:rows], in0=xt[:rows], scalar=sm[:rows, 3:4], in1=gfin[:rows],
                                                                     op0=ALU.mult, op1=ALU.mult), reads=[rx1[i], rsm, rgfin], writes=[rx1[i]])
                        lo_ = max(b0, halo)
                        hi_ = min(b0 + rows, halo + nout)
                        if hi_ > lo_:
                            S.op("sp", lambda h: h.dma_start(out=ydst[lo_ - halo:hi_ - halo, :], in_=xt[lo_ - b0:hi_ - b0, :]),
                                 reads=[rx1[i]], writes=[r_scr["out"]], dsem=dl[4])
                    for i, (b0, rows) in enumerate(blks):
                        fin(i, b0, rows)
                    S.flush()
                    for rg in (wu, wdr):
                        rg.release()
                    S.putd(dl)

        import os as _os
        stop = int(_os.environ.get("MK_STOP", "1000"))
        stepc = [0]

        def go():
            stepc[0] += 1
            return stepc[0] <= stop
        if go():
            setup()
        for s in range(NS):
            if go():
                cache_import(s)
            if go():
                phaseA(ctx_s[s], xs[s], DSEQ, PAST, okv_s[s], 1 + s, DSEQ)
        if go():
            phaseA(ctx_p, xp, SEQ, 0, okv_p, 0, cfg.TT)

        kb_s = [(i * 128, 128) for i in range(PAST // 128)] + [(PAST, DSEQ)]
        for s in range(NS):
            if go():
                win_q(xs[s], DSEQ, 1 + s)
            if go():
                win_attn(ctx_s[s], DSEQ, kb_s, None, strip_s, cfg.ULs,
                         u0_of=lambda kbi: cfg.U0s - (128 * kbi - PAST),
                         near_of=lambda kbi: (128 * kbi - PAST) >= -NEAR - 127,
                         sbmask_of=lambda kbi: (sbmask_s_in[0:DSEQ, :] if kbi == PAST // 128 else None),
                         idxmask_of=lambda qi, kt: None)
            if go():
                win_ffn(xs[s], DSEQ, 1 + s, 0, y_s[s], DSEQ, cst[s], sconv[s], DSEQ, None)
        for m in range(NSLOT):
            kb_p = [(i * 128, min(128, cfg.kext[m] - i * 128)) for i in range(cfg.kextb[m])]
            if go():
                win_q(xw[m], ncols, 0)
            if go():
                win_attn(ctx_p, ncols, kb_p, m, strip_p, cfg.UL,
                         u0_of=(lambda m: lambda kbi: cfg.U0 - (128 * kbi - STRIDE * G * m + 2))(m),
                         near_of=(lambda m: lambda kbi: cfg.near(m, kbi))(m),
                         sbmask_of=(lambda m: lambda kbi: (sbmask_in[cfg.sbm_index[(m, kbi)], 0:kb_p_rows(cfg, m, kbi), :] if (m, kbi) in cfg.sbm_index else None))(m),
                         idxmask_of=(lambda m: lambda qi, kt: (idxmask_in[cfg.im_index[(m, qi, kt)]] if (m, qi, kt) in cfg.im_index else None))(m))
            last = (m == NSLOT - 1)
            ccol = (SEQ - 2) - (STRIDE * (G * m + G - 1) - 2)
            if go():
                win_ffn(xw[m], ncols, 0, 2, y_p[m], STRIDE, None, pconv if last else None, ccol, m)
        S.barrier()
        S.flush()
    return nc


def kb_p_rows(cfg, m, kbi):
    return min(128, cfg.kext[m] - kbi * 128)


def rel_bucket_np(rel):
    rel = np.asarray(rel, np.int64)
    nb = 16
    max_exact = 8
    ret = np.where(rel > 0, nb, 0)
    n = np.abs(rel)
    nf = np.maximum(n, 1).astype(np.float32)
    large = max_exact + (np.log(nf / np.float32(max_exact)) / np.float32(np.log(1024 / max_exact)) * np.float32(nb - max_exact)).astype(np.int32)
    large = np.minimum(large, nb - 1)
    return ret + np.where(n < max_exact, n, large)


def prep_cfg_tables(cfg):
    cfg.sbm_index = {}
    cfg.im_index = {}
    for m in range(cfg.NSLOT):
        for kb in range(cfg.kextb[m]):
            if cfg.sbmasked(m, kb):
                cfg.sbm_index[(m, kb)] = len(cfg.sbm_index)
        nqb = len(blocks_of(cfg.ncols))
        nkt = (cfg.kext[m] + 511) // 512
        for qi in range(nqb):
            for kt in range(nkt):
                if cfg.idxmasked(m, kt):
                    cfg.im_index[(m, qi, kt)] = len(cfg.im_index)
    cfg.n_sbm = max(1, len(cfg.sbm_index))
    cfg.n_im = max(1, len(cfg.im_index))


def core_tables(cfg, j):
    STRIDE, G, ncols = cfg.STRIDE, cfg.G, cfg.ncols
    bf = ml_dtypes.bfloat16
    sbm = np.zeros((cfg.n_sbm, 128, ncols), np.float32)
    for (m, kb), ix in cfg.sbm_index.items():
        qpos = STRIDE * (G * m + j) - 2 + np.arange(ncols)
        kpos = 128 * kb + np.arange(128)
        sbm[ix] = np.where(kpos[:, None] >= qpos[None, :], MASKV, 0.0)
    im = np.zeros((cfg.n_im, 128, 512), np.float32)
    qb = blocks_of(ncols)
    for (m, qi, kt), ix in cfg.im_index.items():
        q0, qrows = qb[qi]
        qpos = STRIDE * (G * m + j) - 2 + q0 + np.arange(128)
        lim = (qpos // 64 + 1) * 64
        kpos = 512 * kt + np.arange(512)
        im[ix] = np.where(kpos[None, :] >= lim[:, None], IMASKV, 0.0)
    i = np.arange(cfg.GL)
    r = (cfg.U0 + 127) - i - STRIDE * j
    b = rel_bucket_np(r)
    ohp = np.zeros((32, cfg.GL), np.float32)
    ohp[b, i] = 1.0
    i = np.arange(cfg.GLs)
    r = (cfg.U0s + 127) - i
    b = rel_bucket_np(r)
    ohs = np.zeros((32, cfg.GLs), np.float32)
    ohs[b, i] = 1.0
    sbs = np.zeros((128, DSEQ), np.float32)
    sbs[:DSEQ] = np.where(np.arange(DSEQ)[:, None] >= np.arange(DSEQ)[None, :], MASKV, 0.0)
    hflag = np.ones((128, cfg.NSLOT), np.float32)
    if j == 0:
        hflag[:, 0] = 0.0
    return {"sbmask": sbm.astype(bf), "idxmask": im.astype(bf), "oh_p": ohp, "oh_s": ohs, "sbmask_s": sbs.astype(bf), "hflag": hflag}


_NC_CACHE = {}


def run_cfg(cfg, inp):
    prep_cfg_tables(cfg)
    import os as _os
    key = (cfg.SEQ, cfg.NB, cfg.G, cfg.NSLOT, cfg.STRIDE, cfg.NS, cfg.TT, _os.environ.get('MK_STOP'), _os.environ.get('MK_QSKIP'))
    if key not in _NC_CACHE:
        _NC_CACHE[key] = build(cfg)
    nc = _NC_CACHE[key]
    SEQ, G, NSLOT, STRIDE, NS, ncols = cfg.SEQ, cfg.G, cfg.NSLOT, cfg.STRIDE, cfg.NS, cfg.ncols
    f32 = np.float32
    ident = np.eye(128, dtype=f32)
    tri = (np.arange(128)[:, None] >= np.arange(128)[None, :]).astype(f32)
    constf = np.stack([ident, -ident, tri, np.ones((128, 128), f32)], axis=1)
    cwb = np.concatenate([inp["conv_w"][0], inp["conv_b"][0][None]], axis=0)
    cwb = np.ascontiguousarray(cwb.reshape(4, NFF, 128).transpose(2, 1, 0))
    shared = {
        "w_ada": inp["w_ada"][0], "w_in": inp["w_in"][0], "w_gate": inp["w_gate"][0], "w_br_sb": inp["w_br_sb"][0],
        "w_br_sa": inp["w_br_sa"][0], "w_out": inp["w_out"][0], "w_up": inp["w_up"][0], "w_down": inp["w_down"][0],
        "g_mix": inp["g_mix"][0][None], "g_ffn": inp["g_ffn"][0][None], "g_final": inp["g_final"][None],
        "rel_table": inp["rel_table"], "cwb": cwb, "constf": constf,
        "wix": np.ascontiguousarray(inp["w_in"][0][:, C_WIX:C_WIX + 16].reshape(16, 128, 16).transpose(1, 0, 2)),
    }
    tabs = [core_tables(cfg, j) for j in range(G)]
    in_maps = []
    tot = STRIDE * G * NSLOT
    for c in range(cfg.ncores):
        b, j = c // G, c % G
        xpad = np.zeros((2 + max(tot, SEQ) + 2, D), f32)
        xpad[2:2 + SEQ] = inp["x_prompt"][b]
        xw = np.stack([xpad[STRIDE * (G * m + j):STRIDE * (G * m + j) + ncols] for m in range(NSLOT)])
        ss = slice(NS * c, NS * (c + 1))
        cvec = np.concatenate([inp["c_prompt"][b:b + 1], inp["c_sample"][ss]], axis=0)
        cT = np.ascontiguousarray(cvec.reshape(cfg.NR, 16, 128).transpose(2, 1, 0))
        st = inp["state_ffn_conv"][0, ss]
        cst = np.ascontiguousarray(st.reshape(NS, 2, NFF, 128).transpose(0, 3, 2, 1))
        m_ = dict(shared)
        m_.update(tabs[j])
        m_.update({
            "xp": inp["x_prompt"][b], "xw": xw, "xs": inp["x_sample"][ss],
            "c_sb_k": inp["cache_sb_k"][0, ss].reshape(NS, PAST, 1024), "c_sb_v": inp["cache_sb_v"][0, ss].reshape(NS, PAST, 1024),
            "c_sa_k": inp["cache_sa_k"][0, ss].reshape(NS, PAST, 1024), "c_sa_v": inp["cache_sa_v"][0, ss].reshape(NS, PAST, 1024),
            "c_ix": inp["cache_idx_k"][0, ss], "c_st": cst, "cT": cT,
            "b_ada_rows": np.repeat(inp["b_ada"][0][None], cfg.NR, axis=0),
        })
        in_maps.append({k: np.ascontiguousarray(v) for k, v in m_.items()})
    res = run_bass_kernel_spmd(nc, in_maps, core_ids=list(range(cfg.ncores))).results
    NB = cfg.NB
    y_prompt = np.zeros((NB, SEQ, D), f32)
    okv = np.zeros((NB, SEQ, KVC), f32)
    p_conv = np.zeros((1, NB, 2, DFF), f32)
    for c in range(cfg.ncores):
        b, j = c // G, c % G
        for m in range(NSLOT):
            p0 = STRIDE * (G * m + j)
            nv = min(STRIDE, SEQ - p0)
            if nv > 0:
                y_prompt[b, p0:p0 + nv] = res[c]["y_p"][m, :nv]
        q0, q1 = SEQ * j // G, SEQ * (j + 1) // G
        okv[b, q0:q1] = res[c]["okv_p"][q0:q1]
        if j == G - 1:
            p_conv[0, b] = res[c]["pconv"].transpose(2, 1, 0).reshape(2, DFF)
    nsb = cfg.ncores * NS
    y_sample = np.concatenate([res[c]["y_s"] for c in range(cfg.ncores)], axis=0)
    okvs = np.concatenate([res[c]["okv_s"] for c in range(cfg.ncores)], axis=0)
    s_conv = np.concatenate([res[c]["sconv"].transpose(0, 3, 2, 1).reshape(NS, 2, DFF) for c in range(cfg.ncores)], axis=0)[None]

    def split(o, L):
        nb_ = o.shape[0]
        return (o[..., 0:1024].reshape(1, nb_, L, NH, 128), o[..., 1024:2048].reshape(1, nb_, L, NH, 128),
                o[..., 2048:3072].reshape(1, nb_, L, NH, 128), o[..., 3072:4096].reshape(1, nb_, L, NH, 128),
                o[..., 4096:4160].reshape(1, nb_, L, 64))
    pk = split(okv, SEQ)
    sk = split(okvs, DSEQ)
    return (y_prompt, y_sample, pk[0], pk[1], pk[2], pk[3], pk[4], p_conv,
            sk[0], sk[1], sk[2], sk[3], sk[4], s_conv)


def kernel(**inputs):
    inp = {k: np.asarray(v) for k, v in inputs.items()}
    cfg = Cfg()
    out = run_cfg(cfg, inp)
    return tuple(np.ascontiguousarray(o, dtype=np.float32) for o in out)
```
